# Optimizing a Trainium2 kernel written in Bass

```python
import jax, jax.numpy as jnp
from jax import lax
import numpy as np

D_MODEL = 2048
BATCH = 8
SEQ = 4096
DEPTH = 1
DEC_BATCH = 2
DEC_SEQ = 4096
PAST_LEN = 128

N_MEM = 256
D_FF = 5632
NORM_EPS = 1e-6
FFN_RESIDUAL_WEIGHT = 0.5

MLA_HEADS = 8
MLA_Q_RANK = 512
MLA_KV_RANK = 256
MLA_NOPE_DIM = 128
MLA_ROPE_DIM = 64
MLA_V_DIM = 128
ROPE_THETA = 10000.0
Q_BLOCK = 128

GDN_HEADS = 8
GDN_K_DIM = 128
GDN_V_DIM = 128
GDN_CONV = 5
GDN_CHUNK = 64

MEM_HEADS = 4
MEM_HEAD_DIM = 128

MLA_WIDTH = MLA_HEADS * MLA_V_DIM
GDN_WIDTH = GDN_HEADS * GDN_V_DIM
MIX_WIDTH = MLA_WIDTH + GDN_WIDTH
GDN_QK_WIDTH = GDN_HEADS * GDN_K_DIM
GDN_QKV_WIDTH = 2 * GDN_QK_WIDTH + GDN_WIDTH
MEM_WIDTH = MEM_HEADS * MEM_HEAD_DIM
IN_SIZES = (MLA_Q_RANK, MLA_KV_RANK, MLA_ROPE_DIM, GDN_QKV_WIDTH, GDN_WIDTH, 2 * GDN_HEADS, 2 * GDN_HEADS)
IN_WIDTH = MLA_Q_RANK + MLA_KV_RANK + MLA_ROPE_DIM + GDN_QKV_WIDTH + GDN_WIDTH + 4 * GDN_HEADS

kernel_name = "hybrid_mla_gdn_macaron_encoder"


def split_columns(x, sizes):
    out, start = [], 0
    for s in sizes:
        out.append(x[..., start:start + s])
        start += s
    return out


def rmsnorm(x, g):
    xf = x.astype(jnp.float32)
    y = xf * lax.rsqrt(jnp.mean(xf * xf, axis=-1, keepdims=True) + NORM_EPS)
    return (y * g.astype(jnp.float32)).astype(x.dtype)


def l2norm(x):
    xf = x.astype(jnp.float32)
    return xf * lax.rsqrt(jnp.sum(xf * xf, axis=-1, keepdims=True) + NORM_EPS)


def swiglu(x, w_gate, w_up, w_down):
    return (jax.nn.silu(x @ w_gate) * (x @ w_up)) @ w_down


def rope_tables(seq_len):
    inv_freq = ROPE_THETA ** (-jnp.arange(0, MLA_ROPE_DIM, 2, dtype=jnp.float32) / MLA_ROPE_DIM)
    ang = jnp.arange(seq_len, dtype=jnp.float32)[:, None] * inv_freq[None, :]
    return jnp.cos(ang), jnp.sin(ang)


def apply_rope(x, cos, sin):
    x1, x2 = jnp.split(x.astype(jnp.float32), 2, axis=-1)
    return jnp.concatenate([x1 * cos - x2 * sin, x2 * cos + x1 * sin], axis=-1).astype(x.dtype)


def mla_group(c_q, c_kv, k_rope_in, q_norm_g, w_uq, kv_norm_g, w_ukv, cos, sin):
    B, S, _ = c_q.shape
    q = (rmsnorm(c_q, q_norm_g) @ w_uq).reshape(B, S, MLA_HEADS, MLA_NOPE_DIM + MLA_ROPE_DIM)
    q_nope = q[..., :MLA_NOPE_DIM]
    q_rope = apply_rope(q[..., MLA_NOPE_DIM:], cos[:, None, :], sin[:, None, :])
    kv = (rmsnorm(c_kv, kv_norm_g) @ w_ukv).reshape(B, S, MLA_HEADS, MLA_NOPE_DIM + MLA_V_DIM)
    k_nope, v = kv[..., :MLA_NOPE_DIM], kv[..., MLA_NOPE_DIM:]
    k_rope = apply_rope(k_rope_in, cos, sin)
    scale = (MLA_NOPE_DIM + MLA_ROPE_DIM) ** -0.5
    nb = S // Q_BLOCK
    qn_blocks = jnp.moveaxis(q_nope.reshape(B, nb, Q_BLOCK, MLA_HEADS, MLA_NOPE_DIM), 1, 0)
    qr_blocks = jnp.moveaxis(q_rope.reshape(B, nb, Q_BLOCK, MLA_HEADS, MLA_ROPE_DIM), 1, 0)

    def query_block(args):
        qn, qr = args
        s = jnp.einsum('bqhd,bkhd->bhqk', qn, k_nope) + jnp.einsum('bqhr,bkr->bhqk', qr, k_rope)
        p = jax.nn.softmax(s.astype(jnp.float32) * scale, axis=-1).astype(v.dtype)
        return jnp.einsum('bhqk,bkhd->bqhd', p, v)

    o = lax.map(query_block, (qn_blocks, qr_blocks))
    return jnp.moveaxis(o, 0, 1).reshape(B, S, MLA_WIDTH)


def centred_depthwise_conv(x, w):
    pad = GDN_CONV // 2
    return lax.conv_general_dilated(x, w[:, None, :], window_strides=(1,), padding=[(pad, pad)],
                                    dimension_numbers=('NWC', 'WIO', 'NWC'),
                                    feature_group_count=x.shape[-1])


def _to_chunks(t):
    B, S, H = t.shape[:3]
    t = t.reshape((B, S // GDN_CHUNK, GDN_CHUNK, H) + t.shape[3:])
    return jnp.moveaxis(t, 3, 1)


def gated_delta_rule_chunked(q, k, v, g, beta):
    B, S, H, DK = q.shape
    DV = v.shape[-1]
    q, k, v, g, beta = [_to_chunks(t.astype(jnp.float32)) for t in (q, k, v, g, beta)]
    q = q * DK ** -0.5
    gc = jnp.cumsum(g, axis=-1)
    incl = jnp.tril(jnp.ones((GDN_CHUNK, GDN_CHUNK), dtype=bool))
    strict = jnp.tril(jnp.ones((GDN_CHUNK, GDN_CHUNK), dtype=bool), -1)
    decay = jnp.exp(jnp.where(incl, gc[..., :, None] - gc[..., None, :], -jnp.inf))
    k_beta = k * beta[..., None]
    lower = jnp.where(strict, jnp.einsum('bhnid,bhnjd->bhnij', k_beta, k) * decay, 0.0)
    rhs = jnp.concatenate([v * beta[..., None], k_beta * jnp.exp(gc)[..., None]], axis=-1)
    sol = lax.linalg.triangular_solve(lower, rhs, left_side=True, lower=True, unit_diagonal=True)
    u, w = sol[..., :DV], sol[..., DV:]
    attn_intra = jnp.where(incl, jnp.einsum('bhnid,bhnjd->bhnij', q, k) * decay, 0.0)
    q_decayed = q * jnp.exp(gc)[..., None]
    k_decayed = k * jnp.exp(gc[..., -1:] - gc)[..., None]
    chunk_decay = jnp.exp(gc[..., -1])

    def chunk_step(state, xs):
        qd, kd, u_c, w_c, a_c, dec = xs
        v_new = u_c - jnp.einsum('bhik,bhkv->bhiv', w_c, state)
        out = jnp.einsum('bhik,bhkv->bhiv', qd, state) + jnp.einsum('bhij,bhjv->bhiv', a_c, v_new)
        state = state * dec[..., None, None] + jnp.einsum('bhik,bhiv->bhkv', kd, v_new)
        return state, out

    xs = tuple(jnp.moveaxis(t, 2, 0) for t in (q_decayed, k_decayed, u, w, attn_intra, chunk_decay))
    state0 = jnp.zeros((B, H, DK, DV), jnp.float32)
    _, out = lax.scan(chunk_step, state0, xs)
    return jnp.transpose(out, (1, 0, 3, 2, 4)).reshape(B, S, H, DV)


def gdn_group(qkv, z, a, b, conv_w, a_log, dt_bias, out_norm_g):
    B, S, _ = qkv.shape
    qkv = jax.nn.silu(centred_depthwise_conv(qkv, conv_w))
    q, k, v = split_columns(qkv, (GDN_QK_WIDTH, GDN_QK_WIDTH, GDN_WIDTH))
    q = l2norm(q.reshape(B, S, GDN_HEADS, GDN_K_DIM))
    k = l2norm(k.reshape(B, S, GDN_HEADS, GDN_K_DIM))
    v = v.reshape(B, S, GDN_HEADS, GDN_V_DIM)
    a = a.astype(jnp.float32).reshape(B, S, 2, GDN_HEADS)
    b = b.astype(jnp.float32).reshape(B, S, 2, GDN_HEADS)
    g = -jnp.exp(a_log.astype(jnp.float32)) * jax.nn.softplus(a + dt_bias.astype(jnp.float32))
    beta = jax.nn.sigmoid(b)
    o_fwd = gated_delta_rule_chunked(q, k, v, g[:, :, 0], beta[:, :, 0])
    flip = lambda t: jnp.flip(t, axis=1)
    o_bwd = flip(gated_delta_rule_chunked(flip(q), flip(k), flip(v), flip(g[:, :, 1]), flip(beta[:, :, 1])))
    zg = jax.nn.silu(z.astype(jnp.float32).reshape(B, S, GDN_HEADS, GDN_V_DIM))
    o = rmsnorm(o_fwd + o_bwd, out_norm_g) * zg
    return o.reshape(B, S, GDN_WIDTH).astype(qkv.dtype)


def memory_cross_attention(n, mem, mem_kv_norm_g, w_mq, w_mk, w_mv, w_mo):
    B, S, _ = n.shape
    m = rmsnorm(mem, mem_kv_norm_g)
    q = (n @ w_mq).reshape(B, S, MEM_HEADS, MEM_HEAD_DIM)
    k = (m @ w_mk).reshape(B, N_MEM, MEM_HEADS, MEM_HEAD_DIM)
    v = (m @ w_mv).reshape(B, N_MEM, MEM_HEADS, MEM_HEAD_DIM)
    s = jnp.einsum('bshd,bmhd->bhsm', q, k).astype(jnp.float32) * MEM_HEAD_DIM ** -0.5
    p = jax.nn.softmax(s, axis=-1).astype(v.dtype)
    o = jnp.einsum('bhsm,bmhd->bshd', p, v).reshape(B, S, MEM_WIDTH)
    return o @ w_mo


def encoder_layer(x, mem, cos, sin,
                  ffn1_pre_g, ffn1_w_gate, ffn1_w_up, ffn1_w_down, ffn1_post_g,
                  mix_pre_g, w_in, mla_q_norm_g, w_uq, mla_kv_norm_g, w_ukv,
                  gdn_conv_w, gdn_a_log, gdn_dt_bias, gdn_out_norm_g, w_out, mix_post_g,
                  mem_pre_g, mem_kv_norm_g, w_mq, w_mk, w_mv, w_mo, mem_post_g,
                  ffn2_pre_g, ffn2_w_gate, ffn2_w_up, ffn2_w_down, ffn2_post_g,
                  final_norm_g):
    h = x + FFN_RESIDUAL_WEIGHT * rmsnorm(swiglu(rmsnorm(x, ffn1_pre_g), ffn1_w_gate, ffn1_w_up, ffn1_w_down), ffn1_post_g)
    n = rmsnorm(h, mix_pre_g)
    c_q, c_kv, k_rope, qkv, z, a, b = split_columns(n @ w_in, IN_SIZES)
    o_mla = mla_group(c_q, c_kv, k_rope, mla_q_norm_g, w_uq, mla_kv_norm_g, w_ukv, cos, sin)
    o_gdn = gdn_group(qkv, z, a, b, gdn_conv_w, gdn_a_log, gdn_dt_bias, gdn_out_norm_g)
    mixed = jnp.concatenate([o_mla.astype(n.dtype), o_gdn.astype(n.dtype)], axis=-1) @ w_out
    h = h + rmsnorm(mixed, mix_post_g)
    h = h + rmsnorm(memory_cross_attention(rmsnorm(h, mem_pre_g), mem, mem_kv_norm_g, w_mq, w_mk, w_mv, w_mo), mem_post_g)
    h = h + FFN_RESIDUAL_WEIGHT * rmsnorm(swiglu(rmsnorm(h, ffn2_pre_g), ffn2_w_gate, ffn2_w_up, ffn2_w_down), ffn2_post_g)
    return rmsnorm(h, final_norm_g)


def setup_inputs(seed: int = 0) -> dict:
    key = jax.random.key(seed)
    keys = jax.random.split(key, 64)
    counter = iter(range(64))

    def nk():
        return keys[next(counter)]

    def weight(shape, fan_in):
        return jax.random.normal(nk(), (DEPTH,) + shape, jnp.float32) * fan_in ** -0.5

    def gain(dim):
        return 1.0 + 0.02 * jax.random.normal(nk(), (DEPTH, dim), jnp.float32)

    x_prompt = jax.random.normal(nk(), (BATCH, SEQ, D_MODEL), jnp.float32)
    x_sample = jax.random.normal(nk(), (DEC_BATCH, DEC_SEQ, D_MODEL), jnp.float32)
    mem_prompt = jax.random.normal(nk(), (BATCH, N_MEM, D_MODEL), jnp.float32)
    mem_sample = jax.random.normal(nk(), (DEC_BATCH, N_MEM, D_MODEL), jnp.float32)

    a_init = jax.random.uniform(nk(), (DEPTH, 2, GDN_HEADS), jnp.float32, 1.0, 16.0)
    dt = jnp.exp(jax.random.uniform(nk(), (DEPTH, 2, GDN_HEADS), jnp.float32, np.log(1e-3), np.log(1e-1)))
    dt_bias = dt + jnp.log(-jnp.expm1(-dt))

    return {
        "x_prompt": x_prompt,
        "x_sample": x_sample,
        "mem_prompt": mem_prompt,
        "mem_sample": mem_sample,
        "ffn1_pre_g": gain(D_MODEL),
        "ffn1_w_gate": weight((D_MODEL, D_FF), D_MODEL),
        "ffn1_w_up": weight((D_MODEL, D_FF), D_MODEL),
        "ffn1_w_down": weight((D_FF, D_MODEL), D_FF),
        "ffn1_post_g": gain(D_MODEL),
        "mix_pre_g": gain(D_MODEL),
        "w_in": weight((D_MODEL, IN_WIDTH), D_MODEL),
        "mla_q_norm_g": gain(MLA_Q_RANK),
        "w_uq": weight((MLA_Q_RANK, MLA_HEADS * (MLA_NOPE_DIM + MLA_ROPE_DIM)), MLA_Q_RANK),
        "mla_kv_norm_g": gain(MLA_KV_RANK),
        "w_ukv": weight((MLA_KV_RANK, MLA_HEADS * (MLA_NOPE_DIM + MLA_V_DIM)), MLA_KV_RANK),
        "gdn_conv_w": weight((GDN_CONV, GDN_QKV_WIDTH), GDN_CONV),
        "gdn_a_log": jnp.log(a_init),
        "gdn_dt_bias": dt_bias,
        "gdn_out_norm_g": gain(GDN_V_DIM),
        "w_out": weight((MIX_WIDTH, D_MODEL), MIX_WIDTH),
        "mix_post_g": gain(D_MODEL),
        "mem_pre_g": gain(D_MODEL),
        "mem_kv_norm_g": gain(D_MODEL),
        "w_mq": weight((D_MODEL, MEM_WIDTH), D_MODEL),
        "w_mk": weight((D_MODEL, MEM_WIDTH), D_MODEL),
        "w_mv": weight((D_MODEL, MEM_WIDTH), D_MODEL),
        "w_mo": weight((MEM_WIDTH, D_MODEL), MEM_WIDTH),
        "mem_post_g": gain(D_MODEL),
        "ffn2_pre_g": gain(D_MODEL),
        "ffn2_w_gate": weight((D_MODEL, D_FF), D_MODEL),
        "ffn2_w_up": weight((D_MODEL, D_FF), D_MODEL),
        "ffn2_w_down": weight((D_FF, D_MODEL), D_FF),
        "ffn2_post_g": gain(D_MODEL),
        "final_norm_g": gain(D_MODEL),
    }


def reference(x_prompt, x_sample, mem_prompt, mem_sample,
              ffn1_pre_g, ffn1_w_gate, ffn1_w_up, ffn1_w_down, ffn1_post_g,
              mix_pre_g, w_in, mla_q_norm_g, w_uq, mla_kv_norm_g, w_ukv,
              gdn_conv_w, gdn_a_log, gdn_dt_bias, gdn_out_norm_g, w_out, mix_post_g,
              mem_pre_g, mem_kv_norm_g, w_mq, w_mk, w_mv, w_mo, mem_post_g,
              ffn2_pre_g, ffn2_w_gate, ffn2_w_up, ffn2_w_down, ffn2_post_g,
              final_norm_g):
    cos_p, sin_p = rope_tables(x_prompt.shape[1])
    cos_s, sin_s = rope_tables(x_sample.shape[1])
    y_prompt, y_sample = x_prompt, x_sample
    for l in range(DEPTH):
        layer_params = (ffn1_pre_g[l], ffn1_w_gate[l], ffn1_w_up[l], ffn1_w_down[l], ffn1_post_g[l],
                        mix_pre_g[l], w_in[l], mla_q_norm_g[l], w_uq[l], mla_kv_norm_g[l], w_ukv[l],
                        gdn_conv_w[l], gdn_a_log[l], gdn_dt_bias[l], gdn_out_norm_g[l], w_out[l], mix_post_g[l],
                        mem_pre_g[l], mem_kv_norm_g[l], w_mq[l], w_mk[l], w_mv[l], w_mo[l], mem_post_g[l],
                        ffn2_pre_g[l], ffn2_w_gate[l], ffn2_w_up[l], ffn2_w_down[l], ffn2_post_g[l],
                        final_norm_g[l])
        y_prompt = encoder_layer(y_prompt, mem_prompt, cos_p, sin_p, *layer_params)
        y_sample = encoder_layer(y_sample, mem_sample, cos_s, sin_s, *layer_params)
    return (y_prompt, y_sample)
```

```python
import contextlib
import numpy as np
import concourse.bass as bass
import concourse.mybir as mybir
from concourse.bass_utils import run_bass_kernel_spmd

F32 = mybir.dt.float32
BF16 = mybir.dt.bfloat16
AF = mybir.ActivationFunctionType
ALU = mybir.AluOpType

D = 2048
DFF = 5632
NMEM = 256
EPS = 1e-6
QR, KVR, ROPE, NOPE, VD = 512, 256, 64, 128, 128
INW = 4960
KC = D // 128
FC = DFF // 128


class Trk:
    __slots__ = ("w", "r", "dsem")

    def __init__(self):
        self.w = {}
        self.r = {}
        self.dsem = None


class Buf:
    def __init__(self, t, psum=False):
        self.t = t
        self.T = Trk()
        self.psum = psum

    def __getitem__(self, idx):
        return self.t[idx]


class Eng:
    def __init__(self, name, e, semid, compute):
        self.name, self.e, self.semid, self.compute = name, e, semid, compute
        self.cnt = 0
        self.seen = {}


class K:
    def __init__(self, nc, stack):
        self.nc = nc
        self.stack = stack
        self.sems = []
        self.semcnt = []
        self.eng = {}
        for name, e, comp in (("pe", nc.tensor, True), ("act", nc.scalar, True), ("dve", nc.vector, True),
                              ("pool", nc.gpsimd, True), ("sp", nc.sync, False)):
            sid = self.newsem("e_" + name)
            self.eng[name] = Eng(name, e, sid, comp)
        self.free_dsems = []
        self.nwait = 0
        self.nins = 0

    def newsem(self, name):
        s = self.stack.enter_context(self.nc.semaphore(name))
        self.sems.append(s)
        self.semcnt.append(0)
        return len(self.sems) - 1

    def _wait(self, E, sid, val, raw=True):
        if val <= 0:
            return
        if sid == E.semid:
            if E.name == "pe" or not E.compute or not raw:
                return
        if E.seen.get(sid, 0) >= val:
            return
        E.e.wait_ge(self.sems[sid], val)
        E.seen[sid] = val
        self.nwait += 1

    def _deps(self, r, w):
        need = {}
        for x in r:
            for s, v in x.T.w.items():
                if need.get(s, 0) < v:
                    need[s] = v
            if x.psum:
                for s, v in x.T.r.items():
                    if need.get(s, 0) < v:
                        need[s] = v
        for x in w:
            for s, v in x.T.w.items():
                if need.get(s, 0) < v:
                    need[s] = v
            for s, v in x.T.r.items():
                if need.get(s, 0) < v:
                    need[s] = v
        return need

    def op(self, eng, fn, r=(), w=()):
        E = self.eng[eng]
        rawv = 0
        for x in r:
            rawv = max(rawv, x.T.w.get(E.semid, 0))
        for s, v in self._deps(r, w).items():
            if s == E.semid:
                self._wait(E, s, v, raw=True)
            else:
                self._wait(E, s, v)
        ins = fn(E.e)
        ins.then_inc(self.sems[E.semid], 1)
        E.cnt += 1
        self.semcnt[E.semid] = E.cnt
        self.nins += 1
        for x in r:
            if x.T.r.get(E.semid, 0) < E.cnt:
                x.T.r[E.semid] = E.cnt
        for x in w:
            x.T.w = {E.semid: E.cnt}
            x.T.r = {}

    def pe(self, fn, r=(), w=()):
        self.op("pe", fn, r, w)

    def act(self, fn, r=(), w=()):
        self.op("act", fn, r, w)

    def dve(self, fn, r=(), w=()):
        self.op("dve", fn, r, w)

    def pool(self, fn, r=(), w=()):
        self.op("pool", fn, r, w)

    def dma(self, out, in_, r=(), w=(), own=None, q="sp", accum_w=False):
        E = self.eng[q]
        if own is None:
            own = w[0] if w else r[0]
        T = own.T
        if T.dsem is None:
            T.dsem = self.newsem("d%d" % len(self.sems))
        sid = T.dsem
        need = self._deps(r, [] if accum_w else w)
        if accum_w:
            for x in w:
                for s, v in x.T.r.items():
                    if need.get(s, 0) < v:
                        need[s] = v
        if need.get(sid, 0) < self.semcnt[sid]:
            need[sid] = self.semcnt[sid]
        for s, v in need.items():
            self._wait(E, s, v)
        ins = E.e.dma_start(out=out, in_=in_)
        ins.then_inc(self.sems[sid], 16)
        self.semcnt[sid] += 16
        val = self.semcnt[sid]
        self.nins += 1
        for x in r:
            if x.T.r.get(sid, 0) < val:
                x.T.r[sid] = val
        for x in w:
            if accum_w:
                x.T.w[sid] = val
            else:
                x.T.w = {sid: val}
                x.T.r = {}

    def barrier(self):
        for E in self.eng.values():
            for sid in range(len(self.sems)):
                if sid != E.semid:
                    self._wait(E, sid, self.semcnt[sid])

    def finish(self):
        E = self.eng["sp"]
        for sid in range(len(self.sems)):
            self._wait(E, sid, self.semcnt[sid])


class Pools:
    def __init__(self, k, stack):
        self.k, self.stack, self.n = k, stack, 0

    def sb(self, shape, dt, name=None):
        self.n += 1
        nm = "%s_%d" % (name or "sb", id(self) % 10000 * 1000 + self.n)
        return Buf(self.stack.enter_context(self.k.nc.sbuf_tensor(nm, list(shape), dt)))

    def ps(self, shape, dt, name=None):
        self.n += 1
        nm = "%s_%d" % (name or "ps", id(self) % 10000 * 1000 + self.n)
        return Buf(self.stack.enter_context(self.k.nc.psum_tensor(nm, list(shape), dt)), psum=True)

    def ring(self, n, shape, dt, name=None):
        return Ring([self.sb(shape, dt, name) for _ in range(n)])


class Ring:
    def __init__(self, bufs):
        self.bufs, self.i = bufs, 0

    def next(self):
        b = self.bufs[self.i % len(self.bufs)]
        self.i += 1
        return b


class _StopD(Exception):
    pass


def build(S, NSEQ, dbg=False):
    nc = bass.Bass("TRN2", target_bir_lowering=False)
    flags = set(str(dbg).split("+")) if dbg else set()
    dstop = 99
    cstop = 99
    for f_ in flags:
        if f_.startswith("ds"):
            dstop = int(f_[2:])
        if f_.startswith("cs"):
            cstop = int(f_[2:])
    if "noB" in flags or "noC" in flags or dstop < 99 or cstop != 99:
        dbg = "X"
    NT = S // 128
    TB = min(512, S)
    NB = S // TB
    TPB = TB // 128
    QB = TB
    NQB = S // QB

    def din(name, shape, dt=F32):
        return nc.dram_tensor(name, list(shape), dt, kind="ExternalInput").ap()

    def dscr(name, shape, dt):
        return nc.dram_tensor(name, list(shape), dt, kind=("ExternalOutput" if dbg else "Internal")).ap()

    x_d = din("x", [NSEQ, S, D])
    mem_d = din("mem", [NSEQ, NMEM, D])
    y_d = nc.dram_tensor("y", [NSEQ, S, D], F32, kind="ExternalOutput").ap()
    wnames = {"ffn1_w_gate": (D, DFF), "ffn1_w_up": (D, DFF), "ffn1_w_down": (DFF, D),
              "ffn2_w_gate": (D, DFF), "ffn2_w_up": (D, DFF), "ffn2_w_down": (DFF, D),
              "w_in": (D, INW), "w_uq": (QR, 8 * 192), "w_ukv": (KVR, 8 * 256), "w_out": (D, D),
              "w_mq": (D, 512), "w_mk": (D, 512), "w_mv": (D, 512), "w_mo": (512, D)}
    wf = {n: din(n, s) for n, s in wnames.items()}
    wb = {n: nc.dram_tensor(n + "_b", list(s), BF16, kind="Internal").ap() for n, s in wnames.items()}
    gnames = ["ffn1_pre_g", "ffn1_post_g", "mix_pre_g", "mix_post_g", "mem_pre_g", "mem_kv_norm_g",
              "mem_post_g", "ffn2_pre_g", "ffn2_post_g", "final_norm_g"]
    gbd = {n: din(n + "_bc", [128, D]) for n in gnames}
    gpd = din("gpre", [128, 5, KC])
    qkg_d = din("qkg", [128, 6])
    consts_d = din("consts", [128, 6, 128])
    rope_d = din("rope", [128, 2, NT, 32])
    gdnp_d = din("gdnp", [128, 2, 16])
    gon_d = din("gon", [128, 128])
    cw_d = din("cw", [128, 24, 5])

    h1_d = dscr("h1_s", [S, D], F32)
    cqnT_d = dscr("cqnT_s", [128, 4, S], BF16)
    ckvnT_d = dscr("ckvnT_s", [128, 2, S], BF16)
    krT_d = dscr("krT_s", [64, S], BF16)
    qkvT_d = dscr("qkvT_s", [128, 24, S], F32)
    zs_d = dscr("zs_s", [S, 1024], F32)
    gb_d = dscr("gb_s", [128, NT, 32], F32)
    omixT_d = dscr("omixT_s", [128, 16, S], BF16)

    with contextlib.ExitStack() as gstack:
        k = K(nc, gstack)
        GP = Pools(k, gstack)
        pf = [GP.ps([128, 512], F32, "pf") for _ in range(6)]
        pb = [GP.ps([128, 1024], BF16, "pb") for _ in range(2)]
        pfr = Ring(pf)
        pbr = Ring(pb)
        class DT:
            pass
        dtrk = {}

        def dt_(name):
            if name not in dtrk:
                dtrk[name] = Buf(None)
            return dtrk[name]

        cst_f = GP.sb([128, 6, 128], F32, "cstf")
        cst_b = GP.sb([128, 6, 128], BF16, "cstb")
        k.dma(cst_f[:], consts_d, w=[cst_f])
        k.dve(lambda e: e.tensor_copy(out=cst_b[:], in_=cst_f[:]), r=[cst_f], w=[cst_b])
        IDB = lambda n=128: cst_b[0:n, 0, 0:n]
        ONESB = cst_b[:, 1, :]
        ONESF = cst_f[:, 1, :]
        LOW, SLOW, UP, SUP = (cst_f[:, i, :] for i in (2, 3, 4, 5))
        gpre = GP.sb([128, 5, KC], F32, "gpre")
        k.dma(gpre[:], gpd, w=[gpre])
        qkg = GP.sb([128, 6], F32, "qkg")
        k.dma(qkg[:], qkg_d, w=[qkg])
        PRE = {"ffn1_pre_g": 0, "mix_pre_g": 1, "mem_pre_g": 2, "mem_kv_norm_g": 3, "ffn2_pre_g": 4}

        wtrk = Buf(None)
        for n, (rows, cols) in wnames.items():
            step = 256
            for r0 in range(0, rows, step):
                r1 = min(rows, r0 + step)
                k.dma(wb[n][r0:r1, :], wf[n][r0:r1, :], w=[wtrk], own=wtrk, q="pool", accum_w=True)

        def rstd_of(P, src, W, junk, extra=1.0):
            sbuf, sap = src
            ss = P["ss"].next()
            k.pool(lambda e: e.memset(ss[:], 0.0), w=[ss])
            k.act(lambda e: e.activation(out=junk[:, 0:W], in_=sap, func=AF.Square, accum_out=ss[:, 0:1]),
                  r=[sbuf, ss], w=[junk, ss])
            rs = P["rs"].next()
            ex2 = float(extra) ** 2
            k.act(lambda e: e.activation(out=rs[:], in_=ss[:], func=AF.Sqrt, scale=1.0 / (W * ex2), bias=EPS / ex2),
                  r=[ss], w=[rs])
            k.dve(lambda e: e.reciprocal(out=rs[:], in_=rs[:]), r=[rs], w=[rs])
            return rs

        def transpose_to(P, nbf, ncols, dstT, t, gidx, alt=[0]):
            nch = ncols // 128
            for c0 in range(0, nch, 8):
                c1 = min(nch, c0 + 8)
                bank = pbr.next()
                for c in range(c0, c1):
                    k.pe(lambda e, c=c: e.transpose(bank[:, (c - c0) * 128:(c - c0 + 1) * 128],
                                                    nbf[:, c * 128:(c + 1) * 128], IDB()),
                         r=[nbf, cst_b], w=[bank])
                for c in range(c0, c1):
                    src = bank[:, (c - c0) * 128:(c - c0 + 1) * 128]
                    dst = dstT[:, c, t * 128:(t + 1) * 128]
                    if gidx is None:
                        fn = lambda e, src=src, dst=dst: e.tensor_copy(out=dst, in_=src)
                        rr = [bank]
                    else:
                        gbuf, g0 = gidx
                        gap = gbuf[:, g0 + c:g0 + c + 1] if len(gbuf.t.shape) == 2 else gbuf[:, g0, c:c + 1]
                        rr = [bank, gbuf]
                        if alt[0] % 2 == 0:
                            fn = lambda e, src=src, dst=dst, gap=gap: e.tensor_scalar(
                                out=dst, in0=src, scalar1=gap, scalar2=None, op0=ALU.mult)
                        else:
                            fn = lambda e, src=src, dst=dst, gap=gap: e.activation(
                                out=dst, in_=src, func=AF.Copy, scale=gap)
                    if gidx is not None and alt[0] % 2 == 1:
                        k.act(fn, r=rr, w=[dstT])
                    else:
                        k.dve(fn, r=rr, w=[dstT])
                alt[0] += 1

        def load_w(P, wd, r0, nkc, c0, ncols, coff=0, slot=None):
            if slot is None:
                slot = P["wring"].next()
            src = wd[r0 * 128:(r0 + nkc) * 128, c0:c0 + ncols].rearrange("(c p) f -> p c f", p=128)
            k.dma(slot[:, 0:nkc, coff:coff + ncols], src, r=[wtrk], w=[slot], own=slot, accum_w=(coff != 0))
            return slot

        def norm_transpose_block(P, load_tile, gain_name, nT, keep=None):
            for t in range(TPB):
                xt = load_tile(t)
                rs = rstd_of(P, (xt, xt[:, :]), D, P["junk"])
                nb = P["nbf"].next()
                k.act(lambda e: e.activation(out=nb[:], in_=xt[:], func=AF.Copy, scale=rs[:, 0:1]),
                      r=[xt, rs], w=[nb])
                transpose_to(P, nb, D, nT, t, (gpre, PRE[gain_name]))

        def ffn(P, nT, wg, wu, wd, ysb):
            hT = P["hT"]
            for g in range(FC // 4):
                sg_ = load_w(P, wg, 0, KC, g * 512, 512)
                su_ = load_w(P, wu, 0, KC, g * 512, 512)
                for f in range(4):
                    pg, pu = pfr.next(), pfr.next()
                    for kc in range(KC):
                        k.pe(lambda e, kc=kc: e.matmul(pg[:, 0:TB], lhsT=sg_[:, kc, f * 128:(f + 1) * 128],
                                                       rhs=nT[:, kc, :], start=(kc == 0), stop=(kc == KC - 1)),
                             r=[sg_, nT], w=[pg])
                    for kc in range(KC):
                        k.pe(lambda e, kc=kc: e.matmul(pu[:, 0:TB], lhsT=su_[:, kc, f * 128:(f + 1) * 128],
                                                       rhs=nT[:, kc, :], start=(kc == 0), stop=(kc == KC - 1)),
                             r=[su_, nT], w=[pu])
                    sl = P["silu"].next()
                    k.act(lambda e: e.activation(out=sl[:, 0:TB], in_=pg[:, 0:TB], func=AF.Silu), r=[pg], w=[sl])
                    fc = g * 4 + f
                    k.dve(lambda e: e.tensor_tensor(out=hT[:, fc, :], in0=sl[:, 0:TB], in1=pu[:, 0:TB], op=ALU.mult),
                          r=[sl, pu], w=[hT])
            for dg in range(4):
                accs = [pf[i] for i in range(TPB)]
                for fg in range(4):
                    sd = load_w(P, wd, fg * 11, 11, dg * 512, 512)
                    for t in range(TPB):
                        for f in range(11):
                            fc = fg * 11 + f
                            k.pe(lambda e, t=t, f=f, fc=fc: e.matmul(
                                accs[t][:, :], lhsT=hT[:, fc, t * 128:(t + 1) * 128], rhs=sd[:, f, :],
                                start=(fc == 0), stop=(fc == FC - 1)), r=[hT, sd], w=[accs[t]])
                for t in range(TPB):
                    k.act(lambda e, t=t: e.activation(out=ysb[t][:, dg * 512:(dg + 1) * 512], in_=accs[t][:, :],
                                                      func=AF.Copy), r=[accs[t]], w=[ysb[t]])
            pfr.i = 0

        def post_residual(P, ysrc, gname, base, out, half):
            rs = rstd_of(P, (ysrc, ysrc[:, :]), D, P["junk"], extra=(0.5 if half else 1.0))
            gB = P["gB"].next()
            k.dma(gB[:], gbd[gname], w=[gB])
            k.dve(lambda e: e.scalar_tensor_tensor(out=ysrc[:], in0=ysrc[:], scalar=rs[:, 0:1], in1=gB[:],
                                                   op0=ALU.mult, op1=ALU.mult), r=[ysrc, rs, gB], w=[ysrc])
            k.pool(lambda e: e.tensor_tensor(out=out[:], in0=base[:], in1=ysrc[:], op=ALU.add),
                   r=[base, ysrc], w=[out])

        for sq in range(NSEQ):
            with contextlib.ExitStack() as st:
                A = Pools(k, st)
                P = {"ss": A.ring(4, [128, 1], F32), "rs": A.ring(4, [128, 1], F32),
                     "junk": A.sb([128, D], BF16), "nbf": A.ring(1, [128, D], BF16),
                     "wring": A.ring(3, [128, KC, 512], BF16), "hT": A.sb([128, FC, TB], BF16),
                     "silu": A.ring(2, [128, 512], F32), "gB": A.ring(1, [128, D], F32)}
                nT = A.sb([128, KC, TB], BF16)
                xr = A.ring(2, [128, D], F32)
                ysb = [A.sb([128, D], F32) for _ in range(TPB)]
                cs = A.sb([128, 2, TPB, 32], F32)
                gdnp = A.sb([128, 2, 16], F32)
                k.dma(gdnp[:], gdnp_d, w=[gdnp])
                negA = A.sb([128, 16], F32)
                k.act(lambda e: e.activation(out=negA[:], in_=gdnp[:, 0, :], func=AF.Exp), r=[gdnp], w=[negA])
                k.dve(lambda e: e.tensor_scalar(out=negA[:], in0=negA[:], scalar1=-1.0, scalar2=None, op0=ALU.mult),
                      r=[negA], w=[negA])
                cqT = A.sb([128, 4, TB], BF16)
                ckT = A.sb([128, 2, TB], BF16)
                krT = A.sb([64, TB], BF16)
                gbs = A.sb([128, TPB, 32], F32)
                stg = A.ring(1, [128, 4, TB], F32)
                small = A.ring(2, [128, 512], F32)
                smallb = A.ring(3, [128, 512], BF16)
                tiny = A.ring(8, [128, 64], F32)
                for blk in range(NB):
                    t0 = blk * TB
                    k.dma(cs[:], rope_d[:, :, blk * TPB:(blk + 1) * TPB, :], w=[cs])

                    def load_x(t):
                        xt = xr.next()
                        k.dma(xt[:], x_d[sq, t0 + t * 128:t0 + (t + 1) * 128, :], w=[xt])
                        return xt
                    norm_transpose_block(P, load_x, "ffn1_pre_g", nT)
                    ffn(P, nT, wb["ffn1_w_gate"], wb["ffn1_w_up"], wb["ffn1_w_down"], ysb)
                    h1t = {}

                    def load_h1(t):
                        xt = load_x(t)
                        post_residual(P, ysb[t], "ffn1_post_g", xt, xt, True)
                        k.dma(h1_d[t0 + t * 128:t0 + (t + 1) * 128, :], xt[:], r=[xt], w=[dt_("h1%d" % (blk))],
                              own=xt, accum_w=True)
                        return xt
                    norm_transpose_block(P, load_h1, "mix_pre_g", nT)
                    s0 = load_w(P, wb["w_in"], 0, KC, 0, 512)
                    for t in range(TPB):
                        acc = pfr.next()
                        for kc in range(KC):
                            k.pe(lambda e, kc=kc: e.matmul(acc[:, :], lhsT=nT[:, kc, t * 128:(t + 1) * 128],
                                                           rhs=s0[:, kc, :], start=(kc == 0), stop=(kc == KC - 1)),
                                 r=[nT, s0], w=[acc])
                        cq = small.next()
                        k.act(lambda e: e.activation(out=cq[:], in_=acc[:, :], func=AF.Copy), r=[acc], w=[cq])
                        rs = rstd_of(P, (cq, cq[:, :]), 512, P["junk"])
                        cqn = smallb.next()
                        k.act(lambda e: e.activation(out=cqn[:], in_=cq[:], func=AF.Copy, scale=rs[:, 0:1]),
                              r=[cq, rs], w=[cqn])
                        transpose_to(P, cqn, 512, cqT, t, (qkg, 0))
                    k.dma(cqnT_d[:, :, t0:t0 + TB], cqT[:], r=[cqT], w=[dt_("cq%d" % blk)], own=cqT)
                    s1 = load_w(P, wb["w_in"], 0, KC, 512, 320)
                    load_w(P, wb["w_in"], 0, KC, 4928, 32, coff=320, slot=s1)
                    for t in range(TPB):
                        acc = pfr.next()
                        for kc in range(KC):
                            k.pe(lambda e, kc=kc: e.matmul(acc[:, 0:352], lhsT=nT[:, kc, t * 128:(t + 1) * 128],
                                                           rhs=s1[:, kc, 0:352], start=(kc == 0), stop=(kc == KC - 1)),
                                 r=[nT, s1], w=[acc])
                        ck = small.next()
                        k.act(lambda e: e.activation(out=ck[:, 0:352], in_=acc[:, 0:352], func=AF.Copy),
                              r=[acc], w=[ck])
                        rs = rstd_of(P, (ck, ck[:, 0:256]), 256, P["junk"])
                        ckn = smallb.next()
                        k.act(lambda e: e.activation(out=ckn[:, 0:256], in_=ck[:, 0:256], func=AF.Copy,
                                                     scale=rs[:, 0:1]), r=[ck, rs], w=[ckn])
                        transpose_to(P, ckn, 256, ckT, t, (qkg, 4))
                        cos, sin = cs[:, 0, t, :], cs[:, 1, t, :]
                        ta, tb_ = tiny.next(), tiny.next()
                        x1, x2 = ck[:, 256:288], ck[:, 288:320]
                        k.dve(lambda e: e.tensor_tensor(out=ta[:, 0:32], in0=x1, in1=cos, op=ALU.mult), r=[ck, cs], w=[ta])
                        k.dve(lambda e: e.tensor_tensor(out=ta[:, 32:64], in0=x2, in1=cos, op=ALU.mult), r=[ck, cs], w=[ta])
                        k.pool(lambda e: e.tensor_tensor(out=tb_[:, 0:32], in0=x2, in1=sin, op=ALU.mult), r=[ck, cs], w=[tb_])
                        k.pool(lambda e: e.tensor_tensor(out=tb_[:, 32:64], in0=x1, in1=sin, op=ALU.mult), r=[ck, cs], w=[tb_])
                        krb = smallb.next()
                        k.dve(lambda e: e.tensor_tensor(out=krb[:, 0:32], in0=ta[:, 0:32], in1=tb_[:, 0:32],
                                                        op=ALU.subtract), r=[ta, tb_], w=[krb])
                        k.dve(lambda e: e.tensor_tensor(out=krb[:, 32:64], in0=ta[:, 32:64], in1=tb_[:, 32:64],
                                                        op=ALU.add), r=[ta, tb_], w=[krb])
                        bank = pbr.next()
                        k.pe(lambda e: e.transpose(bank[0:64, 0:128], krb[:, 0:64], IDB()), r=[krb, cst_b], w=[bank])
                        k.dve(lambda e: e.tensor_copy(out=krT[:, t * 128:(t + 1) * 128], in_=bank[0:64, 0:128]),
                              r=[bank], w=[krT])
                        a_, b_ = ck[:, 320:336], ck[:, 336:352]
                        u0, u1, u2 = tiny.next(), tiny.next(), tiny.next()
                        k.dve(lambda e: e.tensor_tensor(out=u0[:, 0:16], in0=a_, in1=gdnp[:, 1, :], op=ALU.add),
                              r=[ck, gdnp], w=[u0])
                        k.dve(lambda e: e.tensor_scalar(out=u1[:, 0:16], in0=u0[:, 0:16], scalar1=-1.0, scalar2=None,
                                                        op0=ALU.mult), r=[u0], w=[u1])
                        k.dve(lambda e: e.tensor_tensor(out=u1[:, 0:16], in0=u0[:, 0:16], in1=u1[:, 0:16], op=ALU.min),
                              r=[u0, u1], w=[u1])
                        k.act(lambda e: e.activation(out=u1[:, 0:16], in_=u1[:, 0:16], func=AF.Exp),
                              r=[u1], w=[u1])
                        k.act(lambda e: e.activation(out=u1[:, 0:16], in_=u1[:, 0:16], func=AF.Ln, bias=1.0),
                              r=[u1], w=[u1])
                        k.dve(lambda e: e.scalar_tensor_tensor(out=u2[:, 0:16], in0=u0[:, 0:16], scalar=0.0,
                                                               in1=u1[:, 0:16], op0=ALU.max, op1=ALU.add),
                              r=[u0, u1], w=[u2])
                        k.dve(lambda e: e.tensor_tensor(out=gbs[:, t, 0:16], in0=u2[:, 0:16], in1=negA[:], op=ALU.mult),
                              r=[u2, negA], w=[gbs])
                        k.act(lambda e: e.activation(out=gbs[:, t, 16:32], in_=b_, func=AF.Sigmoid), r=[ck], w=[gbs])
                    k.dma(ckvnT_d[:, :, t0:t0 + TB], ckT[:], r=[ckT], w=[dt_("ck%d" % blk)], own=ckT)
                    k.dma(krT_d[:, t0:t0 + TB], krT[:], r=[krT], w=[dt_("kr%d" % blk)], own=krT)
                    k.dma(gb_d[:, blk * TPB:(blk + 1) * TPB, :], gbs[:], r=[gbs], w=[dt_("gb%d" % blk)], own=gbs)
                    for g in range(6):
                        sw = load_w(P, wb["w_in"], 0, KC, 832 + g * 512, 512)
                        sg = stg.next()
                        for f in range(4):
                            acc = pfr.next()
                            for kc in range(KC):
                                k.pe(lambda e, kc=kc: e.matmul(acc[:, 0:TB], lhsT=sw[:, kc, f * 128:(f + 1) * 128],
                                                               rhs=nT[:, kc, :], start=(kc == 0), stop=(kc == KC - 1)),
                                     r=[sw, nT], w=[acc])
                            if f % 2 == 0:
                                k.act(lambda e: e.activation(out=sg[:, f, :], in_=acc[:, 0:TB], func=AF.Copy),
                                      r=[acc], w=[sg])
                            else:
                                k.dve(lambda e: e.tensor_copy(out=sg[:, f, :], in_=acc[:, 0:TB]), r=[acc], w=[sg])
                        k.dma(qkvT_d[:, g * 4:(g + 1) * 4, t0:t0 + TB], sg[:, 0:4, :], r=[sg],
                              w=[dt_("qkv%d" % blk)], own=sg, accum_w=True)
                    sz = [load_w(P, wb["w_in"], 0, KC, 3904 + g * 512, 512) for g in range(2)]
                    for t in range(TPB):
                        zt = stg.next()
                        for g in range(2):
                            acc = pfr.next()
                            for kc in range(KC):
                                k.pe(lambda e, kc=kc: e.matmul(acc[:, :], lhsT=nT[:, kc, t * 128:(t + 1) * 128],
                                                               rhs=sz[g][:, kc, :], start=(kc == 0), stop=(kc == KC - 1)),
                                     r=[nT, sz[g]], w=[acc])
                            k.act(lambda e: e.activation(out=zt[:, g, :], in_=acc[:, :], func=AF.Silu),
                                  r=[acc], w=[zt])
                        k.dma(zs_d[t0 + t * 128:t0 + (t + 1) * 128, :].rearrange("s (g c) -> s g c", g=2), zt[:, 0:2, :],
                              r=[zt], w=[dt_("zs%d" % blk)],
                              own=zt, accum_w=True)
            k.barrier()
            if dbg == "A":
                break
            with contextlib.ExitStack() as st:
                B = Pools(k, st)
                cqT = B.sb([128, 4, S], BF16)
                ckT = B.sb([128, 2, S], BF16)
                krT = B.sb([64, S], BF16)
                rA = [dt_("cq%d" % b) for b in range(NB)] + [dt_("ck%d" % b) for b in range(NB)] + \
                     [dt_("kr%d" % b) for b in range(NB)]
                k.dma(cqT[:], cqnT_d, r=rA, w=[cqT])
                k.dma(ckT[:], ckvnT_d, r=rA, w=[ckT])
                k.dma(krT[:], krT_d, r=rA, w=[krT])
                wuq = B.sb([128, 4, 8 * 192], BF16)
                wukv = B.sb([128, 2, 8 * 256], BF16)
                k.dma(wuq[:], wb["w_uq"].rearrange("(c p) f -> p c f", p=128), r=[wtrk], w=[wuq])
                k.dma(wukv[:], wb["w_ukv"].rearrange("(c p) f -> p c f", p=128), r=[wtrk], w=[wukv])
                cs = B.sb([128, 2, NT, 32], F32)
                k.dma(cs[:], rope_d, w=[cs])
                KT = B.sb([128, S], BF16)
                QnT = B.sb([128, S], BF16)
                QrT = B.sb([64, S], BF16)
                Vh = B.sb([128, NT * 128], BF16)
                OT = B.sb([128, S], BF16)
                Pr = B.ring(3, [128, QB], BF16)
                rinv = B.ring(2, [128, QB], F32)
                qra = B.ring(2, [128, 8, 64], F32)
                qrb = B.ring(2, [128, 8, 64], F32)
                qrbf = B.ring(2, [128, 8, 64], BF16)
                scale = float((NOPE + ROPE) ** -0.5)
                G8 = min(8, NT)
                for h in range(8):
                    if dbg in ("B0", "C", "C0", "C1", "D0", "D1", "CD") or "noB" in flags:
                        break
                    for blk in range(NQB):
                        sl = slice(blk * QB, (blk + 1) * QB)
                        acc = pfr.next()
                        for c in range(2):
                            k.pe(lambda e, c=c: e.matmul(acc[:, 0:QB], lhsT=wukv[:, c, h * 256:h * 256 + 128],
                                                         rhs=ckT[:, c, sl], start=(c == 0), stop=(c == 1)),
                                 r=[wukv, ckT], w=[acc])
                        k.act(lambda e: e.activation(out=KT[:, sl], in_=acc[:, 0:QB], func=AF.Copy), r=[acc], w=[KT])
                        acc2 = pfr.next()
                        for c in range(4):
                            k.pe(lambda e, c=c: e.matmul(acc2[:, 0:QB], lhsT=wuq[:, c, h * 192:h * 192 + 128],
                                                         rhs=cqT[:, c, sl], start=(c == 0), stop=(c == 3)),
                                 r=[wuq, cqT], w=[acc2])
                        k.dve(lambda e: e.tensor_copy(out=QnT[:, sl], in_=acc2[:, 0:QB]), r=[acc2], w=[QnT])
                    for tg in range(0, NT, 4):
                        acc = pfr.next()
                        for t in range(tg, min(NT, tg + 4)):
                            for c in range(2):
                                k.pe(lambda e, c=c, t=t: e.matmul(
                                    acc[:, (t - tg) * 128:(t - tg + 1) * 128], lhsT=ckT[:, c, t * 128:(t + 1) * 128],
                                    rhs=wukv[:, c, h * 256 + 128:(h + 1) * 256], start=(c == 0), stop=(c == 1)),
                                    r=[wukv, ckT], w=[acc])
                        n4 = min(NT, tg + 4) - tg
                        k.act(lambda e: e.activation(out=Vh[:, tg * 128:(tg + n4) * 128], in_=acc[:, 0:n4 * 128],
                                                     func=AF.Copy), r=[acc], w=[Vh])
                    if dbg == "B1":
                        break
                    for tg in range(0, NT, G8):
                        acc = pfr.next()
                        for t in range(tg, tg + G8):
                            for c in range(4):
                                k.pe(lambda e, c=c, t=t: e.matmul(
                                    acc[:, (t - tg) * 64:(t - tg + 1) * 64], lhsT=cqT[:, c, t * 128:(t + 1) * 128],
                                    rhs=wuq[:, c, h * 192 + 128:(h + 1) * 192], start=(c == 0), stop=(c == 3)),
                                    r=[wuq, cqT], w=[acc])
                        av = acc[:, 0:G8 * 64].rearrange("p (t r) -> p t r", r=64)
                        cos, sin = cs[:, 0, tg:tg + G8, :], cs[:, 1, tg:tg + G8, :]
                        ta, tb_, qb_ = qra.next(), qrb.next(), qrbf.next()
                        k.dve(lambda e: e.tensor_tensor(out=ta[:, 0:G8, 0:32], in0=av[:, :, 0:32], in1=cos, op=ALU.mult),
                              r=[acc, cs], w=[ta])
                        k.dve(lambda e: e.tensor_tensor(out=ta[:, 0:G8, 32:64], in0=av[:, :, 32:64], in1=cos, op=ALU.mult),
                              r=[acc, cs], w=[ta])
                        k.dve(lambda e: e.tensor_tensor(out=tb_[:, 0:G8, 0:32], in0=av[:, :, 32:64], in1=sin, op=ALU.mult),
                              r=[acc, cs], w=[tb_])
                        k.dve(lambda e: e.tensor_tensor(out=tb_[:, 0:G8, 32:64], in0=av[:, :, 0:32], in1=sin, op=ALU.mult),
                              r=[acc, cs], w=[tb_])
                        k.pool(lambda e: e.tensor_tensor(out=qb_[:, 0:G8, 0:32], in0=ta[:, 0:G8, 0:32],
                                                         in1=tb_[:, 0:G8, 0:32], op=ALU.subtract), r=[ta, tb_], w=[qb_])
                        k.pool(lambda e: e.tensor_tensor(out=qb_[:, 0:G8, 32:64], in0=ta[:, 0:G8, 32:64],
                                                         in1=tb_[:, 0:G8, 32:64], op=ALU.add), r=[ta, tb_], w=[qb_])
                        bank = pbr.next()
                        for t in range(G8):
                            k.pe(lambda e, t=t: e.transpose(bank[0:64, t * 128:(t + 1) * 128], qb_[:, t, :], IDB()),
                                 r=[qb_, cst_b], w=[bank])
                        k.act(lambda e: e.activation(out=QrT[:, tg * 128:(tg + G8) * 128], in_=bank[0:64, 0:G8 * 128],
                                                     func=AF.Copy), r=[bank], w=[QrT])
                    if dbg == "B2":
                        break
                    for qb in range(NQB):
                        qs = slice(qb * QB, (qb + 1) * QB)
                        accO, accR = pf[(qb % 2) * 2], pf[(qb % 2) * 2 + 1]
                        sps = [pf[4], pf[5]]

                        def qk(kt):
                            sp_ = sps[kt % 2]
                            k.pe(lambda e: e.matmul(sp_[:, 0:QB], lhsT=KT[:, kt * 128:(kt + 1) * 128], rhs=QnT[:, qs],
                                                    start=True, stop=False), r=[KT, QnT], w=[sp_])
                            k.pe(lambda e: e.matmul(sp_[:, 0:QB], lhsT=krT[:, kt * 128:(kt + 1) * 128], rhs=QrT[:, qs],
                                                    start=False, stop=True), r=[krT, QrT], w=[sp_])
                        qk(0)
                        for kt in range(NT):
                            if kt + 1 < NT:
                                qk(kt + 1)
                            sp_ = sps[kt % 2]
                            p_ = Pr.next()
                            k.act(lambda e: e.activation(out=p_[:], in_=sp_[:, 0:QB], func=AF.Exp, scale=scale),
                                  r=[sp_], w=[p_])
                            k.pe(lambda e: e.matmul(accO[:, 0:QB], lhsT=Vh[:, kt * 128:(kt + 1) * 128], rhs=p_[:], start=(kt == 0),
                                                    stop=(kt == NT - 1)), r=[Vh, p_], w=[accO])
                            k.pe(lambda e: e.matmul(accR[:, 0:QB], lhsT=ONESB, rhs=p_[:], start=(kt == 0),
                                                    stop=(kt == NT - 1)), r=[cst_b, p_], w=[accR])
                        ri = rinv.next()
                        k.dve(lambda e: e.reciprocal(out=ri[:], in_=accR[:, 0:QB]), r=[accR], w=[ri])
                        k.dve(lambda e: e.tensor_tensor(out=OT[:, qs], in0=accO[:, 0:QB], in1=ri[:], op=ALU.mult),
                              r=[accO, ri], w=[OT])
                    k.dma(omixT_d[:, h, :], OT[:], r=[OT], w=[dt_("omla")], own=OT, accum_w=True)
                    pfr.i = 0
            k.barrier()
            if dbg in ("B", "B0", "B1", "B2"):
                break
            with contextlib.ExitStack() as st:
                C = Pools(k, st)
                try:
                    P = {"ss": C.ring(4, [128, 1], F32), "rs": C.ring(4, [128, 1], F32), "junk": C.sb([128, 128], BF16)}
                    gon = C.sb([128, 128], F32)
                    k.dma(gon[:], gon_d, w=[gon])
                    cw = C.sb([128, 24, 5], F32)
                    k.dma(cw[:], cw_d, w=[cw])
                    if cstop == 1:
                        raise _StopD()
                    W16 = NT * 16
                    H8 = NT * 8
                    gcs, eg, egs, ek, dec, bgc, nbeta, gq, bq, grem = (C.sb([128, W16], F32) for _ in range(10))
                    DKS = float(128 ** -0.5)
                    for d_ in range(2):
                        k.dma(gq[:, d_ * H8:(d_ + 1) * H8].rearrange("p (t n) -> p t n", n=8), gb_d[:, :, d_ * 8:(d_ + 1) * 8],
                              r=[dt_("gb%d" % b_) for b_ in range(NB)], w=[gq], own=gq, accum_w=(d_ == 1))
                        k.dma(bq[:, d_ * H8:(d_ + 1) * H8].rearrange("p (t n) -> p t n", n=8), gb_d[:, :, 16 + d_ * 8:16 + (d_ + 1) * 8],
                              r=[dt_("gb%d" % b_) for b_ in range(NB)], w=[bq], own=bq, accum_w=(d_ == 1))
                    if cstop == 2:
                        raise _StopD()
                    gpb = [C.sb([128, W16], BF16) for _ in range(3)]
                    gpf = [C.sb([128, W16], F32) for _ in range(3)]
                    k.dve(lambda e: e.tensor_copy(out=grem[:], in_=gq[:]), r=[gq], w=[grem])
                    for i3 in range(3):
                        k.dve(lambda e, i3=i3: e.tensor_copy(out=gpb[i3][:], in_=grem[:]), r=[grem], w=[gpb[i3]])
                        k.dve(lambda e, i3=i3: e.tensor_copy(out=gpf[i3][:], in_=gpb[i3][:]), r=[gpb[i3]], w=[gpf[i3]])
                        if i3 < 2:
                            k.dve(lambda e, i3=i3: e.tensor_tensor(out=grem[:], in0=grem[:], in1=gpf[i3][:], op=ALU.subtract),
                                  r=[grem, gpf[i3]], w=[grem])
                    if cstop == 3:
                        raise _StopD()
                    UPB, LOWB = cst_b[:, 4, :], cst_b[:, 2, :]
                    psA_, psT_ = pfr.next(), pfr.next()
                    for i3 in range(3):
                        k.pe(lambda e, i3=i3: e.matmul(psA_[:, 0:H8], lhsT=UPB, rhs=gpb[i3][:, 0:H8], start=(i3 == 0), stop=(i3 == 2)),
                             r=[cst_b, gpb[i3]], w=[psA_])
                    for i3 in range(3):
                        k.pe(lambda e, i3=i3: e.matmul(psA_[:, H8:W16], lhsT=LOWB, rhs=gpb[i3][:, H8:W16], start=(i3 == 0), stop=(i3 == 2)),
                             r=[cst_b, gpb[i3]], w=[psA_])
                    for i3 in range(3):
                        k.pe(lambda e, i3=i3: e.matmul(psT_[:, 0:W16], lhsT=ONESB, rhs=gpb[i3][:, :], start=(i3 == 0), stop=(i3 == 2)),
                             r=[cst_b, gpb[i3]], w=[psT_])
                    if cstop == 4:
                        raise _StopD()
                    k.act(lambda e: e.activation(out=gcs[:], in_=psA_[:, 0:W16], func=AF.Copy), r=[psA_], w=[gcs])
                    k.act(lambda e: e.activation(out=eg[:], in_=psA_[:, 0:W16], func=AF.Exp), r=[psA_], w=[eg])
                    k.act(lambda e: e.activation(out=dec[:], in_=psT_[:, 0:W16], func=AF.Exp), r=[psT_], w=[dec])
                    if cstop == 41:
                        raise _StopD()
                    k.dve(lambda e: e.tensor_tensor(out=ek[:], in0=psT_[:, 0:W16], in1=gcs[:], op=ALU.subtract), r=[psT_, gcs, dec], w=[ek])
                    if cstop == 411:
                        raise _StopD()
                    k.dve(lambda e: e.tensor_tensor(out=bgc[:], in0=bq[:], in1=eg[:], op=ALU.mult), r=[bq, eg], w=[bgc])
                    if cstop == 412:
                        raise _StopD()
                    k.dve(lambda e: e.tensor_scalar(out=nbeta[:], in0=bq[:], scalar1=-1.0, scalar2=None, op0=ALU.mult), r=[bq], w=[nbeta])
                    if cstop == 42:
                        raise _StopD()
                    k.act(lambda e: e.activation(out=ek[:], in_=ek[:], func=AF.Exp), r=[ek], w=[ek])
                    k.dve(lambda e: e.tensor_scalar(out=egs[:], in0=eg[:], scalar1=DKS, scalar2=None, op0=ALU.mult),
                          r=[eg], w=[egs])
                    if cstop == 5:
                        raise _StopD()
                    raw = C.sb([128, S + 4], F32)
                    cacc = C.sb([128, S], F32)
                    sil = C.sb([128, S], F32)
                    sqb = C.sb([128, S], BF16)
                    qT, kT, vT = (C.sb([128, S], BF16) for _ in range(3))
                    o_d = [C.sb([128, S], F32) for _ in range(2)]
                    z_h = C.sb([128, NT, 128], F32)
                    ogT = C.sb([128, S], BF16)
                    rnr = C.ring(2, [128, QB], F32)
                    S32 = C.sb([128, 128], F32)
                    Sbf = C.sb([128, 128], BF16)
                    Slo = C.sb([128, 128], BF16)
                    f128 = C.ring(8, [128, 128], F32)
                    b128 = C.ring(52, [128, 128], BF16)
                    k.dve(lambda e: e.memset(raw[:, 0:4], 0.0), w=[raw])
                    k.dve(lambda e: e.memset(raw[:, S:S + 4], 0.0), w=[raw])
                    qkv_r = [dt_("qkv%d" % b) for b in range(NB)]
                    zs_r = [dt_("zs%d" % b) for b in range(NB)]
                    for h in range(8):
                        if dbg in ("C0", "D0", "D1", "BD") or "noC" in flags:
                            break
                        for which, dst in ((0, qT), (1, kT), (2, vT)):
                            ch = which * 8 + h
                            k.dma(raw[:, 2:S + 2], qkvT_d[:, ch, :], r=qkv_r, w=[raw])
                            k.dve(lambda e: e.tensor_scalar(out=cacc[:], in0=raw[:, 0:S], scalar1=cw[:, ch, 0:1], scalar2=None,
                                                            op0=ALU.mult), r=[raw, cw], w=[cacc])
                            for j in range(1, 5):
                                k.dve(lambda e, j=j: e.scalar_tensor_tensor(out=cacc[:], in0=raw[:, j:j + S],
                                                                            scalar=cw[:, ch, j:j + 1], in1=cacc[:],
                                                                            op0=ALU.mult, op1=ALU.add), r=[raw, cw, cacc], w=[cacc])
                            if which == 2:
                                k.act(lambda e: e.activation(out=dst[:], in_=cacc[:], func=AF.Silu), r=[cacc], w=[dst])
                                continue
                            k.act(lambda e: e.activation(out=sil[:], in_=cacc[:], func=AF.Silu), r=[cacc], w=[sil])
                            k.act(lambda e: e.activation(out=sqb[:], in_=sil[:], func=AF.Square), r=[sil], w=[sqb])
                            for blk in range(NQB):
                                sl = slice(blk * QB, (blk + 1) * QB)
                                ps = pfr.next()
                                k.pe(lambda e: e.matmul(ps[:, 0:QB], lhsT=ONESB, rhs=sqb[:, sl], start=True, stop=True),
                                     r=[cst_b, sqb], w=[ps])
                                rn = rnr.next()
                                k.act(lambda e: e.activation(out=rn[:], in_=ps[:, 0:QB], func=AF.Sqrt, bias=EPS), r=[ps], w=[rn])
                                k.dve(lambda e: e.reciprocal(out=rn[:], in_=rn[:]), r=[rn], w=[rn])
                                k.dve(lambda e: e.tensor_tensor(out=dst[:, sl], in0=sil[:, sl], in1=rn[:], op=ALU.mult),
                                      r=[sil, rn], w=[dst])
                        if dbg == "C1":
                            break
                        k.dma(z_h[:], zs_d[:, h * 128:(h + 1) * 128].rearrange("(t p) v -> p t v", p=128), r=zs_r, w=[z_h])
                        for dr in range(2):
                            col = dr * 8 + h
                            TRI = UP if dr == 0 else LOW
                            SM = SLOW if dr == 0 else SUP
                            IMT = UP if dr == 0 else LOW
                            k.dve(lambda e: e.memset(S32[:], 0.0), w=[S32])
                            k.dve(lambda e: e.memset(Sbf[:], 0.0), w=[Sbf])
                            k.dve(lambda e: e.memset(Slo[:], 0.0), w=[Slo])
                            od = o_d[dr]
                            for c in (range(NT) if dr == 0 else range(NT - 1, -1, -1)):
                                cs_ = slice(c * 128, (c + 1) * 128)
                                ci = dr * H8 + c * 8 + h
                                bcol = bq[:, ci:ci + 1]
                                psg = pfr.next()
                                for i3 in range(3):
                                    gT = b128.next()
                                    k.dve(lambda e, i3=i3: e.tensor_scalar(out=gT[:], in0=TRI, scalar1=gpf[i3][:, ci:ci + 1],
                                                                           scalar2=None, op0=ALU.mult), r=[cst_f, gpf[i3]], w=[gT])
                                    k.pe(lambda e, i3=i3: e.matmul(psg[:, 0:128], lhsT=ONESB, rhs=gT[:], start=(i3 == 0), stop=(i3 == 2)),
                                         r=[cst_b, gT], w=[psg])
                                dm, dtm = f128.next(), f128.next()
                                k.dve(lambda e: e.tensor_scalar(out=dm[:], in0=psg[:, 0:128], scalar1=gcs[:, ci:ci + 1], scalar2=0.0,
                                                                op0=ALU.subtract, op1=ALU.max), r=[psg, gcs], w=[dm])
                                k.act(lambda e: e.activation(out=dm[:], in_=dm[:], func=AF.Exp, scale=-1.0), r=[dm], w=[dm])
                                k.dve(lambda e: e.tensor_tensor(out=dm[:], in0=dm[:], in1=SM, op=ALU.mult), r=[dm, cst_f], w=[dm])
                                k.dve(lambda e: e.tensor_scalar(out=dtm[:], in0=psg[:, 0:128], scalar1=gcs[:, ci:ci + 1], scalar2=0.0,
                                                                op0=ALU.subtract, op1=ALU.min), r=[psg, gcs], w=[dtm])
                                k.act(lambda e: e.activation(out=dtm[:], in_=dtm[:], func=AF.Exp), r=[dtm], w=[dtm])
                                k.dve(lambda e: e.tensor_tensor(out=dtm[:], in0=dtm[:], in1=IMT, op=ALU.mult), r=[dtm, cst_f], w=[dtm])
                                psG = pfr.next()
                                k.pe(lambda e: e.matmul(psG[:, 0:128], lhsT=kT[:, cs_], rhs=kT[:, cs_], start=True, stop=True),
                                     r=[kT], w=[psG])
                                Ln = b128.next()
                                k.dve(lambda e: e.scalar_tensor_tensor(out=Ln[:], in0=psG[:, 0:128], scalar=nbeta[:, ci:ci + 1],
                                                                       in1=dm[:], op0=ALU.mult, op1=ALU.mult),
                                      r=[psG, nbeta, dm], w=[Ln])
                                psK = pfr.next()
                                k.pe(lambda e: e.matmul(psK[:, 0:128], lhsT=kT[:, cs_], rhs=qT[:, cs_], start=True, stop=True),
                                     r=[kT, qT], w=[psK])
                                AT = b128.next()
                                k.dve(lambda e: e.scalar_tensor_tensor(out=AT[:], in0=psK[:, 0:128], scalar=DKS, in1=dtm[:],
                                                                       op0=ALU.mult, op1=ALU.mult), r=[psK, dtm], w=[AT])
                                bank = pbr.next()
                                k.pe(lambda e: e.transpose(bank[:, 0:128], Ln[:], IDB()), r=[Ln, cst_b], w=[bank])
                                Nk = b128.next()
                                k.act(lambda e: e.activation(out=Nk[:], in_=bank[:, 0:128], func=AF.Copy), r=[bank], w=[Nk])
                                NkT = Ln
                                Pm = b128.next()
                                k.dve(lambda e: e.tensor_tensor(out=Pm[:], in0=Nk[:], in1=cst_b[:, 0, :], op=ALU.add),
                                      r=[Nk, cst_b], w=[Pm])
                                Pt = b128.next()
                                k.dve(lambda e: e.tensor_tensor(out=Pt[:], in0=NkT[:], in1=cst_b[:, 0, :], op=ALU.add),
                                      r=[NkT, cst_b], w=[Pt])
                                for lev in range(1, 7):
                                    psA = pfr.next()
                                    k.pe(lambda e: e.matmul(psA[:, 0:128], lhsT=Nk[:], rhs=NkT[:], start=True, stop=True),
                                         r=[Nk, NkT], w=[psA])
                                    NkT2 = b128.next()
                                    k.act(lambda e: e.activation(out=NkT2[:], in_=psA[:, 0:128], func=AF.Copy), r=[psA], w=[NkT2])
                                    if lev < 6:
                                        psB = pfr.next()
                                        k.pe(lambda e: e.matmul(psB[:, 0:128], lhsT=NkT[:], rhs=Nk[:], start=True, stop=True),
                                             r=[Nk, NkT], w=[psB])
                                        Nk2 = b128.next()
                                        k.act(lambda e: e.activation(out=Nk2[:], in_=psB[:, 0:128], func=AF.Copy), r=[psB], w=[Nk2])
                                    else:
                                        Nk2 = None
                                    psC = pfr.next()
                                    k.pe(lambda e: e.matmul(psC[:, 0:128], lhsT=NkT2[:], rhs=Pm[:], start=True, stop=True),
                                         r=[NkT2, Pm], w=[psC])
                                    psD = pfr.next()
                                    k.pe(lambda e: e.matmul(psD[:, 0:128], lhsT=Pm[:], rhs=NkT2[:], start=True, stop=True),
                                         r=[NkT2, Pm], w=[psD])
                                    Pn = b128.next()
                                    k.dve(lambda e: e.tensor_tensor(out=Pn[:], in0=psC[:, 0:128], in1=Pm[:], op=ALU.add),
                                          r=[psC, Pm], w=[Pn])
                                    Ptn = b128.next()
                                    k.dve(lambda e: e.tensor_tensor(out=Ptn[:], in0=psD[:, 0:128], in1=Pt[:], op=ALU.add),
                                          r=[psD, Pt], w=[Ptn])
                                    Pm, Pt, Nk, NkT = Pn, Ptn, Nk2, NkT2
                                psR = pfr.next()
                                k.pe(lambda e: e.matmul(psR[:, 0:128], lhsT=Ln[:], rhs=Pm[:], start=True, stop=True), r=[Ln, Pm], w=[psR])
                                IX = b128.next()
                                k.dve(lambda e: e.tensor_tensor(out=IX[:], in0=cst_b[:, 0, :], in1=Pm[:], op=ALU.subtract),
                                      r=[cst_b, Pm], w=[IX])
                                Rr = b128.next()
                                k.dve(lambda e: e.tensor_tensor(out=Rr[:], in0=psR[:, 0:128], in1=IX[:], op=ALU.add),
                                      r=[psR, IX], w=[Rr])
                                psX = pfr.next()
                                k.pe(lambda e: e.matmul(psX[:, 0:128], lhsT=Pt[:], rhs=Rr[:], start=True, stop=True), r=[Pt, Rr], w=[psX])
                                Pn = b128.next()
                                k.dve(lambda e: e.tensor_tensor(out=Pn[:], in0=psX[:, 0:128], in1=Pm[:], op=ALU.add),
                                      r=[psX, Pm], w=[Pn])
                                Pm = Pn
                                TT = Pm
                                bank2 = pbr.next()
                                k.pe(lambda e: e.transpose(bank2[:, 0:128], kT[:, cs_], IDB()), r=[kT, cst_b], w=[bank2])
                                k.pe(lambda e: e.transpose(bank2[:, 128:256], vT[:, cs_], IDB()), r=[vT, cst_b], w=[bank2])
                                kbg, kd, vb = b128.next(), b128.next(), b128.next()
                                k.dve(lambda e: e.tensor_scalar(out=kbg[:], in0=bank2[:, 0:128], scalar1=bgc[:, ci:ci + 1], scalar2=None,
                                                                op0=ALU.mult), r=[bank2, bgc], w=[kbg])
                                k.dve(lambda e: e.tensor_scalar(out=kd[:], in0=bank2[:, 0:128], scalar1=ek[:, ci:ci + 1], scalar2=None,
                                                                op0=ALU.mult), r=[bank2, ek], w=[kd])
                                k.dve(lambda e: e.tensor_scalar(out=vb[:], in0=bank2[:, 128:256], scalar1=bcol, scalar2=None,
                                                                op0=ALU.mult), r=[bank2, bq], w=[vb])
                                psu = pfr.next()
                                k.pe(lambda e: e.matmul(psu[:, 0:128], lhsT=TT[:], rhs=vb[:], start=True, stop=True), r=[TT, vb], w=[psu])
                                u = f128.next()
                                k.act(lambda e: e.activation(out=u[:], in_=psu[:, 0:128], func=AF.Copy), r=[psu], w=[u])
                                psw = pfr.next()
                                k.pe(lambda e: e.matmul(psw[:, 0:128], lhsT=kbg[:], rhs=TT[:], start=True, stop=True), r=[TT, kbg], w=[psw])
                                wT = b128.next()
                                k.dve(lambda e: e.tensor_copy(out=wT[:], in_=psw[:, 0:128]), r=[psw], w=[wT])
                                wTl = b128.next()
                                k.dve(lambda e: e.tensor_tensor(out=wTl[:], in0=psw[:, 0:128], in1=wT[:], op=ALU.subtract),
                                      r=[psw, wT], w=[wTl])
                                ps1 = pfr.next()
                                k.pe(lambda e: e.matmul(ps1[:, 0:128], lhsT=wT[:], rhs=Sbf[:], start=True, stop=False), r=[wT, Sbf], w=[ps1])
                                k.pe(lambda e: e.matmul(ps1[:, 0:128], lhsT=wTl[:], rhs=Sbf[:], start=False, stop=False), r=[wTl, Sbf], w=[ps1])
                                k.pe(lambda e: e.matmul(ps1[:, 0:128], lhsT=wT[:], rhs=Slo[:], start=False, stop=True), r=[wT, Slo], w=[ps1])
                                vn = b128.next()
                                k.dve(lambda e: e.tensor_tensor(out=vn[:], in0=u[:], in1=ps1[:, 0:128], op=ALU.subtract),
                                      r=[u, ps1], w=[vn])
                                ps2 = pfr.next()
                                k.pe(lambda e: e.matmul(ps2[:, 0:128], lhsT=qT[:, cs_], rhs=Sbf[:], start=True, stop=True), r=[qT, Sbf], w=[ps2])
                                ps3 = pfr.next()
                                k.pe(lambda e: e.matmul(ps3[:, 0:128], lhsT=AT[:], rhs=vn[:], start=True, stop=True), r=[AT, vn], w=[ps3])
                                tmp = f128.next()
                                k.act(lambda e: e.activation(out=tmp[:], in_=ps2[:, 0:128], func=AF.Copy, scale=egs[:, ci:ci + 1]),
                                      r=[ps2, egs], w=[tmp])
                                k.dve(lambda e: e.tensor_tensor(out=od[:, cs_], in0=tmp[:], in1=ps3[:, 0:128], op=ALU.add),
                                      r=[tmp, ps3], w=[od])
                                ps4 = pfr.next()
                                k.pe(lambda e: e.matmul(ps4[:, 0:128], lhsT=kd[:], rhs=vn[:], start=True, stop=True), r=[kd, vn], w=[ps4])
                                k.dve(lambda e: e.scalar_tensor_tensor(out=S32[:], in0=S32[:], scalar=dec[:, ci:ci + 1], in1=ps4[:, 0:128],
                                                                       op0=ALU.mult, op1=ALU.add), r=[S32, dec, ps4], w=[S32])
                                k.dve(lambda e: e.tensor_copy(out=Sbf[:], in_=S32[:]), r=[S32], w=[Sbf])
                                k.dve(lambda e: e.tensor_tensor(out=Slo[:], in0=S32[:], in1=Sbf[:], op=ALU.subtract),
                                      r=[S32, Sbf], w=[Slo])
                        for c in range(NT):
                            cs_ = slice(c * 128, (c + 1) * 128)
                            k.dve(lambda e: e.tensor_tensor(out=o_d[0][:, cs_], in0=o_d[0][:, cs_], in1=o_d[1][:, cs_], op=ALU.add),
                                   r=[o_d[0], o_d[1]], w=[o_d[0]])
                            rs = rstd_of(P, (o_d[0], o_d[0][:, cs_]), 128, P["junk"])
                            tmp = f128.next()
                            k.dve(lambda e: e.scalar_tensor_tensor(out=tmp[:], in0=o_d[0][:, cs_], scalar=rs[:, 0:1], in1=gon[:],
                                                                   op0=ALU.mult, op1=ALU.mult), r=[o_d[0], rs, gon], w=[tmp])
                            onb = b128.next()
                            k.dve(lambda e: e.tensor_tensor(out=onb[:], in0=tmp[:], in1=z_h[:, c, :], op=ALU.mult),
                                   r=[tmp, z_h], w=[onb])
                            bank = pbr.next()
                            k.pe(lambda e: e.transpose(bank[:, 0:128], onb[:], IDB()), r=[onb, cst_b], w=[bank])
                            k.act(lambda e: e.activation(out=ogT[:, cs_], in_=bank[:, 0:128], func=AF.Copy), r=[bank], w=[ogT])
                        k.dma(omixT_d[:, 8 + h, :], ogT[:], r=[ogT], w=[dt_("ogdn")], own=ogT, accum_w=True)
                except _StopD:
                    pass
            k.barrier()
            if dbg in ("C", "C0", "C1"):
                break
            if cstop != 99:
                break
            with contextlib.ExitStack() as st:
                Dp = Pools(k, st)
                P = {"ss": Dp.ring(4, [128, 1], F32), "rs": Dp.ring(4, [128, 1], F32),
                     "junk": Dp.sb([128, D], BF16), "nbf": Dp.ring(1, [128, D], BF16),
                     "wring": Dp.ring(3, [128, KC, 512], BF16), "hT": Dp.sb([128, FC, TB], BF16),
                     "silu": Dp.ring(2, [128, 512], F32), "gB": Dp.ring(1, [128, D], F32)}
                nT = Dp.sb([128, KC, TB], BF16)
                xr = Dp.ring(2, [128, D], F32)
                ysb = [Dp.sb([128, D], F32) for _ in range(TPB)]
                mT = Dp.sb([128, KC, NMEM], BF16)
                KmT = Dp.sb([128, 4, NMEM], BF16)
                Vm = Dp.sb([128, 2, 512], BF16)
                qmT = Dp.sb([128, 4, TB], BF16)
                omT = Dp.sb([128, 4, TB], BF16)
                Pr = Dp.ring(2, [128, TB], BF16)
                rinv = Dp.ring(1, [128, TB], F32)
                mscale = float(128 ** -0.5)
                TPB_save = TPB

                def load_mem(t):
                    xt = xr.next()
                    k.dma(xt[:], mem_d[sq, t * 128:(t + 1) * 128, :], w=[xt])
                    return xt
                for t in range(NMEM // 128 if dstop > 0 else 0):
                    xt = load_mem(t)
                    rs = rstd_of(P, (xt, xt[:, :]), D, P["junk"])
                    nb = P["nbf"].next()
                    k.act(lambda e: e.activation(out=nb[:], in_=xt[:], func=AF.Copy, scale=rs[:, 0:1]), r=[xt, rs], w=[nb])
                    transpose_to(P, nb, D, mT, t, (gpre, PRE["mem_kv_norm_g"]))
                sk = load_w(P, wb["w_mk"], 0, KC, 0, 512)
                for hh in range(4 if dstop > 0 else 0):
                    acc = pfr.next()
                    for kc in range(KC):
                        k.pe(lambda e, kc=kc: e.matmul(acc[:, 0:NMEM], lhsT=sk[:, kc, hh * 128:(hh + 1) * 128], rhs=mT[:, kc, :],
                                                       start=(kc == 0), stop=(kc == KC - 1)), r=[sk, mT], w=[acc])
                    k.act(lambda e: e.activation(out=KmT[:, hh, :], in_=acc[:, 0:NMEM], func=AF.Copy), r=[acc], w=[KmT])
                sv = load_w(P, wb["w_mv"], 0, KC, 0, 512)
                for t in range(NMEM // 128 if dstop > 0 else 0):
                    acc = pfr.next()
                    for kc in range(KC):
                        k.pe(lambda e, kc=kc: e.matmul(acc[:, :], lhsT=mT[:, kc, t * 128:(t + 1) * 128], rhs=sv[:, kc, :],
                                                       start=(kc == 0), stop=(kc == KC - 1)), r=[sv, mT], w=[acc])
                    k.act(lambda e: e.activation(out=Vm[:, t, :], in_=acc[:, :], func=AF.Copy), r=[acc], w=[Vm])
                om_r = [dt_("omla"), dt_("ogdn")]
                for blk in range(NB if dstop > 1 else 0):
                    if dbg == "D1" and blk == 1:
                        break
                    t0 = blk * TB
                    hrow = dt_("h1%d" % blk)
                    k.dma(nT[:], omixT_d[:, :, t0:t0 + TB], r=om_r, w=[nT])
                    for dg in range(4):
                        so = load_w(P, wb["w_out"], 0, KC, dg * 512, 512)
                        for t in range(TPB):
                            acc = pfr.next()
                            for kc in range(KC):
                                k.pe(lambda e, kc=kc: e.matmul(acc[:, :], lhsT=nT[:, kc, t * 128:(t + 1) * 128], rhs=so[:, kc, :],
                                                               start=(kc == 0), stop=(kc == KC - 1)), r=[so, nT], w=[acc])
                            k.act(lambda e: e.activation(out=ysb[t][:, dg * 512:(dg + 1) * 512], in_=acc[:, :], func=AF.Copy),
                                  r=[acc], w=[ysb[t]])

                    def mk_loader(gname, half):
                        def ld(t):
                            xt = xr.next()
                            k.dma(xt[:], h1_d[t0 + t * 128:t0 + (t + 1) * 128, :], r=[hrow], w=[xt])
                            post_residual(P, ysb[t], gname, xt, xt, half)
                            k.dma(h1_d[t0 + t * 128:t0 + (t + 1) * 128, :], xt[:], r=[xt], w=[hrow], own=xt, accum_w=True)
                            return xt
                        return ld
                    norm_transpose_block(P, mk_loader("mix_post_g", False), "mem_pre_g", nT)
                    if dstop == 2:
                        break
                    sq_ = load_w(P, wb["w_mq"], 0, KC, 0, 512)
                    for hh in range(4):
                        acc = pfr.next()
                        for kc in range(KC):
                            k.pe(lambda e, kc=kc: e.matmul(acc[:, 0:TB], lhsT=sq_[:, kc, hh * 128:(hh + 1) * 128], rhs=nT[:, kc, :],
                                                           start=(kc == 0), stop=(kc == KC - 1)), r=[sq_, nT], w=[acc])
                        k.dve(lambda e: e.tensor_copy(out=qmT[:, hh, :], in_=acc[:, 0:TB]), r=[acc], w=[qmT])
                    for hh in range(4):
                        accO, accR = pf[0], pf[1]
                        for mt in range(2):
                            sp_ = pf[2 + mt]
                            k.pe(lambda e: e.matmul(sp_[:, 0:TB], lhsT=KmT[:, hh, mt * 128:(mt + 1) * 128], rhs=qmT[:, hh, :],
                                                    start=True, stop=True), r=[KmT, qmT], w=[sp_])
                            p_ = Pr.next()
                            k.act(lambda e: e.activation(out=p_[:], in_=sp_[:, 0:TB], func=AF.Exp, scale=mscale), r=[sp_], w=[p_])
                            k.pe(lambda e: e.matmul(accO[:, 0:TB], lhsT=Vm[:, mt, hh * 128:(hh + 1) * 128], rhs=p_[:],
                                                    start=(mt == 0), stop=(mt == 1)), r=[Vm, p_], w=[accO])
                            k.pe(lambda e: e.matmul(accR[:, 0:TB], lhsT=ONESB, rhs=p_[:], start=(mt == 0), stop=(mt == 1)),
                                 r=[cst_b, p_], w=[accR])
                        ri = rinv.next()
                        k.dve(lambda e: e.reciprocal(out=ri[:], in_=accR[:, 0:TB]), r=[accR], w=[ri])
                        k.dve(lambda e: e.tensor_tensor(out=omT[:, hh, :], in0=accO[:, 0:TB], in1=ri[:], op=ALU.mult),
                              r=[accO, ri], w=[omT])
                    pfr.i = 0
                    for dg in range(4):
                        so = load_w(P, wb["w_mo"], 0, 4, dg * 512, 512)
                        for t in range(TPB):
                            acc = pfr.next()
                            for hh in range(4):
                                k.pe(lambda e, hh=hh: e.matmul(acc[:, :], lhsT=omT[:, hh, t * 128:(t + 1) * 128], rhs=so[:, hh, :],
                                                               start=(hh == 0), stop=(hh == 3)), r=[so, omT], w=[acc])
                            k.act(lambda e: e.activation(out=ysb[t][:, dg * 512:(dg + 1) * 512], in_=acc[:, :], func=AF.Copy),
                                  r=[acc], w=[ysb[t]])
                    if dstop == 3:
                        break
                    norm_transpose_block(P, mk_loader("mem_post_g", False), "ffn2_pre_g", nT)
                    ffn(P, nT, wb["ffn2_w_gate"], wb["ffn2_w_up"], wb["ffn2_w_down"], ysb)
                    if dstop == 4:
                        break
                    for t in range(TPB):
                        xt = xr.next()
                        k.dma(xt[:], h1_d[t0 + t * 128:t0 + (t + 1) * 128, :], r=[hrow], w=[xt])
                        post_residual(P, ysb[t], "ffn2_post_g", xt, xt, True)
                        rs = rstd_of(P, (xt, xt[:, :]), D, P["junk"])
                        gB = P["gB"].next()
                        k.dma(gB[:], gbd["final_norm_g"], w=[gB])
                        k.dve(lambda e: e.scalar_tensor_tensor(out=xt[:], in0=xt[:], scalar=rs[:, 0:1], in1=gB[:],
                                                               op0=ALU.mult, op1=ALU.mult), r=[xt, rs, gB], w=[xt])
                        k.dma(y_d[sq, t0 + t * 128:t0 + (t + 1) * 128, :], xt[:], r=[xt], w=[dt_("y")], own=xt, accum_w=True)
            k.barrier()
        k.finish()
        print("BUILD stats: nins=%d nwait=%d nsems=%d" % (k.nins, k.nwait, len(k.sems)), flush=True)
    return nc


def host_layouts(inp, S):
    f = np.float32
    out = {}
    for n in ["ffn1_w_gate", "ffn1_w_up", "ffn1_w_down", "ffn2_w_gate", "ffn2_w_up", "ffn2_w_down",
              "w_in", "w_uq", "w_ukv", "w_out", "w_mq", "w_mk", "w_mv", "w_mo"]:
        out[n] = np.ascontiguousarray(np.asarray(inp[n], f)[0])
    for n in ["ffn1_pre_g", "ffn1_post_g", "mix_pre_g", "mix_post_g", "mem_pre_g", "mem_kv_norm_g",
              "mem_post_g", "ffn2_pre_g", "ffn2_post_g", "final_norm_g"]:
        out[n + "_bc"] = np.ascontiguousarray(np.broadcast_to(np.asarray(inp[n], f)[0][None, :], (128, D)))
    pre = ["ffn1_pre_g", "mix_pre_g", "mem_pre_g", "mem_kv_norm_g", "ffn2_pre_g"]
    out["gpre"] = np.ascontiguousarray(
        np.stack([np.asarray(inp[n], f)[0].reshape(KC, 128).T for n in pre], axis=1))
    qg = np.asarray(inp["mla_q_norm_g"], f)[0].reshape(4, 128).T
    kg = np.asarray(inp["mla_kv_norm_g"], f)[0].reshape(2, 128).T
    out["qkg"] = np.ascontiguousarray(np.concatenate([qg, kg], axis=1))
    i = np.arange(128)
    p, fr = i[:, None], i[None, :]
    out["consts"] = np.ascontiguousarray(np.stack(
        [np.eye(128), np.ones((128, 128)), fr <= p, fr < p, fr >= p, fr > p], axis=1).astype(f))
    NT = S // 128
    inv_freq = (10000.0 ** (-np.arange(0, ROPE, 2, dtype=f) / f(ROPE))).astype(f)
    ang = (np.arange(S, dtype=f)[:, None] * inv_freq[None, :]).astype(f)
    cos = np.cos(ang).astype(f).reshape(NT, 128, 32).transpose(1, 0, 2)
    sin = np.sin(ang).astype(f).reshape(NT, 128, 32).transpose(1, 0, 2)
    out["rope"] = np.ascontiguousarray(np.stack([cos, sin], axis=1))
    al = np.asarray(inp["gdn_a_log"], f)[0].reshape(16)
    dtb = np.asarray(inp["gdn_dt_bias"], f)[0].reshape(16)
    out["gdnp"] = np.ascontiguousarray(np.broadcast_to(np.stack([al, dtb], 0)[None], (128, 2, 16)))
    out["gon"] = np.ascontiguousarray(np.broadcast_to(np.asarray(inp["gdn_out_norm_g"], f)[0][None, :], (128, 128)))
    cw = np.asarray(inp["gdn_conv_w"], f)[0]
    out["cw"] = np.ascontiguousarray(cw.reshape(5, 24, 128).transpose(2, 1, 0))
    return out


S_FULL = 4096
_NC_CACHE = {}


def kernel(**inputs):
    S = S_FULL
    NSEQ = 2
    if "nc" not in _NC_CACHE:
        _NC_CACHE["nc"] = build(S, NSEQ)
    nc = _NC_CACHE["nc"]
    hl = host_layouts(inputs, S)
    xp = np.asarray(inputs["x_prompt"], np.float32)
    xs = np.asarray(inputs["x_sample"], np.float32)
    mp = np.asarray(inputs["mem_prompt"], np.float32)
    ms = np.asarray(inputs["mem_sample"], np.float32)
    in_maps = []
    for c in range(8):
        m = dict(hl)
        m["x"] = np.ascontiguousarray(np.stack([xp[c], xs[c % 2]], axis=0))
        m["mem"] = np.ascontiguousarray(np.stack([mp[c], ms[c % 2]], axis=0))
        in_maps.append(m)
    res = run_bass_kernel_spmd(nc, in_maps, core_ids=list(range(8)))
    yp = np.stack([np.asarray(res.results[c]["y"])[0] for c in range(8)], axis=0).astype(np.float32)
    ys = np.stack([np.asarray(res.results[c]["y"])[1] for c in range(2)], axis=0).astype(np.float32)
    return (yp, ys)
```

```python
import contextlib
import numpy as np
import concourse.bass as bass
import concourse.mybir as mybir
from concourse.bass_utils import run_bass_kernel_spmd

F32 = mybir.dt.float32
BF16 = mybir.dt.bfloat16
AF = mybir.ActivationFunctionType
ALU = mybir.AluOpType

D = 2048
DFF = 5632
NMEM = 256
EPS = 1e-6
QR, KVR, ROPE, NOPE, VD = 512, 256, 64, 128, 128
INW = 4960
KC = D // 128
FC = DFF // 128


class Trk:
    __slots__ = ("w", "r", "dsem")

    def __init__(self):
        self.w = {}
        self.r = {}
        self.dsem = None


class Buf:
    def __init__(self, t, psum=False):
        self.t = t
        self.T = Trk()
        self.psum = psum

    def __getitem__(self, idx):
        return self.t[idx]


class Eng:
    def __init__(self, name, e, semid, compute):
        self.name, self.e, self.semid, self.compute = name, e, semid, compute
        self.cnt = 0
        self.seen = {}


class K:
    def __init__(self, nc, stack):
        self.nc = nc
        self.stack = stack
        self.sems = []
        self.semcnt = []
        self.eng = {}
        for name, e, comp in (("pe", nc.tensor, True), ("act", nc.scalar, True), ("dve", nc.vector, True),
                              ("pool", nc.gpsimd, True), ("sp", nc.sync, False)):
            sid = self.newsem("e_" + name)
            self.eng[name] = Eng(name, e, sid, comp)
        self.free_dsems = []
        self.nwait = 0
        self.nins = 0

    def newsem(self, name):
        s = self.stack.enter_context(self.nc.semaphore(name))
        self.sems.append(s)
        self.semcnt.append(0)
        return len(self.sems) - 1

    def _wait(self, E, sid, val, raw=True):
        if val <= 0:
            return
        if sid == E.semid:
            if E.name == "pe" or not E.compute or not raw:
                return
        if E.seen.get(sid, 0) >= val:
            return
        E.e.wait_ge(self.sems[sid], val)
        E.seen[sid] = val
        self.nwait += 1

    def _deps(self, r, w):
        need = {}
        for x in r:
            for s, v in x.T.w.items():
                if need.get(s, 0) < v:
                    need[s] = v
            if x.psum:
                for s, v in x.T.r.items():
                    if need.get(s, 0) < v:
                        need[s] = v
        for x in w:
            for s, v in x.T.w.items():
                if need.get(s, 0) < v:
                    need[s] = v
            for s, v in x.T.r.items():
                if need.get(s, 0) < v:
                    need[s] = v
        return need

    def op(self, eng, fn, r=(), w=()):
        E = self.eng[eng]
        rawv = 0
        for x in r:
            rawv = max(rawv, x.T.w.get(E.semid, 0))
        for s, v in self._deps(r, w).items():
            if s == E.semid:
                self._wait(E, s, v, raw=True)
            else:
                self._wait(E, s, v)
        ins = fn(E.e)
        ins.then_inc(self.sems[E.semid], 1)
        E.cnt += 1
        self.semcnt[E.semid] = E.cnt
        self.nins += 1
        for x in r:
            if x.T.r.get(E.semid, 0) < E.cnt:
                x.T.r[E.semid] = E.cnt
        for x in w:
            x.T.w = {E.semid: E.cnt}
            x.T.r = {}

    def pe(self, fn, r=(), w=()):
        self.op("pe", fn, r, w)

    def act(self, fn, r=(), w=()):
        self.op("act", fn, r, w)

    def dve(self, fn, r=(), w=()):
        self.op("dve", fn, r, w)

    def pool(self, fn, r=(), w=()):
        self.op("pool", fn, r, w)

    def dma(self, out, in_, r=(), w=(), own=None, q="sp", accum_w=False):
        E = self.eng[q]
        if own is None:
            own = w[0] if w else r[0]
        T = own.T
        if T.dsem is None:
            T.dsem = self.newsem("d%d" % len(self.sems))
        sid = T.dsem
        need = self._deps(r, [] if accum_w else w)
        if accum_w:
            for x in w:
                for s, v in x.T.r.items():
                    if need.get(s, 0) < v:
                        need[s] = v
        if not accum_w and need.get(sid, 0) < self.semcnt[sid]:
            need[sid] = self.semcnt[sid]
        for s, v in need.items():
            self._wait(E, s, v)
        ins = E.e.dma_start(out=out, in_=in_)
        ins.then_inc(self.sems[sid], 16)
        self.semcnt[sid] += 16
        val = self.semcnt[sid]
        self.nins += 1
        for x in r:
            if x.T.r.get(sid, 0) < val:
                x.T.r[sid] = val
        for x in w:
            if accum_w:
                x.T.w[sid] = val
            else:
                x.T.w = {sid: val}
                x.T.r = {}

    def barrier(self):
        for E in self.eng.values():
            for sid in range(len(self.sems)):
                if sid != E.semid:
                    self._wait(E, sid, self.semcnt[sid])

    def finish(self):
        E = self.eng["sp"]
        for sid in range(len(self.sems)):
            self._wait(E, sid, self.semcnt[sid])


class Pools:
    def __init__(self, k, stack):
        self.k, self.stack, self.n = k, stack, 0

    def sb(self, shape, dt, name=None):
        self.n += 1
        nm = "%s_%d" % (name or "sb", id(self) % 10000 * 1000 + self.n)
        return Buf(self.stack.enter_context(self.k.nc.sbuf_tensor(nm, list(shape), dt)))

    def ps(self, shape, dt, name=None):
        self.n += 1
        nm = "%s_%d" % (name or "ps", id(self) % 10000 * 1000 + self.n)
        return Buf(self.stack.enter_context(self.k.nc.psum_tensor(nm, list(shape), dt)), psum=True)

    def ring(self, n, shape, dt, name=None):
        return Ring([self.sb(shape, dt, name) for _ in range(n)])


class Ring:
    def __init__(self, bufs):
        self.bufs, self.i = bufs, 0

    def next(self):
        b = self.bufs[self.i % len(self.bufs)]
        self.i += 1
        return b


class _StopD(Exception):
    pass


def build(S, NSEQ, dbg=False):
    nc = bass.Bass("TRN2", target_bir_lowering=False)
    flags = set(str(dbg).split("+")) if dbg else set()
    dstop = 99
    cstop = 99
    for f_ in flags:
        if f_.startswith("ds"):
            dstop = int(f_[2:])
        if f_.startswith("cs"):
            cstop = int(f_[2:])
    if "noB" in flags or "noC" in flags or dstop < 99 or cstop != 99:
        dbg = "X"
    NT = S // 128
    TB = min(512, S)
    NB = S // TB
    TPB = TB // 128
    QB = TB
    NQB = S // QB

    def din(name, shape, dt=F32):
        return nc.dram_tensor(name, list(shape), dt, kind="ExternalInput").ap()

    def dscr(name, shape, dt):
        return nc.dram_tensor(name, list(shape), dt, kind=("ExternalOutput" if dbg else "Internal")).ap()

    x_d = din("x", [NSEQ, S, D])
    mem_d = din("mem", [NSEQ, NMEM, D])
    y_d = nc.dram_tensor("y", [NSEQ, S, D], F32, kind="ExternalOutput").ap()
    wnames = {"ffn1_w_gate": (D, DFF), "ffn1_w_up": (D, DFF), "ffn1_w_down": (DFF, D),
              "ffn2_w_gate": (D, DFF), "ffn2_w_up": (D, DFF), "ffn2_w_down": (DFF, D),
              "w_in": (D, INW), "w_uq": (QR, 8 * 192), "w_ukv": (KVR, 8 * 256), "w_out": (D, D),
              "w_mq": (D, 512), "w_mk": (D, 512), "w_mv": (D, 512), "w_mo": (512, D)}
    wf = {n: din(n, s) for n, s in wnames.items()}
    wb = {n: nc.dram_tensor(n + "_b", list(s), BF16, kind="Internal").ap() for n, s in wnames.items()}
    gnames = ["ffn1_pre_g", "ffn1_post_g", "mix_pre_g", "mix_post_g", "mem_pre_g", "mem_kv_norm_g",
              "mem_post_g", "ffn2_pre_g", "ffn2_post_g", "final_norm_g"]
    gbd = {n: din(n + "_bc", [128, D]) for n in gnames}
    gpd = din("gpre", [128, 5, KC])
    qkg_d = din("qkg", [128, 6])
    consts_d = din("consts", [128, 6, 128])
    rope_d = din("rope", [128, 2, NT, 32])
    gdnp_d = din("gdnp", [128, 2, 16])
    gon_d = din("gon", [128, 128])
    cw_d = din("cw", [128, 24, 5])

    h1_d = dscr("h1_s", [S, D], F32)
    cqnT_d = dscr("cqnT_s", [128, 4, S], BF16)
    ckvnT_d = dscr("ckvnT_s", [128, 2, S], BF16)
    krT_d = dscr("krT_s", [64, S], BF16)
    qkvT_d = dscr("qkvT_s", [128, 24, S], F32)
    zs_d = dscr("zs_s", [S, 1024], F32)
    gb_d = dscr("gb_s", [128, NT, 32], F32)
    omixT_d = dscr("omixT_s", [128, 16, S], BF16)

    with contextlib.ExitStack() as gstack:
        k = K(nc, gstack)
        GP = Pools(k, gstack)
        pf = [GP.ps([128, 512], F32, "pf") for _ in range(6)]
        pb = [GP.ps([128, 1024], BF16, "pb") for _ in range(2)]
        pfr = Ring(pf)
        pbr = Ring(pb)
        class DT:
            pass
        dtrk = {}

        def dt_(name):
            if name not in dtrk:
                dtrk[name] = Buf(None)
            return dtrk[name]

        cst_f = GP.sb([128, 6, 128], F32, "cstf")
        cst_b = GP.sb([128, 6, 128], BF16, "cstb")
        k.dma(cst_f[:], consts_d, w=[cst_f])
        k.dve(lambda e: e.tensor_copy(out=cst_b[:], in_=cst_f[:]), r=[cst_f], w=[cst_b])
        IDB = lambda n=128: cst_b[0:n, 0, 0:n]
        ONESB = cst_b[:, 1, :]
        ONESF = cst_f[:, 1, :]
        LOW, SLOW, UP, SUP = (cst_f[:, i, :] for i in (2, 3, 4, 5))
        gpre = GP.sb([128, 5, KC], F32, "gpre")
        k.dma(gpre[:], gpd, w=[gpre])
        qkg = GP.sb([128, 6], F32, "qkg")
        k.dma(qkg[:], qkg_d, w=[qkg])
        PRE = {"ffn1_pre_g": 0, "mix_pre_g": 1, "mem_pre_g": 2, "mem_kv_norm_g": 3, "ffn2_pre_g": 4}

        wtrks = {n: Buf(None) for n in wnames}
        wkey = {id(wb[n]): n for n in wnames}
        corder = ["ffn1_w_gate", "ffn1_w_up", "ffn1_w_down", "w_in", "w_uq", "w_ukv", "w_out", "w_mq", "w_mk", "w_mv", "w_mo",
                  "ffn2_w_gate", "ffn2_w_up", "ffn2_w_down"]
        for n in corder:
            rows, cols = wnames[n]
            step = 256
            for r0 in range(0, rows, step):
                r1 = min(rows, r0 + step)
                k.dma(wb[n][r0:r1, :], wf[n][r0:r1, :], w=[wtrks[n]], own=wtrks[n], q="pool", accum_w=True)

        def rstd_of(P, src, W, junk, extra=1.0):
            sbuf, sap = src
            ss = P["ss"].next()
            k.pool(lambda e: e.memset(ss[:], 0.0), w=[ss])
            k.act(lambda e: e.activation(out=junk[:, 0:W], in_=sap, func=AF.Square, accum_out=ss[:, 0:1]),
                  r=[sbuf, ss], w=[junk, ss])
            rs = P["rs"].next()
            ex2 = float(extra) ** 2
            k.act(lambda e: e.activation(out=rs[:], in_=ss[:], func=AF.Sqrt, scale=1.0 / (W * ex2), bias=EPS / ex2),
                  r=[ss], w=[rs])
            k.dve(lambda e: e.reciprocal(out=rs[:], in_=rs[:]), r=[rs], w=[rs])
            return rs

        def transpose_to(P, nbf, ncols, dstT, t, gidx, alt=[0]):
            nch = ncols // 128
            for c0 in range(0, nch, 8):
                c1 = min(nch, c0 + 8)
                bank = pbr.next()
                for c in range(c0, c1):
                    k.pe(lambda e, c=c: e.transpose(bank[:, (c - c0) * 128:(c - c0 + 1) * 128],
                                                    nbf[:, c * 128:(c + 1) * 128], IDB()),
                         r=[nbf, cst_b], w=[bank])
                for c in range(c0, c1):
                    src = bank[:, (c - c0) * 128:(c - c0 + 1) * 128]
                    dst = dstT[:, c, t * 128:(t + 1) * 128]
                    if gidx is None:
                        fn = lambda e, src=src, dst=dst: e.tensor_copy(out=dst, in_=src)
                        rr = [bank]
                    else:
                        gbuf, g0 = gidx
                        gap = gbuf[:, g0 + c:g0 + c + 1] if len(gbuf.t.shape) == 2 else gbuf[:, g0, c:c + 1]
                        rr = [bank, gbuf]
                        if alt[0] % 2 == 0:
                            fn = lambda e, src=src, dst=dst, gap=gap: e.tensor_scalar(
                                out=dst, in0=src, scalar1=gap, scalar2=None, op0=ALU.mult)
                        else:
                            fn = lambda e, src=src, dst=dst, gap=gap: e.activation(
                                out=dst, in_=src, func=AF.Copy, scale=gap)
                    if gidx is not None and alt[0] % 2 == 1:
                        k.act(fn, r=rr, w=[dstT])
                    else:
                        k.dve(fn, r=rr, w=[dstT])
                alt[0] += 1

        def load_w(P, wd, r0, nkc, c0, ncols, coff=0, slot=None):
            if slot is None:
                slot = P["wring"].next()
            src = wd[r0 * 128:(r0 + nkc) * 128, c0:c0 + ncols].rearrange("(c p) f -> p c f", p=128)
            k.dma(slot[:, 0:nkc, coff:coff + ncols], src, r=[wtrks[wkey[id(wd)]]], w=[slot], own=slot, accum_w=(coff != 0))
            return slot

        def norm_transpose_block(P, load_tile, gain_name, nT, keep=None):
            for t in range(TPB):
                xt = load_tile(t)
                rs = rstd_of(P, (xt, xt[:, :]), D, P["junk"])
                nb = P["nbf"].next()
                k.act(lambda e: e.activation(out=nb[:], in_=xt[:], func=AF.Copy, scale=rs[:, 0:1]),
                      r=[xt, rs], w=[nb])
                transpose_to(P, nb, D, nT, t, (gpre, PRE[gain_name]))

        def ffn(P, nT, wg, wu, wd, ysb):
            hT = P["hT"]
            for g in range(FC // 2):
                sgu = load_w(P, wg, 0, KC, g * 256, 256)
                load_w(P, wu, 0, KC, g * 256, 256, coff=256, slot=sgu)
                for f in range(2):
                    pg, pu = pfr.next(), pfr.next()
                    for kc in range(KC):
                        k.pe(lambda e, kc=kc: e.matmul(pg[:, 0:TB], lhsT=sgu[:, kc, f * 128:(f + 1) * 128],
                                                       rhs=nT[:, kc, :], start=(kc == 0), stop=(kc == KC - 1)),
                             r=[sgu, nT], w=[pg])
                    for kc in range(KC):
                        k.pe(lambda e, kc=kc: e.matmul(pu[:, 0:TB], lhsT=sgu[:, kc, 256 + f * 128:256 + (f + 1) * 128],
                                                       rhs=nT[:, kc, :], start=(kc == 0), stop=(kc == KC - 1)),
                             r=[sgu, nT], w=[pu])
                    sl = P["silu"].next()
                    k.act(lambda e: e.activation(out=sl[:, 0:TB], in_=pg[:, 0:TB], func=AF.Silu), r=[pg], w=[sl])
                    fc = g * 2 + f
                    k.dve(lambda e: e.tensor_tensor(out=hT[:, fc, :], in0=sl[:, 0:TB], in1=pu[:, 0:TB], op=ALU.mult),
                          r=[sl, pu], w=[hT])
            for dg in range(4):
                accs = [pf[i] for i in range(TPB)]
                for fg in range(4):
                    sd = load_w(P, wd, fg * 11, 11, dg * 512, 512)
                    for t in range(TPB):
                        for f in range(11):
                            fc = fg * 11 + f
                            k.pe(lambda e, t=t, f=f, fc=fc: e.matmul(
                                accs[t][:, :], lhsT=hT[:, fc, t * 128:(t + 1) * 128], rhs=sd[:, f, :],
                                start=(fc == 0), stop=(fc == FC - 1)), r=[hT, sd], w=[accs[t]])
                for t in range(TPB):
                    k.act(lambda e, t=t: e.activation(out=ysb[t][:, dg * 512:(dg + 1) * 512], in_=accs[t][:, :],
                                                      func=AF.Copy), r=[accs[t]], w=[ysb[t]])
            pfr.i = 0

        def post_residual(P, ysrc, gname, base, out, half):
            rs = rstd_of(P, (ysrc, ysrc[:, :]), D, P["junk"], extra=(0.5 if half else 1.0))
            gB = P["gB"].next()
            k.dma(gB[:], gbd[gname], w=[gB])
            k.dve(lambda e: e.scalar_tensor_tensor(out=ysrc[:], in0=ysrc[:], scalar=rs[:, 0:1], in1=gB[:],
                                                   op0=ALU.mult, op1=ALU.mult), r=[ysrc, rs, gB], w=[ysrc])
            k.pool(lambda e: e.tensor_tensor(out=out[:], in0=base[:], in1=ysrc[:], op=ALU.add),
                   r=[base, ysrc], w=[out])

        for sq in range(NSEQ):
            with contextlib.ExitStack() as st:
                A = Pools(k, st)
                P = {"ss": A.ring(4, [128, 1], F32), "rs": A.ring(4, [128, 1], F32),
                     "junk": A.sb([128, D], BF16), "nbf": A.ring(1, [128, D], BF16),
                     "wring": A.ring(3, [128, KC, 512], BF16), "hT": A.sb([128, FC, TB], BF16),
                     "silu": A.ring(2, [128, 512], F32), "gB": A.ring(1, [128, D], F32)}
                nT = A.sb([128, KC, TB], BF16)
                xr = A.ring(2, [128, D], F32)
                ysb = [A.sb([128, D], F32) for _ in range(TPB)]
                cs = A.sb([128, 2, TPB, 32], F32)
                gdnp = A.sb([128, 2, 16], F32)
                k.dma(gdnp[:], gdnp_d, w=[gdnp])
                negA = A.sb([128, 16], F32)
                k.act(lambda e: e.activation(out=negA[:], in_=gdnp[:, 0, :], func=AF.Exp), r=[gdnp], w=[negA])
                k.dve(lambda e: e.tensor_scalar(out=negA[:], in0=negA[:], scalar1=-1.0, scalar2=None, op0=ALU.mult),
                      r=[negA], w=[negA])
                cqT = A.sb([128, 4, TB], BF16)
                ckT = A.sb([128, 2, TB], BF16)
                krT = A.sb([64, TB], BF16)
                gbs = A.sb([128, TPB, 32], F32)
                stg = A.ring(1, [128, 4, TB], F32)
                small = A.ring(2, [128, 512], F32)
                smallb = A.ring(3, [128, 512], BF16)
                tiny = A.ring(8, [128, 64], F32)
                for blk in range(NB):
                    t0 = blk * TB
                    k.dma(cs[:], rope_d[:, :, blk * TPB:(blk + 1) * TPB, :], w=[cs])

                    def load_x(t):
                        xt = xr.next()
                        k.dma(xt[:], x_d[sq, t0 + t * 128:t0 + (t + 1) * 128, :], w=[xt])
                        return xt
                    norm_transpose_block(P, load_x, "ffn1_pre_g", nT)
                    ffn(P, nT, wb["ffn1_w_gate"], wb["ffn1_w_up"], wb["ffn1_w_down"], ysb)
                    h1t = {}

                    def load_h1(t):
                        xt = load_x(t)
                        post_residual(P, ysb[t], "ffn1_post_g", xt, xt, True)
                        k.dma(h1_d[t0 + t * 128:t0 + (t + 1) * 128, :], xt[:], r=[xt], w=[dt_("h1%d" % (blk))],
                              own=xt, accum_w=True)
                        return xt
                    norm_transpose_block(P, load_h1, "mix_pre_g", nT)
                    s0 = load_w(P, wb["w_in"], 0, KC, 0, 512)
                    for t in range(TPB):
                        acc = pfr.next()
                        for kc in range(KC):
                            k.pe(lambda e, kc=kc: e.matmul(acc[:, :], lhsT=nT[:, kc, t * 128:(t + 1) * 128],
                                                           rhs=s0[:, kc, :], start=(kc == 0), stop=(kc == KC - 1)),
                                 r=[nT, s0], w=[acc])
                        cq = small.next()
                        k.act(lambda e: e.activation(out=cq[:], in_=acc[:, :], func=AF.Copy), r=[acc], w=[cq])
                        rs = rstd_of(P, (cq, cq[:, :]), 512, P["junk"])
                        cqn = smallb.next()
                        k.act(lambda e: e.activation(out=cqn[:], in_=cq[:], func=AF.Copy, scale=rs[:, 0:1]),
                              r=[cq, rs], w=[cqn])
                        transpose_to(P, cqn, 512, cqT, t, (qkg, 0))
                    k.dma(cqnT_d[:, :, t0:t0 + TB], cqT[:], r=[cqT], w=[dt_("cq%d" % blk)], own=cqT)
                    s1 = load_w(P, wb["w_in"], 0, KC, 512, 320)
                    load_w(P, wb["w_in"], 0, KC, 4928, 32, coff=320, slot=s1)
                    for t in range(TPB):
                        acc = pfr.next()
                        for kc in range(KC):
                            k.pe(lambda e, kc=kc: e.matmul(acc[:, 0:352], lhsT=nT[:, kc, t * 128:(t + 1) * 128],
                                                           rhs=s1[:, kc, 0:352], start=(kc == 0), stop=(kc == KC - 1)),
                                 r=[nT, s1], w=[acc])
                        ck = small.next()
                        k.act(lambda e: e.activation(out=ck[:, 0:352], in_=acc[:, 0:352], func=AF.Copy),
                              r=[acc], w=[ck])
                        rs = rstd_of(P, (ck, ck[:, 0:256]), 256, P["junk"])
                        ckn = smallb.next()
                        k.act(lambda e: e.activation(out=ckn[:, 0:256], in_=ck[:, 0:256], func=AF.Copy,
                                                     scale=rs[:, 0:1]), r=[ck, rs], w=[ckn])
                        transpose_to(P, ckn, 256, ckT, t, (qkg, 4))
                        cos, sin = cs[:, 0, t, :], cs[:, 1, t, :]
                        ta, tb_ = tiny.next(), tiny.next()
                        x1, x2 = ck[:, 256:288], ck[:, 288:320]
                        k.dve(lambda e: e.tensor_tensor(out=ta[:, 0:32], in0=x1, in1=cos, op=ALU.mult), r=[ck, cs], w=[ta])
                        k.dve(lambda e: e.tensor_tensor(out=ta[:, 32:64], in0=x2, in1=cos, op=ALU.mult), r=[ck, cs], w=[ta])
                        k.pool(lambda e: e.tensor_tensor(out=tb_[:, 0:32], in0=x2, in1=sin, op=ALU.mult), r=[ck, cs], w=[tb_])
                        k.pool(lambda e: e.tensor_tensor(out=tb_[:, 32:64], in0=x1, in1=sin, op=ALU.mult), r=[ck, cs], w=[tb_])
                        krb = smallb.next()
                        k.dve(lambda e: e.tensor_tensor(out=krb[:, 0:32], in0=ta[:, 0:32], in1=tb_[:, 0:32],
                                                        op=ALU.subtract), r=[ta, tb_], w=[krb])
                        k.dve(lambda e: e.tensor_tensor(out=krb[:, 32:64], in0=ta[:, 32:64], in1=tb_[:, 32:64],
                                                        op=ALU.add), r=[ta, tb_], w=[krb])
                        bank = pbr.next()
                        k.pe(lambda e: e.transpose(bank[0:64, 0:128], krb[:, 0:64], IDB()), r=[krb, cst_b], w=[bank])
                        k.dve(lambda e: e.tensor_copy(out=krT[:, t * 128:(t + 1) * 128], in_=bank[0:64, 0:128]),
                              r=[bank], w=[krT])
                        a_, b_ = ck[:, 320:336], ck[:, 336:352]
                        u0, u1, u2 = tiny.next(), tiny.next(), tiny.next()
                        k.dve(lambda e: e.tensor_tensor(out=u0[:, 0:16], in0=a_, in1=gdnp[:, 1, :], op=ALU.add),
                              r=[ck, gdnp], w=[u0])
                        k.dve(lambda e: e.tensor_scalar(out=u1[:, 0:16], in0=u0[:, 0:16], scalar1=-1.0, scalar2=None,
                                                        op0=ALU.mult), r=[u0], w=[u1])
                        k.dve(lambda e: e.tensor_tensor(out=u1[:, 0:16], in0=u0[:, 0:16], in1=u1[:, 0:16], op=ALU.min),
                              r=[u0, u1], w=[u1])
                        k.act(lambda e: e.activation(out=u1[:, 0:16], in_=u1[:, 0:16], func=AF.Exp),
                              r=[u1], w=[u1])
                        k.act(lambda e: e.activation(out=u1[:, 0:16], in_=u1[:, 0:16], func=AF.Ln, bias=1.0),
                              r=[u1], w=[u1])
                        k.dve(lambda e: e.scalar_tensor_tensor(out=u2[:, 0:16], in0=u0[:, 0:16], scalar=0.0,
                                                               in1=u1[:, 0:16], op0=ALU.max, op1=ALU.add),
                              r=[u0, u1], w=[u2])
                        k.dve(lambda e: e.tensor_tensor(out=gbs[:, t, 0:16], in0=u2[:, 0:16], in1=negA[:], op=ALU.mult),
                              r=[u2, negA], w=[gbs])
                        k.act(lambda e: e.activation(out=gbs[:, t, 16:32], in_=b_, func=AF.Sigmoid), r=[ck], w=[gbs])
                    k.dma(ckvnT_d[:, :, t0:t0 + TB], ckT[:], r=[ckT], w=[dt_("ck%d" % blk)], own=ckT)
                    k.dma(krT_d[:, t0:t0 + TB], krT[:], r=[krT], w=[dt_("kr%d" % blk)], own=krT)
                    k.dma(gb_d[:, blk * TPB:(blk + 1) * TPB, :], gbs[:], r=[gbs], w=[dt_("gb%d" % blk)], own=gbs)
                    for g in range(6):
                        sw = load_w(P, wb["w_in"], 0, KC, 832 + g * 512, 512)
                        sg = stg.next()
                        for f in range(4):
                            acc = pfr.next()
                            for kc in range(KC):
                                k.pe(lambda e, kc=kc: e.matmul(acc[:, 0:TB], lhsT=sw[:, kc, f * 128:(f + 1) * 128],
                                                               rhs=nT[:, kc, :], start=(kc == 0), stop=(kc == KC - 1)),
                                     r=[sw, nT], w=[acc])
                            if f % 2 == 0:
                                k.act(lambda e: e.activation(out=sg[:, f, :], in_=acc[:, 0:TB], func=AF.Copy),
                                      r=[acc], w=[sg])
                            else:
                                k.dve(lambda e: e.tensor_copy(out=sg[:, f, :], in_=acc[:, 0:TB]), r=[acc], w=[sg])
                        k.dma(qkvT_d[:, g * 4:(g + 1) * 4, t0:t0 + TB], sg[:, 0:4, :], r=[sg],
                              w=[dt_("qkv%d" % blk)], own=sg, accum_w=True)
                    sz = [load_w(P, wb["w_in"], 0, KC, 3904 + g * 512, 512) for g in range(2)]
                    for t in range(TPB):
                        zt = stg.next()
                        for g in range(2):
                            acc = pfr.next()
                            for kc in range(KC):
                                k.pe(lambda e, kc=kc: e.matmul(acc[:, :], lhsT=nT[:, kc, t * 128:(t + 1) * 128],
                                                               rhs=sz[g][:, kc, :], start=(kc == 0), stop=(kc == KC - 1)),
                                     r=[nT, sz[g]], w=[acc])
                            k.act(lambda e: e.activation(out=zt[:, g, :], in_=acc[:, :], func=AF.Silu),
                                  r=[acc], w=[zt])
                        k.dma(zs_d[t0 + t * 128:t0 + (t + 1) * 128, :].rearrange("s (g c) -> s g c", g=2), zt[:, 0:2, :],
                              r=[zt], w=[dt_("zs%d" % blk)],
                              own=zt, accum_w=True)
            k.barrier()
            if dbg == "A":
                break
            with contextlib.ExitStack() as st:
                B = Pools(k, st)
                cqT = B.sb([128, 4, S], BF16)
                ckT = B.sb([128, 2, S], BF16)
                krT = B.sb([64, S], BF16)
                rA = [dt_("cq%d" % b) for b in range(NB)] + [dt_("ck%d" % b) for b in range(NB)] + \
                     [dt_("kr%d" % b) for b in range(NB)]
                k.dma(cqT[:], cqnT_d, r=rA, w=[cqT])
                k.dma(ckT[:], ckvnT_d, r=rA, w=[ckT])
                k.dma(krT[:], krT_d, r=rA, w=[krT])
                wuq = B.sb([128, 4, 8 * 192], BF16)
                wukv = B.sb([128, 2, 8 * 256], BF16)
                k.dma(wuq[:], wb["w_uq"].rearrange("(c p) f -> p c f", p=128), r=[wtrks["w_uq"]], w=[wuq])
                k.dma(wukv[:], wb["w_ukv"].rearrange("(c p) f -> p c f", p=128), r=[wtrks["w_ukv"]], w=[wukv])
                cs = B.sb([128, 2, NT, 32], F32)
                k.dma(cs[:], rope_d, w=[cs])
                KT = B.sb([128, S], BF16)
                QnT = B.sb([128, S], BF16)
                QrT = B.sb([64, S], BF16)
                Vh = B.sb([128, NT * 128], BF16)
                OT = B.sb([128, S], BF16)
                Pr = B.ring(3, [128, QB], BF16)
                rinv = B.ring(2, [128, QB], F32)
                qra = B.ring(2, [128, 8, 64], F32)
                qrb = B.ring(2, [128, 8, 64], F32)
                qrbf = B.ring(2, [128, 8, 64], BF16)
                scale = float((NOPE + ROPE) ** -0.5)
                G8 = min(8, NT)
                for h in range(8):
                    if dbg in ("B0", "C", "C0", "C1", "D0", "D1", "CD") or "noB" in flags:
                        break
                    for blk in range(NQB):
                        sl = slice(blk * QB, (blk + 1) * QB)
                        acc = pfr.next()
                        for c in range(2):
                            k.pe(lambda e, c=c: e.matmul(acc[:, 0:QB], lhsT=wukv[:, c, h * 256:h * 256 + 128],
                                                         rhs=ckT[:, c, sl], start=(c == 0), stop=(c == 1)),
                                 r=[wukv, ckT], w=[acc])
                        k.act(lambda e: e.activation(out=KT[:, sl], in_=acc[:, 0:QB], func=AF.Copy), r=[acc], w=[KT])
                        acc2 = pfr.next()
                        for c in range(4):
                            k.pe(lambda e, c=c: e.matmul(acc2[:, 0:QB], lhsT=wuq[:, c, h * 192:h * 192 + 128],
                                                         rhs=cqT[:, c, sl], start=(c == 0), stop=(c == 3)),
                                 r=[wuq, cqT], w=[acc2])
                        k.dve(lambda e: e.tensor_copy(out=QnT[:, sl], in_=acc2[:, 0:QB]), r=[acc2], w=[QnT])
                    for tg in range(0, NT, 4):
                        acc = pfr.next()
                        for t in range(tg, min(NT, tg + 4)):
                            for c in range(2):
                                k.pe(lambda e, c=c, t=t: e.matmul(
                                    acc[:, (t - tg) * 128:(t - tg + 1) * 128], lhsT=ckT[:, c, t * 128:(t + 1) * 128],
                                    rhs=wukv[:, c, h * 256 + 128:(h + 1) * 256], start=(c == 0), stop=(c == 1)),
                                    r=[wukv, ckT], w=[acc])
                        n4 = min(NT, tg + 4) - tg
                        k.act(lambda e: e.activation(out=Vh[:, tg * 128:(tg + n4) * 128], in_=acc[:, 0:n4 * 128],
                                                     func=AF.Copy), r=[acc], w=[Vh])
                    if dbg == "B1":
                        break
                    for tg in range(0, NT, G8):
                        acc = pfr.next()
                        for t in range(tg, tg + G8):
                            for c in range(4):
                                k.pe(lambda e, c=c, t=t: e.matmul(
                                    acc[:, (t - tg) * 64:(t - tg + 1) * 64], lhsT=cqT[:, c, t * 128:(t + 1) * 128],
                                    rhs=wuq[:, c, h * 192 + 128:(h + 1) * 192], start=(c == 0), stop=(c == 3)),
                                    r=[wuq, cqT], w=[acc])
                        av = acc[:, 0:G8 * 64].rearrange("p (t r) -> p t r", r=64)
                        cos, sin = cs[:, 0, tg:tg + G8, :], cs[:, 1, tg:tg + G8, :]
                        ta, tb_, qb_ = qra.next(), qrb.next(), qrbf.next()
                        k.dve(lambda e: e.tensor_tensor(out=ta[:, 0:G8, 0:32], in0=av[:, :, 0:32], in1=cos, op=ALU.mult),
                              r=[acc, cs], w=[ta])
                        k.dve(lambda e: e.tensor_tensor(out=ta[:, 0:G8, 32:64], in0=av[:, :, 32:64], in1=cos, op=ALU.mult),
                              r=[acc, cs], w=[ta])
                        k.dve(lambda e: e.tensor_tensor(out=tb_[:, 0:G8, 0:32], in0=av[:, :, 32:64], in1=sin, op=ALU.mult),
                              r=[acc, cs], w=[tb_])
                        k.dve(lambda e: e.tensor_tensor(out=tb_[:, 0:G8, 32:64], in0=av[:, :, 0:32], in1=sin, op=ALU.mult),
                              r=[acc, cs], w=[tb_])
                        k.pool(lambda e: e.tensor_tensor(out=qb_[:, 0:G8, 0:32], in0=ta[:, 0:G8, 0:32],
                                                         in1=tb_[:, 0:G8, 0:32], op=ALU.subtract), r=[ta, tb_], w=[qb_])
                        k.pool(lambda e: e.tensor_tensor(out=qb_[:, 0:G8, 32:64], in0=ta[:, 0:G8, 32:64],
                                                         in1=tb_[:, 0:G8, 32:64], op=ALU.add), r=[ta, tb_], w=[qb_])
                        bank = pbr.next()
                        for t in range(G8):
                            k.pe(lambda e, t=t: e.transpose(bank[0:64, t * 128:(t + 1) * 128], qb_[:, t, :], IDB()),
                                 r=[qb_, cst_b], w=[bank])
                        k.act(lambda e: e.activation(out=QrT[:, tg * 128:(tg + G8) * 128], in_=bank[0:64, 0:G8 * 128],
                                                     func=AF.Copy), r=[bank], w=[QrT])
                    if dbg == "B2":
                        break
                    for qb in range(NQB):
                        qs = slice(qb * QB, (qb + 1) * QB)
                        accO, accR = pf[(qb % 2) * 2], pf[(qb % 2) * 2 + 1]
                        sps = [pf[4], pf[5]]

                        def qk(kt):
                            sp_ = sps[kt % 2]
                            k.pe(lambda e: e.matmul(sp_[:, 0:QB], lhsT=KT[:, kt * 128:(kt + 1) * 128], rhs=QnT[:, qs],
                                                    start=True, stop=False), r=[KT, QnT], w=[sp_])
                            k.pe(lambda e: e.matmul(sp_[:, 0:QB], lhsT=krT[:, kt * 128:(kt + 1) * 128], rhs=QrT[:, qs],
                                                    start=False, stop=True), r=[krT, QrT], w=[sp_])
                        qk(0)
                        for kt in range(NT):
                            if kt + 1 < NT:
                                qk(kt + 1)
                            sp_ = sps[kt % 2]
                            p_ = Pr.next()
                            k.act(lambda e: e.activation(out=p_[:], in_=sp_[:, 0:QB], func=AF.Exp, scale=scale),
                                  r=[sp_], w=[p_])
                            k.pe(lambda e: e.matmul(accO[:, 0:QB], lhsT=Vh[:, kt * 128:(kt + 1) * 128], rhs=p_[:], start=(kt == 0),
                                                    stop=(kt == NT - 1)), r=[Vh, p_], w=[accO])
                            k.pe(lambda e: e.matmul(accR[:, 0:QB], lhsT=ONESB, rhs=p_[:], start=(kt == 0),
                                                    stop=(kt == NT - 1)), r=[cst_b, p_], w=[accR])
                        ri = rinv.next()
                        k.dve(lambda e: e.reciprocal(out=ri[:], in_=accR[:, 0:QB]), r=[accR], w=[ri])
                        k.dve(lambda e: e.tensor_tensor(out=OT[:, qs], in0=accO[:, 0:QB], in1=ri[:], op=ALU.mult),
                              r=[accO, ri], w=[OT])
                    k.dma(omixT_d[:, h, :], OT[:], r=[OT], w=[dt_("omla")], own=OT, accum_w=True)
                    pfr.i = 0
            k.barrier()
            if dbg in ("B", "B0", "B1", "B2"):
                break
            with contextlib.ExitStack() as st:
                C = Pools(k, st)
                try:
                    P = {"ss": C.ring(4, [128, 1], F32), "rs": C.ring(4, [128, 1], F32), "junk": C.sb([128, 128], BF16)}
                    gon = C.sb([128, 128], F32)
                    k.dma(gon[:], gon_d, w=[gon])
                    cw = C.sb([128, 24, 5], F32)
                    k.dma(cw[:], cw_d, w=[cw])
                    if cstop == 1:
                        raise _StopD()
                    W16 = NT * 16
                    H8 = NT * 8
                    gcs, eg, egs, ek, dec, bgc, nbeta, gq, bq, grem = (C.sb([128, W16], F32) for _ in range(10))
                    DKS = float(128 ** -0.5)
                    for d_ in range(2):
                        k.dma(gq[:, d_ * H8:(d_ + 1) * H8].rearrange("p (t n) -> p t n", n=8), gb_d[:, :, d_ * 8:(d_ + 1) * 8],
                              r=[dt_("gb%d" % b_) for b_ in range(NB)], w=[gq], own=gq, accum_w=(d_ == 1))
                        k.dma(bq[:, d_ * H8:(d_ + 1) * H8].rearrange("p (t n) -> p t n", n=8), gb_d[:, :, 16 + d_ * 8:16 + (d_ + 1) * 8],
                              r=[dt_("gb%d" % b_) for b_ in range(NB)], w=[bq], own=bq, accum_w=(d_ == 1))
                    if cstop == 2:
                        raise _StopD()
                    gpb = [C.sb([128, W16], BF16) for _ in range(3)]
                    gpf = [C.sb([128, W16], F32) for _ in range(3)]
                    k.dve(lambda e: e.tensor_copy(out=grem[:], in_=gq[:]), r=[gq], w=[grem])
                    for i3 in range(3):
                        k.dve(lambda e, i3=i3: e.tensor_copy(out=gpb[i3][:], in_=grem[:]), r=[grem], w=[gpb[i3]])
                        k.dve(lambda e, i3=i3: e.tensor_copy(out=gpf[i3][:], in_=gpb[i3][:]), r=[gpb[i3]], w=[gpf[i3]])
                        if i3 < 2:
                            k.dve(lambda e, i3=i3: e.tensor_tensor(out=grem[:], in0=grem[:], in1=gpf[i3][:], op=ALU.subtract),
                                  r=[grem, gpf[i3]], w=[grem])
                    if cstop == 3:
                        raise _StopD()
                    UPB, LOWB = cst_b[:, 4, :], cst_b[:, 2, :]
                    psA_, psT_ = pfr.next(), pfr.next()
                    for i3 in range(3):
                        k.pe(lambda e, i3=i3: e.matmul(psA_[:, 0:H8], lhsT=UPB, rhs=gpb[i3][:, 0:H8], start=(i3 == 0), stop=(i3 == 2)),
                             r=[cst_b, gpb[i3]], w=[psA_])
                    for i3 in range(3):
                        k.pe(lambda e, i3=i3: e.matmul(psA_[:, H8:W16], lhsT=LOWB, rhs=gpb[i3][:, H8:W16], start=(i3 == 0), stop=(i3 == 2)),
                             r=[cst_b, gpb[i3]], w=[psA_])
                    for i3 in range(3):
                        k.pe(lambda e, i3=i3: e.matmul(psT_[:, 0:W16], lhsT=ONESB, rhs=gpb[i3][:, :], start=(i3 == 0), stop=(i3 == 2)),
                             r=[cst_b, gpb[i3]], w=[psT_])
                    if cstop == 4:
                        raise _StopD()
                    k.act(lambda e: e.activation(out=gcs[:], in_=psA_[:, 0:W16], func=AF.Copy), r=[psA_], w=[gcs])
                    k.act(lambda e: e.activation(out=eg[:], in_=psA_[:, 0:W16], func=AF.Exp), r=[psA_], w=[eg])
                    k.act(lambda e: e.activation(out=dec[:], in_=psT_[:, 0:W16], func=AF.Exp), r=[psT_], w=[dec])
                    if cstop == 41:
                        raise _StopD()
                    k.dve(lambda e: e.tensor_tensor(out=ek[:], in0=psT_[:, 0:W16], in1=gcs[:], op=ALU.subtract), r=[psT_, gcs, dec], w=[ek])
                    if cstop == 411:
                        raise _StopD()
                    k.dve(lambda e: e.tensor_tensor(out=bgc[:], in0=bq[:], in1=eg[:], op=ALU.mult), r=[bq, eg], w=[bgc])
                    if cstop == 412:
                        raise _StopD()
                    k.dve(lambda e: e.tensor_scalar(out=nbeta[:], in0=bq[:], scalar1=-1.0, scalar2=None, op0=ALU.mult), r=[bq], w=[nbeta])
                    if cstop == 42:
                        raise _StopD()
                    k.act(lambda e: e.activation(out=ek[:], in_=ek[:], func=AF.Exp), r=[ek], w=[ek])
                    k.dve(lambda e: e.tensor_scalar(out=egs[:], in0=eg[:], scalar1=DKS, scalar2=None, op0=ALU.mult),
                          r=[eg], w=[egs])
                    if cstop == 5:
                        raise _StopD()
                    raw = C.sb([128, S + 4], F32)
                    cacc = C.sb([128, S], F32)
                    sil = cacc
                    sqb = C.sb([128, S], BF16)
                    qT, kT, vT = (C.sb([128, S], BF16) for _ in range(3))
                    o_d = [C.sb([128, S], F32) for _ in range(2)]
                    z_h = C.sb([128, NT, 128], F32)
                    ogT = C.sb([128, S], BF16)
                    rnr = C.ring(2, [128, QB], F32)
                    S32s = [C.sb([128, 128], F32) for _ in range(2)]
                    Sbfs = [C.sb([128, 128], BF16) for _ in range(2)]
                    f128 = C.ring(16, [128, 128], F32)
                    b128 = C.ring(100, [128, 128], BF16)
                    k.dve(lambda e: e.memset(raw[:, 0:4], 0.0), w=[raw])
                    k.dve(lambda e: e.memset(raw[:, S:S + 4], 0.0), w=[raw])
                    qkv_r = [dt_("qkv%d" % b) for b in range(NB)]
                    zs_r = [dt_("zs%d" % b) for b in range(NB)]
                    for h in range(8):
                        if dbg in ("C0", "D0", "D1", "BD") or "noC" in flags:
                            break
                        for which, dst in ((0, qT), (1, kT), (2, vT)):
                            ch = which * 8 + h
                            k.dma(raw[:, 2:S + 2], qkvT_d[:, ch, :], r=qkv_r, w=[raw])
                            k.dve(lambda e: e.tensor_scalar(out=cacc[:], in0=raw[:, 0:S], scalar1=cw[:, ch, 0:1], scalar2=None,
                                                            op0=ALU.mult), r=[raw, cw], w=[cacc])
                            for j in range(1, 5):
                                k.dve(lambda e, j=j: e.scalar_tensor_tensor(out=cacc[:], in0=raw[:, j:j + S],
                                                                            scalar=cw[:, ch, j:j + 1], in1=cacc[:],
                                                                            op0=ALU.mult, op1=ALU.add), r=[raw, cw, cacc], w=[cacc])
                            if which == 2:
                                k.act(lambda e: e.activation(out=dst[:], in_=cacc[:], func=AF.Silu), r=[cacc], w=[dst])
                                continue
                            k.act(lambda e: e.activation(out=sil[:], in_=cacc[:], func=AF.Silu), r=[cacc], w=[sil])
                            k.act(lambda e: e.activation(out=sqb[:], in_=sil[:], func=AF.Square), r=[sil], w=[sqb])
                            for blk in range(NQB):
                                sl = slice(blk * QB, (blk + 1) * QB)
                                ps = pfr.next()
                                k.pe(lambda e: e.matmul(ps[:, 0:QB], lhsT=ONESB, rhs=sqb[:, sl], start=True, stop=True),
                                     r=[cst_b, sqb], w=[ps])
                                rn = rnr.next()
                                k.act(lambda e: e.activation(out=rn[:], in_=ps[:, 0:QB], func=AF.Sqrt, bias=EPS), r=[ps], w=[rn])
                                k.dve(lambda e: e.reciprocal(out=rn[:], in_=rn[:]), r=[rn], w=[rn])
                                k.dve(lambda e: e.tensor_tensor(out=dst[:, sl], in0=sil[:, sl], in1=rn[:], op=ALU.mult),
                                      r=[sil, rn], w=[dst])
                        if dbg == "C1":
                            break
                        k.dma(z_h[:], zs_d[:, h * 128:(h + 1) * 128].rearrange("(t p) v -> p t v", p=128), r=zs_r, w=[z_h])
                        def unit_gen(dr):
                            TRI = UP if dr == 0 else LOW
                            SM = SLOW if dr == 0 else SUP
                            IMT = UP if dr == 0 else LOW
                            S32, Sbf = S32s[dr], Sbfs[dr]
                            k.pool(lambda e: e.memset(S32[:], 0.0), w=[S32])
                            k.pool(lambda e: e.memset(Sbf[:], 0.0), w=[Sbf])
                            od = o_d[dr]
                            for c in (range(NT) if dr == 0 else range(NT - 1, -1, -1)):
                                cs_ = slice(c * 128, (c + 1) * 128)
                                ci = dr * H8 + c * 8 + h
                                bcol = bq[:, ci:ci + 1]
                                psg = pfr.next()
                                for i3 in range(3):
                                    gT = b128.next()
                                    k.dve(lambda e, i3=i3: e.tensor_scalar(out=gT[:], in0=TRI, scalar1=gpf[i3][:, ci:ci + 1],
                                                                           scalar2=None, op0=ALU.mult), r=[cst_f, gpf[i3]], w=[gT])
                                    k.pe(lambda e, i3=i3: e.matmul(psg[:, 0:128], lhsT=ONESB, rhs=gT[:], start=(i3 == 0), stop=(i3 == 2)),
                                         r=[cst_b, gT], w=[psg])
                                yield
                                dm, dtm = f128.next(), f128.next()
                                k.dve(lambda e: e.tensor_scalar(out=dm[:], in0=psg[:, 0:128], scalar1=gcs[:, ci:ci + 1], scalar2=0.0,
                                                                op0=ALU.subtract, op1=ALU.max), r=[psg, gcs], w=[dm])
                                k.dve(lambda e: e.tensor_scalar(out=dtm[:], in0=psg[:, 0:128], scalar1=gcs[:, ci:ci + 1], scalar2=0.0,
                                                                op0=ALU.subtract, op1=ALU.min), r=[psg, gcs], w=[dtm])
                                k.act(lambda e: e.activation(out=dm[:], in_=dm[:], func=AF.Exp, scale=-1.0), r=[dm], w=[dm])
                                k.act(lambda e: e.activation(out=dtm[:], in_=dtm[:], func=AF.Exp), r=[dtm], w=[dtm])
                                k.pool(lambda e: e.tensor_tensor(out=dm[:], in0=dm[:], in1=SM, op=ALU.mult), r=[dm, cst_f], w=[dm])
                                k.pool(lambda e: e.tensor_tensor(out=dtm[:], in0=dtm[:], in1=IMT, op=ALU.mult), r=[dtm, cst_f], w=[dtm])
                                psG = pfr.next()
                                k.pe(lambda e: e.matmul(psG[:, 0:128], lhsT=kT[:, cs_], rhs=kT[:, cs_], start=True, stop=True),
                                     r=[kT], w=[psG])
                                psK = pfr.next()
                                k.pe(lambda e: e.matmul(psK[:, 0:128], lhsT=kT[:, cs_], rhs=qT[:, cs_], start=True, stop=True),
                                     r=[kT, qT], w=[psK])
                                bank2 = pbr.next()
                                k.pe(lambda e: e.transpose(bank2[:, 0:128], kT[:, cs_], IDB()), r=[kT, cst_b], w=[bank2])
                                k.pe(lambda e: e.transpose(bank2[:, 128:256], vT[:, cs_], IDB()), r=[vT, cst_b], w=[bank2])
                                yield
                                Ln = b128.next()
                                k.dve(lambda e: e.scalar_tensor_tensor(out=Ln[:], in0=psG[:, 0:128], scalar=nbeta[:, ci:ci + 1],
                                                                       in1=dm[:], op0=ALU.mult, op1=ALU.mult),
                                      r=[psG, nbeta, dm], w=[Ln])
                                AT = b128.next()
                                k.dve(lambda e: e.scalar_tensor_tensor(out=AT[:], in0=psK[:, 0:128], scalar=DKS, in1=dtm[:],
                                                                       op0=ALU.mult, op1=ALU.mult), r=[psK, dtm], w=[AT])
                                kbg, kd, vb = b128.next(), b128.next(), b128.next()
                                k.act(lambda e: e.activation(out=kbg[:], in_=bank2[:, 0:128], func=AF.Copy, scale=bgc[:, ci:ci + 1]),
                                      r=[bank2, bgc], w=[kbg])
                                k.act(lambda e: e.activation(out=kd[:], in_=bank2[:, 0:128], func=AF.Copy, scale=ek[:, ci:ci + 1]),
                                      r=[bank2, ek], w=[kd])
                                k.act(lambda e: e.activation(out=vb[:], in_=bank2[:, 128:256], func=AF.Copy, scale=bcol),
                                      r=[bank2, bq], w=[vb])
                                bank = pbr.next()
                                k.pe(lambda e: e.transpose(bank[:, 0:128], Ln[:], IDB()), r=[Ln, cst_b], w=[bank])
                                yield
                                Nk = b128.next()
                                k.act(lambda e: e.activation(out=Nk[:], in_=bank[:, 0:128], func=AF.Copy), r=[bank], w=[Nk])
                                NkT = Ln
                                Pm = b128.next()
                                k.dve(lambda e: e.tensor_tensor(out=Pm[:], in0=Nk[:], in1=cst_b[:, 0, :], op=ALU.add),
                                      r=[Nk, cst_b], w=[Pm])
                                Pt = b128.next()
                                k.pool(lambda e: e.tensor_tensor(out=Pt[:], in0=NkT[:], in1=cst_b[:, 0, :], op=ALU.add),
                                       r=[NkT, cst_b], w=[Pt])
                                for lev in range(1, 7):
                                    psA = pfr.next()
                                    k.pe(lambda e: e.matmul(psA[:, 0:128], lhsT=Nk[:], rhs=NkT[:], start=True, stop=True),
                                         r=[Nk, NkT], w=[psA])
                                    if lev < 6:
                                        psB = pfr.next()
                                        k.pe(lambda e: e.matmul(psB[:, 0:128], lhsT=NkT[:], rhs=Nk[:], start=True, stop=True),
                                             r=[Nk, NkT], w=[psB])
                                    yield
                                    NkT2 = b128.next()
                                    k.act(lambda e: e.activation(out=NkT2[:], in_=psA[:, 0:128], func=AF.Copy), r=[psA], w=[NkT2])
                                    if lev < 6:
                                        Nk2 = b128.next()
                                        k.act(lambda e: e.activation(out=Nk2[:], in_=psB[:, 0:128], func=AF.Copy), r=[psB], w=[Nk2])
                                    else:
                                        Nk2 = None
                                    psC = pfr.next()
                                    k.pe(lambda e: e.matmul(psC[:, 0:128], lhsT=NkT2[:], rhs=Pm[:], start=True, stop=True),
                                         r=[NkT2, Pm], w=[psC])
                                    psD = pfr.next()
                                    k.pe(lambda e: e.matmul(psD[:, 0:128], lhsT=Pm[:], rhs=NkT2[:], start=True, stop=True),
                                         r=[NkT2, Pm], w=[psD])
                                    yield
                                    Pn = b128.next()
                                    k.dve(lambda e: e.tensor_tensor(out=Pn[:], in0=psC[:, 0:128], in1=Pm[:], op=ALU.add),
                                          r=[psC, Pm], w=[Pn])
                                    Ptn = b128.next()
                                    k.dve(lambda e: e.tensor_tensor(out=Ptn[:], in0=psD[:, 0:128], in1=Pt[:], op=ALU.add),
                                          r=[psD, Pt], w=[Ptn])
                                    Pm, Pt, Nk, NkT = Pn, Ptn, Nk2, NkT2
                                psR = pfr.next()
                                k.pe(lambda e: e.matmul(psR[:, 0:128], lhsT=Ln[:], rhs=Pm[:], start=True, stop=True), r=[Ln, Pm], w=[psR])
                                IX = b128.next()
                                k.pool(lambda e: e.tensor_tensor(out=IX[:], in0=cst_b[:, 0, :], in1=Pm[:], op=ALU.subtract),
                                       r=[cst_b, Pm], w=[IX])
                                yield
                                Rr = b128.next()
                                k.dve(lambda e: e.tensor_tensor(out=Rr[:], in0=psR[:, 0:128], in1=IX[:], op=ALU.add),
                                      r=[psR, IX], w=[Rr])
                                psX = pfr.next()
                                k.pe(lambda e: e.matmul(psX[:, 0:128], lhsT=Pt[:], rhs=Rr[:], start=True, stop=True), r=[Pt, Rr], w=[psX])
                                yield
                                TT = b128.next()
                                k.dve(lambda e: e.tensor_tensor(out=TT[:], in0=psX[:, 0:128], in1=Pm[:], op=ALU.add),
                                      r=[psX, Pm], w=[TT])
                                psu = pfr.next()
                                k.pe(lambda e: e.matmul(psu[:, 0:128], lhsT=TT[:], rhs=vb[:], start=True, stop=True), r=[TT, vb], w=[psu])
                                psw = pfr.next()
                                k.pe(lambda e: e.matmul(psw[:, 0:128], lhsT=kbg[:], rhs=TT[:], start=True, stop=True), r=[TT, kbg], w=[psw])
                                yield
                                u = f128.next()
                                k.act(lambda e: e.activation(out=u[:], in_=psu[:, 0:128], func=AF.Copy), r=[psu], w=[u])
                                wT = b128.next()
                                k.act(lambda e: e.activation(out=wT[:], in_=psw[:, 0:128], func=AF.Copy), r=[psw], w=[wT])
                                ps1 = pfr.next()
                                k.pe(lambda e: e.matmul(ps1[:, 0:128], lhsT=wT[:], rhs=Sbf[:], start=True, stop=True), r=[wT, Sbf], w=[ps1])
                                ps2 = pfr.next()
                                k.pe(lambda e: e.matmul(ps2[:, 0:128], lhsT=qT[:, cs_], rhs=Sbf[:], start=True, stop=True), r=[qT, Sbf], w=[ps2])
                                yield
                                vn = b128.next()
                                k.dve(lambda e: e.tensor_tensor(out=vn[:], in0=u[:], in1=ps1[:, 0:128], op=ALU.subtract),
                                      r=[u, ps1], w=[vn])
                                tmp = f128.next()
                                k.act(lambda e: e.activation(out=tmp[:], in_=ps2[:, 0:128], func=AF.Copy, scale=egs[:, ci:ci + 1]),
                                      r=[ps2, egs], w=[tmp])
                                ps3 = pfr.next()
                                k.pe(lambda e: e.matmul(ps3[:, 0:128], lhsT=AT[:], rhs=vn[:], start=True, stop=True), r=[AT, vn], w=[ps3])
                                ps4 = pfr.next()
                                k.pe(lambda e: e.matmul(ps4[:, 0:128], lhsT=kd[:], rhs=vn[:], start=True, stop=True), r=[kd, vn], w=[ps4])
                                yield
                                k.dve(lambda e: e.tensor_tensor(out=od[:, cs_], in0=tmp[:], in1=ps3[:, 0:128], op=ALU.add),
                                      r=[tmp, ps3], w=[od])
                                k.dve(lambda e: e.scalar_tensor_tensor(out=S32[:], in0=S32[:], scalar=dec[:, ci:ci + 1], in1=ps4[:, 0:128],
                                                                       op0=ALU.mult, op1=ALU.add), r=[S32, dec, ps4], w=[S32])
                                k.pool(lambda e: e.tensor_copy(out=Sbf[:], in_=S32[:]), r=[S32], w=[Sbf])
                                yield

                        gens = [unit_gen(0), unit_gen(1)]
                        while gens:
                            for g_ in list(gens):
                                try:
                                    next(g_)
                                except StopIteration:
                                    gens.remove(g_)
                        for c in range(NT):
                            cs_ = slice(c * 128, (c + 1) * 128)
                            k.dve(lambda e: e.tensor_tensor(out=o_d[0][:, cs_], in0=o_d[0][:, cs_], in1=o_d[1][:, cs_], op=ALU.add),
                                   r=[o_d[0], o_d[1]], w=[o_d[0]])
                            rs = rstd_of(P, (o_d[0], o_d[0][:, cs_]), 128, P["junk"])
                            tmp = f128.next()
                            k.dve(lambda e: e.scalar_tensor_tensor(out=tmp[:], in0=o_d[0][:, cs_], scalar=rs[:, 0:1], in1=gon[:],
                                                                   op0=ALU.mult, op1=ALU.mult), r=[o_d[0], rs, gon], w=[tmp])
                            onb = b128.next()
                            k.dve(lambda e: e.tensor_tensor(out=onb[:], in0=tmp[:], in1=z_h[:, c, :], op=ALU.mult),
                                   r=[tmp, z_h], w=[onb])
                            bank = pbr.next()
                            k.pe(lambda e: e.transpose(bank[:, 0:128], onb[:], IDB()), r=[onb, cst_b], w=[bank])
                            k.act(lambda e: e.activation(out=ogT[:, cs_], in_=bank[:, 0:128], func=AF.Copy), r=[bank], w=[ogT])
                        k.dma(omixT_d[:, 8 + h, :], ogT[:], r=[ogT], w=[dt_("ogdn")], own=ogT, accum_w=True)
                except _StopD:
                    pass
            k.barrier()
            if dbg in ("C", "C0", "C1"):
                break
            if cstop != 99:
                break
            with contextlib.ExitStack() as st:
                Dp = Pools(k, st)
                P = {"ss": Dp.ring(4, [128, 1], F32), "rs": Dp.ring(4, [128, 1], F32),
                     "junk": Dp.sb([128, D], BF16), "nbf": Dp.ring(1, [128, D], BF16),
                     "wring": Dp.ring(3, [128, KC, 512], BF16), "hT": Dp.sb([128, FC, TB], BF16),
                     "silu": Dp.ring(2, [128, 512], F32), "gB": Dp.ring(1, [128, D], F32)}
                nT = Dp.sb([128, KC, TB], BF16)
                xr = Dp.ring(2, [128, D], F32)
                ysb = [Dp.sb([128, D], F32) for _ in range(TPB)]
                mT = Dp.sb([128, KC, NMEM], BF16)
                KmT = Dp.sb([128, 4, NMEM], BF16)
                Vm = Dp.sb([128, 2, 512], BF16)
                qmT = Dp.sb([128, 4, TB], BF16)
                omT = Dp.sb([128, 4, TB], BF16)
                Pr = Dp.ring(2, [128, TB], BF16)
                rinv = Dp.ring(1, [128, TB], F32)
                mscale = float(128 ** -0.5)
                TPB_save = TPB

                def load_mem(t):
                    xt = xr.next()
                    k.dma(xt[:], mem_d[sq, t * 128:(t + 1) * 128, :], w=[xt])
                    return xt
                for t in range(NMEM // 128 if dstop > 0 else 0):
                    xt = load_mem(t)
                    rs = rstd_of(P, (xt, xt[:, :]), D, P["junk"])
                    nb = P["nbf"].next()
                    k.act(lambda e: e.activation(out=nb[:], in_=xt[:], func=AF.Copy, scale=rs[:, 0:1]), r=[xt, rs], w=[nb])
                    transpose_to(P, nb, D, mT, t, (gpre, PRE["mem_kv_norm_g"]))
                sk = load_w(P, wb["w_mk"], 0, KC, 0, 512)
                for hh in range(4 if dstop > 0 else 0):
                    acc = pfr.next()
                    for kc in range(KC):
                        k.pe(lambda e, kc=kc: e.matmul(acc[:, 0:NMEM], lhsT=sk[:, kc, hh * 128:(hh + 1) * 128], rhs=mT[:, kc, :],
                                                       start=(kc == 0), stop=(kc == KC - 1)), r=[sk, mT], w=[acc])
                    k.act(lambda e: e.activation(out=KmT[:, hh, :], in_=acc[:, 0:NMEM], func=AF.Copy), r=[acc], w=[KmT])
                sv = load_w(P, wb["w_mv"], 0, KC, 0, 512)
                for t in range(NMEM // 128 if dstop > 0 else 0):
                    acc = pfr.next()
                    for kc in range(KC):
                        k.pe(lambda e, kc=kc: e.matmul(acc[:, :], lhsT=mT[:, kc, t * 128:(t + 1) * 128], rhs=sv[:, kc, :],
                                                       start=(kc == 0), stop=(kc == KC - 1)), r=[sv, mT], w=[acc])
                    k.act(lambda e: e.activation(out=Vm[:, t, :], in_=acc[:, :], func=AF.Copy), r=[acc], w=[Vm])
                om_r = [dt_("omla"), dt_("ogdn")]
                for blk in range(NB if dstop > 1 else 0):
                    if dbg == "D1" and blk == 1:
                        break
                    t0 = blk * TB
                    hrow = dt_("h1%d" % blk)
                    k.dma(nT[:], omixT_d[:, :, t0:t0 + TB], r=om_r, w=[nT])
                    for dg in range(4):
                        so = load_w(P, wb["w_out"], 0, KC, dg * 512, 512)
                        for t in range(TPB):
                            acc = pfr.next()
                            for kc in range(KC):
                                k.pe(lambda e, kc=kc: e.matmul(acc[:, :], lhsT=nT[:, kc, t * 128:(t + 1) * 128], rhs=so[:, kc, :],
                                                               start=(kc == 0), stop=(kc == KC - 1)), r=[so, nT], w=[acc])
                            k.act(lambda e: e.activation(out=ysb[t][:, dg * 512:(dg + 1) * 512], in_=acc[:, :], func=AF.Copy),
                                  r=[acc], w=[ysb[t]])

                    def mk_loader(gname, half):
                        def ld(t):
                            xt = xr.next()
                            k.dma(xt[:], h1_d[t0 + t * 128:t0 + (t + 1) * 128, :], r=[hrow], w=[xt])
                            post_residual(P, ysb[t], gname, xt, xt, half)
                            k.dma(h1_d[t0 + t * 128:t0 + (t + 1) * 128, :], xt[:], r=[xt], w=[hrow], own=xt, accum_w=True)
                            return xt
                        return ld
                    norm_transpose_block(P, mk_loader("mix_post_g", False), "mem_pre_g", nT)
                    if dstop == 2:
                        break
                    sq_ = load_w(P, wb["w_mq"], 0, KC, 0, 512)
                    for hh in range(4):
                        acc = pfr.next()
                        for kc in range(KC):
                            k.pe(lambda e, kc=kc: e.matmul(acc[:, 0:TB], lhsT=sq_[:, kc, hh * 128:(hh + 1) * 128], rhs=nT[:, kc, :],
                                                           start=(kc == 0), stop=(kc == KC - 1)), r=[sq_, nT], w=[acc])
                        k.dve(lambda e: e.tensor_copy(out=qmT[:, hh, :], in_=acc[:, 0:TB]), r=[acc], w=[qmT])
                    for hh in range(4):
                        accO, accR = pf[0], pf[1]
                        for mt in range(2):
                            sp_ = pf[2 + mt]
                            k.pe(lambda e: e.matmul(sp_[:, 0:TB], lhsT=KmT[:, hh, mt * 128:(mt + 1) * 128], rhs=qmT[:, hh, :],
                                                    start=True, stop=True), r=[KmT, qmT], w=[sp_])
                            p_ = Pr.next()
                            k.act(lambda e: e.activation(out=p_[:], in_=sp_[:, 0:TB], func=AF.Exp, scale=mscale), r=[sp_], w=[p_])
                            k.pe(lambda e: e.matmul(accO[:, 0:TB], lhsT=Vm[:, mt, hh * 128:(hh + 1) * 128], rhs=p_[:],
                                                    start=(mt == 0), stop=(mt == 1)), r=[Vm, p_], w=[accO])
                            k.pe(lambda e: e.matmul(accR[:, 0:TB], lhsT=ONESB, rhs=p_[:], start=(mt == 0), stop=(mt == 1)),
                                 r=[cst_b, p_], w=[accR])
                        ri = rinv.next()
                        k.dve(lambda e: e.reciprocal(out=ri[:], in_=accR[:, 0:TB]), r=[accR], w=[ri])
                        k.dve(lambda e: e.tensor_tensor(out=omT[:, hh, :], in0=accO[:, 0:TB], in1=ri[:], op=ALU.mult),
                              r=[accO, ri], w=[omT])
                    pfr.i = 0
                    for dg in range(4):
                        so = load_w(P, wb["w_mo"], 0, 4, dg * 512, 512)
                        for t in range(TPB):
                            acc = pfr.next()
                            for hh in range(4):
                                k.pe(lambda e, hh=hh: e.matmul(acc[:, :], lhsT=omT[:, hh, t * 128:(t + 1) * 128], rhs=so[:, hh, :],
                                                               start=(hh == 0), stop=(hh == 3)), r=[so, omT], w=[acc])
                            k.act(lambda e: e.activation(out=ysb[t][:, dg * 512:(dg + 1) * 512], in_=acc[:, :], func=AF.Copy),
                                  r=[acc], w=[ysb[t]])
                    if dstop == 3:
                        break
                    norm_transpose_block(P, mk_loader("mem_post_g", False), "ffn2_pre_g", nT)
                    ffn(P, nT, wb["ffn2_w_gate"], wb["ffn2_w_up"], wb["ffn2_w_down"], ysb)
                    if dstop == 4:
                        break
                    for t in range(TPB):
                        xt = xr.next()
                        k.dma(xt[:], h1_d[t0 + t * 128:t0 + (t + 1) * 128, :], r=[hrow], w=[xt])
                        post_residual(P, ysb[t], "ffn2_post_g", xt, xt, True)
                        rs = rstd_of(P, (xt, xt[:, :]), D, P["junk"])
                        gB = P["gB"].next()
                        k.dma(gB[:], gbd["final_norm_g"], w=[gB])
                        k.dve(lambda e: e.scalar_tensor_tensor(out=xt[:], in0=xt[:], scalar=rs[:, 0:1], in1=gB[:],
                                                               op0=ALU.mult, op1=ALU.mult), r=[xt, rs, gB], w=[xt])
                        k.dma(y_d[sq, t0 + t * 128:t0 + (t + 1) * 128, :], xt[:], r=[xt], w=[dt_("y")], own=xt, accum_w=True)
            k.barrier()
        k.finish()
        print("BUILD stats: nins=%d nwait=%d nsems=%d" % (k.nins, k.nwait, len(k.sems)), flush=True)
    return nc


def host_layouts(inp, S):
    f = np.float32
    out = {}
    for n in ["ffn1_w_gate", "ffn1_w_up", "ffn1_w_down", "ffn2_w_gate", "ffn2_w_up", "ffn2_w_down",
              "w_in", "w_uq", "w_ukv", "w_out", "w_mq", "w_mk", "w_mv", "w_mo"]:
        out[n] = np.ascontiguousarray(np.asarray(inp[n], f)[0])
    for n in ["ffn1_pre_g", "ffn1_post_g", "mix_pre_g", "mix_post_g", "mem_pre_g", "mem_kv_norm_g",
              "mem_post_g", "ffn2_pre_g", "ffn2_post_g", "final_norm_g"]:
        out[n + "_bc"] = np.ascontiguousarray(np.broadcast_to(np.asarray(inp[n], f)[0][None, :], (128, D)))
    pre = ["ffn1_pre_g", "mix_pre_g", "mem_pre_g", "mem_kv_norm_g", "ffn2_pre_g"]
    out["gpre"] = np.ascontiguousarray(
        np.stack([np.asarray(inp[n], f)[0].reshape(KC, 128).T for n in pre], axis=1))
    qg = np.asarray(inp["mla_q_norm_g"], f)[0].reshape(4, 128).T
    kg = np.asarray(inp["mla_kv_norm_g"], f)[0].reshape(2, 128).T
    out["qkg"] = np.ascontiguousarray(np.concatenate([qg, kg], axis=1))
    i = np.arange(128)
    p, fr = i[:, None], i[None, :]
    out["consts"] = np.ascontiguousarray(np.stack(
        [np.eye(128), np.ones((128, 128)), fr <= p, fr < p, fr >= p, fr > p], axis=1).astype(f))
    NT = S // 128
    inv_freq = (10000.0 ** (-np.arange(0, ROPE, 2, dtype=f) / f(ROPE))).astype(f)
    ang = (np.arange(S, dtype=f)[:, None] * inv_freq[None, :]).astype(f)
    cos = np.cos(ang).astype(f).reshape(NT, 128, 32).transpose(1, 0, 2)
    sin = np.sin(ang).astype(f).reshape(NT, 128, 32).transpose(1, 0, 2)
    out["rope"] = np.ascontiguousarray(np.stack([cos, sin], axis=1))
    al = np.asarray(inp["gdn_a_log"], f)[0].reshape(16)
    dtb = np.asarray(inp["gdn_dt_bias"], f)[0].reshape(16)
    out["gdnp"] = np.ascontiguousarray(np.broadcast_to(np.stack([al, dtb], 0)[None], (128, 2, 16)))
    out["gon"] = np.ascontiguousarray(np.broadcast_to(np.asarray(inp["gdn_out_norm_g"], f)[0][None, :], (128, 128)))
    cw = np.asarray(inp["gdn_conv_w"], f)[0]
    out["cw"] = np.ascontiguousarray(cw.reshape(5, 24, 128).transpose(2, 1, 0))
    return out


S_FULL = 4096
_NC_CACHE = {}


def kernel(**inputs):
    S = S_FULL
    NSEQ = 2
    if "nc" not in _NC_CACHE:
        _NC_CACHE["nc"] = build(S, NSEQ)
    nc = _NC_CACHE["nc"]
    hl = host_layouts(inputs, S)
    xp = np.asarray(inputs["x_prompt"], np.float32)
    xs = np.asarray(inputs["x_sample"], np.float32)
    mp = np.asarray(inputs["mem_prompt"], np.float32)
    ms = np.asarray(inputs["mem_sample"], np.float32)
    in_maps = []
    for c in range(8):
        m = dict(hl)
        m["x"] = np.ascontiguousarray(np.stack([xp[c], xs[c % 2]], axis=0))
        m["mem"] = np.ascontiguousarray(np.stack([mp[c], ms[c % 2]], axis=0))
        in_maps.append(m)
    res = run_bass_kernel_spmd(nc, in_maps, core_ids=list(range(8)))
    yp = np.stack([np.asarray(res.results[c]["y"])[0] for c in range(8)], axis=0).astype(np.float32)
    ys = np.stack([np.asarray(res.results[c]["y"])[1] for c in range(2)], axis=0).astype(np.float32)
    return (yp, ys)
```

```python
import contextlib
import numpy as np
import concourse.bass as bass
import concourse.mybir as mybir
from concourse.bass_utils import run_bass_kernel_spmd

F32 = mybir.dt.float32
BF16 = mybir.dt.bfloat16
AF = mybir.ActivationFunctionType
ALU = mybir.AluOpType

D = 2048
DFF = 5632
NMEM = 256
EPS = 1e-6
QR, KVR, ROPE, NOPE, VD = 512, 256, 64, 128, 128
INW = 4960
KC = D // 128
FC = DFF // 128


class Trk:
    __slots__ = ("w", "r", "dsem")

    def __init__(self):
        self.w = {}
        self.r = {}
        self.dsem = None


class Buf:
    def __init__(self, t, psum=False):
        self.t = t
        self.T = Trk()
        self.psum = psum

    def __getitem__(self, idx):
        return self.t[idx]


class Eng:
    def __init__(self, name, e, semid, compute):
        self.name, self.e, self.semid, self.compute = name, e, semid, compute
        self.cnt = 0
        self.seen = {}


class K:
    def __init__(self, nc, stack):
        self.nc = nc
        self.stack = stack
        self.sems = []
        self.semcnt = []
        self.eng = {}
        for name, e, comp in (("pe", nc.tensor, True), ("act", nc.scalar, True), ("dve", nc.vector, True),
                              ("pool", nc.gpsimd, True), ("sp", nc.sync, False)):
            sid = self.newsem("e_" + name)
            self.eng[name] = Eng(name, e, sid, comp)
        self.free_dsems = []
        self.nwait = 0
        self.nins = 0

    def newsem(self, name):
        s = self.stack.enter_context(self.nc.semaphore(name))
        self.sems.append(s)
        self.semcnt.append(0)
        return len(self.sems) - 1

    def _wait(self, E, sid, val, raw=True):
        if val <= 0:
            return
        if sid == E.semid:
            if E.name == "pe" or not E.compute or not raw:
                return
        if E.seen.get(sid, 0) >= val:
            return
        E.e.wait_ge(self.sems[sid], val)
        E.seen[sid] = val
        self.nwait += 1

    def _deps(self, r, w):
        need = {}
        for x in r:
            for s, v in x.T.w.items():
                if need.get(s, 0) < v:
                    need[s] = v
            if x.psum:
                for s, v in x.T.r.items():
                    if need.get(s, 0) < v:
                        need[s] = v
        for x in w:
            for s, v in x.T.w.items():
                if need.get(s, 0) < v:
                    need[s] = v
            for s, v in x.T.r.items():
                if need.get(s, 0) < v:
                    need[s] = v
        return need

    def op(self, eng, fn, r=(), w=()):
        E = self.eng[eng]
        rawv = 0
        for x in r:
            rawv = max(rawv, x.T.w.get(E.semid, 0))
        for s, v in self._deps(r, w).items():
            if s == E.semid:
                self._wait(E, s, v, raw=True)
            else:
                self._wait(E, s, v)
        ins = fn(E.e)
        ins.then_inc(self.sems[E.semid], 1)
        E.cnt += 1
        self.semcnt[E.semid] = E.cnt
        self.nins += 1
        for x in r:
            if x.T.r.get(E.semid, 0) < E.cnt:
                x.T.r[E.semid] = E.cnt
        for x in w:
            x.T.w = {E.semid: E.cnt}
            x.T.r = {}

    def pe(self, fn, r=(), w=()):
        self.op("pe", fn, r, w)

    def act(self, fn, r=(), w=()):
        self.op("act", fn, r, w)

    def dve(self, fn, r=(), w=()):
        self.op("dve", fn, r, w)

    def pool(self, fn, r=(), w=()):
        self.op("pool", fn, r, w)

    def dma(self, out, in_, r=(), w=(), own=None, q="sp", accum_w=False):
        E = self.eng[q]
        if own is None:
            own = w[0] if w else r[0]
        T = own.T
        if T.dsem is None:
            T.dsem = self.newsem("d%d" % len(self.sems))
        sid = T.dsem
        need = self._deps(r, [] if accum_w else w)
        if accum_w:
            for x in w:
                for s, v in x.T.r.items():
                    if need.get(s, 0) < v:
                        need[s] = v
        if not accum_w and need.get(sid, 0) < self.semcnt[sid]:
            need[sid] = self.semcnt[sid]
        for s, v in need.items():
            self._wait(E, s, v)
        ins = E.e.dma_start(out=out, in_=in_)
        ins.then_inc(self.sems[sid], 16)
        self.semcnt[sid] += 16
        val = self.semcnt[sid]
        self.nins += 1
        for x in r:
            if x.T.r.get(sid, 0) < val:
                x.T.r[sid] = val
        for x in w:
            if accum_w:
                x.T.w[sid] = val
            else:
                x.T.w = {sid: val}
                x.T.r = {}

    def barrier(self):
        for E in self.eng.values():
            for sid in range(len(self.sems)):
                if sid != E.semid:
                    self._wait(E, sid, self.semcnt[sid])

    def finish(self):
        E = self.eng["sp"]
        for sid in range(len(self.sems)):
            self._wait(E, sid, self.semcnt[sid])


class Pools:
    _uid = [0]

    def __init__(self, k, stack):
        self.k, self.stack, self.n = k, stack, 0

    def sb(self, shape, dt, name=None):
        Pools._uid[0] += 1
        nm = "%s_%d" % (name or "sb", Pools._uid[0])
        return Buf(self.stack.enter_context(self.k.nc.sbuf_tensor(nm, list(shape), dt)))

    def ps(self, shape, dt, name=None):
        Pools._uid[0] += 1
        nm = "%s_%d" % (name or "ps", Pools._uid[0])
        return Buf(self.stack.enter_context(self.k.nc.psum_tensor(nm, list(shape), dt)), psum=True)

    def ring(self, n, shape, dt, name=None):
        return Ring([self.sb(shape, dt, name) for _ in range(n)])


class Ring:
    def __init__(self, bufs):
        self.bufs, self.i = bufs, 0

    def next(self):
        b = self.bufs[self.i % len(self.bufs)]
        self.i += 1
        return b


class _StopD(Exception):
    pass


def build(S, NSEQ, dbg=False):
    nc = bass.Bass("TRN2", target_bir_lowering=False)
    flags = set(str(dbg).split("+")) if dbg else set()
    dstop = 99
    cstop = 99
    for f_ in flags:
        if f_.startswith("ds"):
            dstop = int(f_[2:])
        if f_.startswith("cs"):
            cstop = int(f_[2:])
    if "noB" in flags or "noC" in flags or dstop < 99 or cstop != 99:
        dbg = "X"
    NT = S // 128
    TB = min(512, S)
    NB = S // TB
    TPB = TB // 128
    QB = TB
    NQB = S // QB

    def din(name, shape, dt=F32):
        return nc.dram_tensor(name, list(shape), dt, kind="ExternalInput").ap()

    def dscr(name, shape, dt):
        return nc.dram_tensor(name, list(shape), dt, kind=("ExternalOutput" if dbg else "Internal")).ap()

    x_d = din("x", [NSEQ, S, D])
    mem_d = din("mem", [NSEQ, NMEM, D])
    y_d = nc.dram_tensor("y", [NSEQ, S, D], F32, kind="ExternalOutput").ap()
    wnames = {"ffn1_w_gate": (D, DFF), "ffn1_w_up": (D, DFF), "ffn1_w_down": (DFF, D),
              "ffn2_w_gate": (D, DFF), "ffn2_w_up": (D, DFF), "ffn2_w_down": (DFF, D),
              "w_in": (D, INW), "w_uq": (QR, 8 * 192), "w_ukv": (KVR, 8 * 256), "w_out": (D, D),
              "w_mq": (D, 512), "w_mk": (D, 512), "w_mv": (D, 512), "w_mo": (512, D)}
    wf = {n: din(n, s) for n, s in wnames.items()}
    wb = {n: nc.dram_tensor(n + "_b", list(s), BF16, kind="Internal").ap() for n, s in wnames.items()}
    gnames = ["ffn1_pre_g", "ffn1_post_g", "mix_pre_g", "mix_post_g", "mem_pre_g", "mem_kv_norm_g",
              "mem_post_g", "ffn2_pre_g", "ffn2_post_g", "final_norm_g"]
    gbd = {n: din(n + "_bc", [128, D]) for n in gnames}
    gpd = din("gpre", [128, 5, KC])
    qkg_d = din("qkg", [128, 6])
    consts_d = din("consts", [128, 6, 128])
    rope_d = din("rope", [128, 2, NT, 32])
    gdnp_d = din("gdnp", [128, 2, 16])
    gon_d = din("gon", [128, 128])
    cw_d = din("cw", [128, 24, 5])

    h1_d = dscr("h1_s", [S, D], F32)
    cqnT_d = dscr("cqnT_s", [128, 4, S], BF16)
    ckvnT_d = dscr("ckvnT_s", [128, 2, S], BF16)
    krT_d = dscr("krT_s", [64, S], BF16)
    qkvT_d = dscr("qkvT_s", [128, 24, S], F32)
    zs_d = dscr("zs_s", [S, 1024], F32)
    gb_d = dscr("gb_s", [128, NT, 32], F32)
    omixT_d = dscr("omixT_s", [128, 16, S], BF16)

    with contextlib.ExitStack() as gstack:
        k = K(nc, gstack)
        GP = Pools(k, gstack)
        pf = [GP.ps([128, 512], F32, "pf") for _ in range(6)]
        pb = [GP.ps([128, 1024], BF16, "pb") for _ in range(2)]
        pfr = Ring(pf)
        pbr = Ring(pb)
        class DT:
            pass
        dtrk = {}

        def dt_(name):
            if name not in dtrk:
                dtrk[name] = Buf(None)
            return dtrk[name]

        cst_f = GP.sb([128, 6, 128], F32, "cstf")
        cst_b = GP.sb([128, 6, 128], BF16, "cstb")
        k.dma(cst_f[:], consts_d, w=[cst_f])
        k.dve(lambda e: e.tensor_copy(out=cst_b[:], in_=cst_f[:]), r=[cst_f], w=[cst_b])
        IDB = lambda n=128: cst_b[0:n, 0, 0:n]
        ONESB = cst_b[:, 1, :]
        ONESF = cst_f[:, 1, :]
        LOW, SLOW, UP, SUP = (cst_f[:, i, :] for i in (2, 3, 4, 5))
        gpre = GP.sb([128, 5, KC], F32, "gpre")
        k.dma(gpre[:], gpd, w=[gpre])
        qkg = GP.sb([128, 6], F32, "qkg")
        k.dma(qkg[:], qkg_d, w=[qkg])
        PRE = {"ffn1_pre_g": 0, "mix_pre_g": 1, "mem_pre_g": 2, "mem_kv_norm_g": 3, "ffn2_pre_g": 4}

        wtrks = {n: Buf(None) for n in wnames}
        wkey = {id(wb[n]): n for n in wnames}
        corder = ["ffn1_w_gate", "ffn1_w_up", "ffn1_w_down", "w_in", "w_uq", "w_ukv", "w_out", "w_mq", "w_mk", "w_mv", "w_mo",
                  "ffn2_w_gate", "ffn2_w_up", "ffn2_w_down"]
        for n in corder:
            rows, cols = wnames[n]
            step = 256
            for r0 in range(0, rows, step):
                r1 = min(rows, r0 + step)
                k.dma(wb[n][r0:r1, :], wf[n][r0:r1, :], w=[wtrks[n]], own=wtrks[n], q="pool", accum_w=True)

        def rstd_of(P, src, W, junk, extra=1.0):
            sbuf, sap = src
            ss = P["ss"].next()
            k.pool(lambda e: e.memset(ss[:], 0.0), w=[ss])
            k.act(lambda e: e.activation(out=junk[:, 0:W], in_=sap, func=AF.Square, accum_out=ss[:, 0:1]),
                  r=[sbuf, ss], w=[junk, ss])
            rs = P["rs"].next()
            ex2 = float(extra) ** 2
            k.act(lambda e: e.activation(out=rs[:], in_=ss[:], func=AF.Sqrt, scale=1.0 / (W * ex2), bias=EPS / ex2),
                  r=[ss], w=[rs])
            k.dve(lambda e: e.reciprocal(out=rs[:], in_=rs[:]), r=[rs], w=[rs])
            return rs

        def transpose_to(P, nbf, ncols, dstT, t, gidx, alt=[0]):
            nch = ncols // 128
            for c0 in range(0, nch, 8):
                c1 = min(nch, c0 + 8)
                bank = pbr.next()
                for c in range(c0, c1):
                    k.pe(lambda e, c=c: e.transpose(bank[:, (c - c0) * 128:(c - c0 + 1) * 128],
                                                    nbf[:, c * 128:(c + 1) * 128], IDB()),
                         r=[nbf, cst_b], w=[bank])
                if gidx is None:
                    nn = c1 - c0
                    k.dve(lambda e: e.tensor_copy(out=dstT[:, c0:c1, t * 128:(t + 1) * 128],
                                                  in_=bank[:, 0:nn * 128].rearrange("p (c q) -> p c q", q=128)),
                          r=[bank], w=[dstT])
                    continue
                for c in range(c0, c1):
                    src = bank[:, (c - c0) * 128:(c - c0 + 1) * 128]
                    dst = dstT[:, c, t * 128:(t + 1) * 128]
                    if gidx is None:
                        fn = lambda e, src=src, dst=dst: e.tensor_copy(out=dst, in_=src)
                        rr = [bank]
                    else:
                        gbuf, g0 = gidx
                        gap = gbuf[:, g0 + c:g0 + c + 1] if len(gbuf.t.shape) == 2 else gbuf[:, g0, c:c + 1]
                        rr = [bank, gbuf]
                        if alt[0] % 2 == 0:
                            fn = lambda e, src=src, dst=dst, gap=gap: e.tensor_scalar(
                                out=dst, in0=src, scalar1=gap, scalar2=None, op0=ALU.mult)
                        else:
                            fn = lambda e, src=src, dst=dst, gap=gap: e.activation(
                                out=dst, in_=src, func=AF.Copy, scale=gap)
                    if gidx is not None and alt[0] % 2 == 1:
                        k.act(fn, r=rr, w=[dstT])
                    else:
                        k.dve(fn, r=rr, w=[dstT])
                alt[0] += 1

        def load_w(P, wd, r0, nkc, c0, ncols, coff=0, slot=None):
            if slot is None:
                slot = P["wring"].next()
            src = wd[r0 * 128:(r0 + nkc) * 128, c0:c0 + ncols].rearrange("(c p) f -> p c f", p=128)
            k.dma(slot[:, 0:nkc, coff:coff + ncols], src, r=[wtrks[wkey[id(wd)]]], w=[slot], own=slot, accum_w=(coff != 0))
            return slot

        def load_gain(P, gname, slot):
            gt = P["gBs"][slot]
            if P["gcur"].get(slot) != gname:
                k.dma(gt[:], gbd[gname], w=[gt])
                P["gcur"][slot] = gname
            return gt

        def norm_transpose_block(P, load_tile, gain_name, nT, keep=None):
            gt = load_gain(P, gain_name, 1)
            for t in range(TPB):
                xt = load_tile(t)
                rs = rstd_of(P, (xt, xt[:, :]), D, P["junk"])
                nb = P["nbf"].next()
                k.dve(lambda e: e.scalar_tensor_tensor(out=nb[:], in0=xt[:], scalar=rs[:, 0:1], in1=gt[:],
                                                       op0=ALU.mult, op1=ALU.mult), r=[xt, rs, gt], w=[nb])
                transpose_to(P, nb, D, nT, t, None)

        def ffn(P, nT, wg, wu, wd, ysb):
            hT = P["hT"]
            for g in range(FC // 2):
                sgu = load_w(P, wg, 0, KC, g * 256, 256)
                load_w(P, wu, 0, KC, g * 256, 256, coff=256, slot=sgu)
                for f in range(2):
                    pg, pu = pfr.next(), pfr.next()
                    for kc in range(KC):
                        k.pe(lambda e, kc=kc: e.matmul(pg[:, 0:TB], lhsT=sgu[:, kc, f * 128:(f + 1) * 128],
                                                       rhs=nT[:, kc, :], start=(kc == 0), stop=(kc == KC - 1)),
                             r=[sgu, nT], w=[pg])
                    for kc in range(KC):
                        k.pe(lambda e, kc=kc: e.matmul(pu[:, 0:TB], lhsT=sgu[:, kc, 256 + f * 128:256 + (f + 1) * 128],
                                                       rhs=nT[:, kc, :], start=(kc == 0), stop=(kc == KC - 1)),
                             r=[sgu, nT], w=[pu])
                    sl = P["silu"].next()
                    k.act(lambda e: e.activation(out=sl[:, 0:TB], in_=pg[:, 0:TB], func=AF.Silu), r=[pg], w=[sl])
                    fc = g * 2 + f
                    k.dve(lambda e: e.tensor_tensor(out=hT[:, fc, :], in0=sl[:, 0:TB], in1=pu[:, 0:TB], op=ALU.mult),
                          r=[sl, pu], w=[hT])
            for dg in range(4):
                accs = [pf[i] for i in range(TPB)]
                for fg in range(4):
                    sd = load_w(P, wd, fg * 11, 11, dg * 512, 512)
                    for t in range(TPB):
                        for f in range(11):
                            fc = fg * 11 + f
                            k.pe(lambda e, t=t, f=f, fc=fc: e.matmul(
                                accs[t][:, :], lhsT=hT[:, fc, t * 128:(t + 1) * 128], rhs=sd[:, f, :],
                                start=(fc == 0), stop=(fc == FC - 1)), r=[hT, sd], w=[accs[t]])
                for t in range(TPB):
                    k.act(lambda e, t=t: e.activation(out=ysb[t][:, dg * 512:(dg + 1) * 512], in_=accs[t][:, :],
                                                      func=AF.Copy), r=[accs[t]], w=[ysb[t]])
            pfr.i = 0

        def post_residual(P, ysrc, gname, base, out, half):
            gB = load_gain(P, gname, 0)
            rs = rstd_of(P, (ysrc, ysrc[:, :]), D, P["junk"], extra=(0.5 if half else 1.0))
            k.dve(lambda e: e.scalar_tensor_tensor(out=ysrc[:], in0=ysrc[:], scalar=rs[:, 0:1], in1=gB[:],
                                                   op0=ALU.mult, op1=ALU.mult), r=[ysrc, rs, gB], w=[ysrc])
            k.pool(lambda e: e.tensor_tensor(out=out[:], in0=base[:], in1=ysrc[:], op=ALU.add),
                   r=[base, ysrc], w=[out])

        for sq in range(NSEQ):
            with contextlib.ExitStack() as st:
                A = Pools(k, st)
                P = {"ss": A.ring(4, [128, 1], F32), "rs": A.ring(4, [128, 1], F32),
                     "junk": A.sb([128, D], BF16), "nbf": A.ring(1, [128, D], BF16),
                     "wring": A.ring(3, [128, KC, 512], BF16), "hT": A.sb([128, FC, TB], BF16),
                     "silu": A.ring(2, [128, 512], F32), "gBs": [A.sb([128, D], F32) for _ in range(2)], "gcur": {}}
                nT = A.sb([128, KC, TB], BF16)
                xr = A.ring(2, [128, D], F32)
                ysb = [A.sb([128, D], F32) for _ in range(TPB)]
                cs = A.sb([128, 2, TPB, 32], F32)
                gdnp = A.sb([128, 2, 16], F32)
                k.dma(gdnp[:], gdnp_d, w=[gdnp])
                negA = A.sb([128, 16], F32)
                k.act(lambda e: e.activation(out=negA[:], in_=gdnp[:, 0, :], func=AF.Exp), r=[gdnp], w=[negA])
                k.dve(lambda e: e.tensor_scalar(out=negA[:], in0=negA[:], scalar1=-1.0, scalar2=None, op0=ALU.mult),
                      r=[negA], w=[negA])
                cqT = A.sb([128, 4, TB], BF16)
                ckT = A.sb([128, 2, TB], BF16)
                krT = A.sb([64, TB], BF16)
                gbs = A.sb([128, TPB, 32], F32)
                stg = A.ring(1, [128, 2, TB], F32)
                small = A.ring(1, [128, 512], F32)
                smallb = A.ring(2, [128, 512], BF16)
                tiny = A.ring(8, [128, 64], F32)
                for blk in range(NB):
                    t0 = blk * TB
                    k.dma(cs[:], rope_d[:, :, blk * TPB:(blk + 1) * TPB, :], w=[cs])

                    def load_x(t):
                        xt = xr.next()
                        k.dma(xt[:], x_d[sq, t0 + t * 128:t0 + (t + 1) * 128, :], w=[xt])
                        return xt
                    norm_transpose_block(P, load_x, "ffn1_pre_g", nT)
                    ffn(P, nT, wb["ffn1_w_gate"], wb["ffn1_w_up"], wb["ffn1_w_down"], ysb)
                    h1t = {}

                    def load_h1(t):
                        xt = load_x(t)
                        post_residual(P, ysb[t], "ffn1_post_g", xt, xt, True)
                        k.dma(h1_d[t0 + t * 128:t0 + (t + 1) * 128, :], xt[:], r=[xt], w=[dt_("h1%d" % (blk))],
                              own=xt, accum_w=True)
                        return xt
                    norm_transpose_block(P, load_h1, "mix_pre_g", nT)
                    s0 = load_w(P, wb["w_in"], 0, KC, 0, 512)
                    for t in range(TPB):
                        acc = pfr.next()
                        for kc in range(KC):
                            k.pe(lambda e, kc=kc: e.matmul(acc[:, :], lhsT=nT[:, kc, t * 128:(t + 1) * 128],
                                                           rhs=s0[:, kc, :], start=(kc == 0), stop=(kc == KC - 1)),
                                 r=[nT, s0], w=[acc])
                        cq = small.next()
                        k.act(lambda e: e.activation(out=cq[:], in_=acc[:, :], func=AF.Copy), r=[acc], w=[cq])
                        rs = rstd_of(P, (cq, cq[:, :]), 512, P["junk"])
                        cqn = smallb.next()
                        k.act(lambda e: e.activation(out=cqn[:], in_=cq[:], func=AF.Copy, scale=rs[:, 0:1]),
                              r=[cq, rs], w=[cqn])
                        transpose_to(P, cqn, 512, cqT, t, (qkg, 0))
                    k.dma(cqnT_d[:, :, t0:t0 + TB], cqT[:], r=[cqT], w=[dt_("cq%d" % blk)], own=cqT)
                    s1 = load_w(P, wb["w_in"], 0, KC, 512, 320)
                    load_w(P, wb["w_in"], 0, KC, 4928, 32, coff=320, slot=s1)
                    for t in range(TPB):
                        acc = pfr.next()
                        for kc in range(KC):
                            k.pe(lambda e, kc=kc: e.matmul(acc[:, 0:352], lhsT=nT[:, kc, t * 128:(t + 1) * 128],
                                                           rhs=s1[:, kc, 0:352], start=(kc == 0), stop=(kc == KC - 1)),
                                 r=[nT, s1], w=[acc])
                        ck = small.next()
                        k.act(lambda e: e.activation(out=ck[:, 0:352], in_=acc[:, 0:352], func=AF.Copy),
                              r=[acc], w=[ck])
                        rs = rstd_of(P, (ck, ck[:, 0:256]), 256, P["junk"])
                        ckn = smallb.next()
                        k.act(lambda e: e.activation(out=ckn[:, 0:256], in_=ck[:, 0:256], func=AF.Copy,
                                                     scale=rs[:, 0:1]), r=[ck, rs], w=[ckn])
                        transpose_to(P, ckn, 256, ckT, t, (qkg, 4))
                        cos, sin = cs[:, 0, t, :], cs[:, 1, t, :]
                        ta, tb_ = tiny.next(), tiny.next()
                        x1, x2 = ck[:, 256:288], ck[:, 288:320]
                        k.dve(lambda e: e.tensor_tensor(out=ta[:, 0:32], in0=x1, in1=cos, op=ALU.mult), r=[ck, cs], w=[ta])
                        k.dve(lambda e: e.tensor_tensor(out=ta[:, 32:64], in0=x2, in1=cos, op=ALU.mult), r=[ck, cs], w=[ta])
                        k.pool(lambda e: e.tensor_tensor(out=tb_[:, 0:32], in0=x2, in1=sin, op=ALU.mult), r=[ck, cs], w=[tb_])
                        k.pool(lambda e: e.tensor_tensor(out=tb_[:, 32:64], in0=x1, in1=sin, op=ALU.mult), r=[ck, cs], w=[tb_])
                        krb = smallb.next()
                        k.dve(lambda e: e.tensor_tensor(out=krb[:, 0:32], in0=ta[:, 0:32], in1=tb_[:, 0:32],
                                                        op=ALU.subtract), r=[ta, tb_], w=[krb])
                        k.dve(lambda e: e.tensor_tensor(out=krb[:, 32:64], in0=ta[:, 32:64], in1=tb_[:, 32:64],
                                                        op=ALU.add), r=[ta, tb_], w=[krb])
                        bank = pbr.next()
                        k.pe(lambda e: e.transpose(bank[0:64, 0:128], krb[:, 0:64], IDB()), r=[krb, cst_b], w=[bank])
                        k.dve(lambda e: e.tensor_copy(out=krT[:, t * 128:(t + 1) * 128], in_=bank[0:64, 0:128]),
                              r=[bank], w=[krT])
                        a_, b_ = ck[:, 320:336], ck[:, 336:352]
                        u0, u1, u2 = tiny.next(), tiny.next(), tiny.next()
                        k.dve(lambda e: e.tensor_tensor(out=u0[:, 0:16], in0=a_, in1=gdnp[:, 1, :], op=ALU.add),
                              r=[ck, gdnp], w=[u0])
                        k.dve(lambda e: e.tensor_scalar(out=u1[:, 0:16], in0=u0[:, 0:16], scalar1=-1.0, scalar2=None,
                                                        op0=ALU.mult), r=[u0], w=[u1])
                        k.dve(lambda e: e.tensor_tensor(out=u1[:, 0:16], in0=u0[:, 0:16], in1=u1[:, 0:16], op=ALU.min),
                              r=[u0, u1], w=[u1])
                        k.act(lambda e: e.activation(out=u1[:, 0:16], in_=u1[:, 0:16], func=AF.Exp),
                              r=[u1], w=[u1])
                        k.act(lambda e: e.activation(out=u1[:, 0:16], in_=u1[:, 0:16], func=AF.Ln, bias=1.0),
                              r=[u1], w=[u1])
                        k.dve(lambda e: e.scalar_tensor_tensor(out=u2[:, 0:16], in0=u0[:, 0:16], scalar=0.0,
                                                               in1=u1[:, 0:16], op0=ALU.max, op1=ALU.add),
                              r=[u0, u1], w=[u2])
                        k.dve(lambda e: e.tensor_tensor(out=gbs[:, t, 0:16], in0=u2[:, 0:16], in1=negA[:], op=ALU.mult),
                              r=[u2, negA], w=[gbs])
                        k.act(lambda e: e.activation(out=gbs[:, t, 16:32], in_=b_, func=AF.Sigmoid), r=[ck], w=[gbs])
                    k.dma(ckvnT_d[:, :, t0:t0 + TB], ckT[:], r=[ckT], w=[dt_("ck%d" % blk)], own=ckT)
                    k.dma(krT_d[:, t0:t0 + TB], krT[:], r=[krT], w=[dt_("kr%d" % blk)], own=krT)
                    k.dma(gb_d[:, blk * TPB:(blk + 1) * TPB, :], gbs[:], r=[gbs], w=[dt_("gb%d" % blk)], own=gbs)
                    for g in range(6):
                        sw = load_w(P, wb["w_in"], 0, KC, 832 + g * 512, 512)
                        for f2 in range(2):
                            sg = stg.next()
                            for ff in range(2):
                                f = f2 * 2 + ff
                                acc = pfr.next()
                                for kc in range(KC):
                                    k.pe(lambda e, kc=kc: e.matmul(acc[:, 0:TB], lhsT=sw[:, kc, f * 128:(f + 1) * 128],
                                                                   rhs=nT[:, kc, :], start=(kc == 0), stop=(kc == KC - 1)),
                                         r=[sw, nT], w=[acc])
                                if ff == 0:
                                    k.act(lambda e: e.activation(out=sg[:, ff, :], in_=acc[:, 0:TB], func=AF.Copy),
                                          r=[acc], w=[sg])
                                else:
                                    k.dve(lambda e: e.tensor_copy(out=sg[:, ff, :], in_=acc[:, 0:TB]), r=[acc], w=[sg])
                            k.dma(qkvT_d[:, g * 4 + f2 * 2:g * 4 + f2 * 2 + 2, t0:t0 + TB], sg[:, 0:2, :], r=[sg],
                                  w=[dt_("qkv%d" % blk)], own=sg, accum_w=True)
                    sz = [load_w(P, wb["w_in"], 0, KC, 3904 + g * 512, 512) for g in range(2)]
                    for t in range(TPB):
                        zt = stg.next()
                        for g in range(2):
                            acc = pfr.next()
                            for kc in range(KC):
                                k.pe(lambda e, kc=kc: e.matmul(acc[:, :], lhsT=nT[:, kc, t * 128:(t + 1) * 128],
                                                               rhs=sz[g][:, kc, :], start=(kc == 0), stop=(kc == KC - 1)),
                                     r=[nT, sz[g]], w=[acc])
                            k.act(lambda e: e.activation(out=zt[:, g, :], in_=acc[:, :], func=AF.Silu),
                                  r=[acc], w=[zt])
                        k.dma(zs_d[t0 + t * 128:t0 + (t + 1) * 128, :].rearrange("s (g c) -> s g c", g=2), zt[:, 0:2, :],
                              r=[zt], w=[dt_("zs%d" % blk)],
                              own=zt, accum_w=True)
            k.barrier()
            if dbg == "A":
                break
            with contextlib.ExitStack() as st:
                B = Pools(k, st)
                cqT = B.sb([128, 4, S], BF16)
                ckT = B.sb([128, 2, S], BF16)
                krT = B.sb([64, S], BF16)
                rA = [dt_("cq%d" % b) for b in range(NB)] + [dt_("ck%d" % b) for b in range(NB)] + \
                     [dt_("kr%d" % b) for b in range(NB)]
                k.dma(cqT[:], cqnT_d, r=rA, w=[cqT])
                k.dma(ckT[:], ckvnT_d, r=rA, w=[ckT])
                k.dma(krT[:], krT_d, r=rA, w=[krT])
                wuq = B.sb([128, 4, 8 * 192], BF16)
                wukv = B.sb([128, 2, 8 * 256], BF16)
                k.dma(wuq[:], wb["w_uq"].rearrange("(c p) f -> p c f", p=128), r=[wtrks["w_uq"]], w=[wuq])
                k.dma(wukv[:], wb["w_ukv"].rearrange("(c p) f -> p c f", p=128), r=[wtrks["w_ukv"]], w=[wukv])
                cs = B.sb([128, 2, NT, 32], F32)
                k.dma(cs[:], rope_d, w=[cs])
                KT = B.sb([128, S], BF16)
                QnT = B.sb([128, S], BF16)
                QrT = B.sb([64, S], BF16)
                Vh = B.sb([128, NT * 128], BF16)
                OT = B.sb([128, S], BF16)
                Pr = B.ring(3, [128, QB], BF16)
                rinv = B.ring(2, [128, QB], F32)
                qra = B.ring(2, [128, 8, 64], F32)
                qrb = B.ring(2, [128, 8, 64], F32)
                qrbf = B.ring(2, [128, 8, 64], BF16)
                scale = float((NOPE + ROPE) ** -0.5)
                G8 = min(8, NT)
                for h in range(8):
                    if dbg in ("B0", "C", "C0", "C1", "D0", "D1", "CD") or "noB" in flags:
                        break
                    for blk in range(NQB):
                        sl = slice(blk * QB, (blk + 1) * QB)
                        acc = pfr.next()
                        for c in range(2):
                            k.pe(lambda e, c=c: e.matmul(acc[:, 0:QB], lhsT=wukv[:, c, h * 256:h * 256 + 128],
                                                         rhs=ckT[:, c, sl], start=(c == 0), stop=(c == 1)),
                                 r=[wukv, ckT], w=[acc])
                        k.act(lambda e: e.activation(out=KT[:, sl], in_=acc[:, 0:QB], func=AF.Copy), r=[acc], w=[KT])
                        acc2 = pfr.next()
                        for c in range(4):
                            k.pe(lambda e, c=c: e.matmul(acc2[:, 0:QB], lhsT=wuq[:, c, h * 192:h * 192 + 128],
                                                         rhs=cqT[:, c, sl], start=(c == 0), stop=(c == 3)),
                                 r=[wuq, cqT], w=[acc2])
                        k.dve(lambda e: e.tensor_copy(out=QnT[:, sl], in_=acc2[:, 0:QB]), r=[acc2], w=[QnT])
                    for tg in range(0, NT, 4):
                        acc = pfr.next()
                        for t in range(tg, min(NT, tg + 4)):
                            for c in range(2):
                                k.pe(lambda e, c=c, t=t: e.matmul(
                                    acc[:, (t - tg) * 128:(t - tg + 1) * 128], lhsT=ckT[:, c, t * 128:(t + 1) * 128],
                                    rhs=wukv[:, c, h * 256 + 128:(h + 1) * 256], start=(c == 0), stop=(c == 1)),
                                    r=[wukv, ckT], w=[acc])
                        n4 = min(NT, tg + 4) - tg
                        k.act(lambda e: e.activation(out=Vh[:, tg * 128:(tg + n4) * 128], in_=acc[:, 0:n4 * 128],
                                                     func=AF.Copy), r=[acc], w=[Vh])
                    if dbg == "B1":
                        break
                    for tg in range(0, NT, G8):
                        acc = pfr.next()
                        for t in range(tg, tg + G8):
                            for c in range(4):
                                k.pe(lambda e, c=c, t=t: e.matmul(
                                    acc[:, (t - tg) * 64:(t - tg + 1) * 64], lhsT=cqT[:, c, t * 128:(t + 1) * 128],
                                    rhs=wuq[:, c, h * 192 + 128:(h + 1) * 192], start=(c == 0), stop=(c == 3)),
                                    r=[wuq, cqT], w=[acc])
                        av = acc[:, 0:G8 * 64].rearrange("p (t r) -> p t r", r=64)
                        cos, sin = cs[:, 0, tg:tg + G8, :], cs[:, 1, tg:tg + G8, :]
                        ta, tb_, qb_ = qra.next(), qrb.next(), qrbf.next()
                        k.dve(lambda e: e.tensor_tensor(out=ta[:, 0:G8, 0:32], in0=av[:, :, 0:32], in1=cos, op=ALU.mult),
                              r=[acc, cs], w=[ta])
                        k.dve(lambda e: e.tensor_tensor(out=ta[:, 0:G8, 32:64], in0=av[:, :, 32:64], in1=cos, op=ALU.mult),
                              r=[acc, cs], w=[ta])
                        k.dve(lambda e: e.tensor_tensor(out=tb_[:, 0:G8, 0:32], in0=av[:, :, 32:64], in1=sin, op=ALU.mult),
                              r=[acc, cs], w=[tb_])
                        k.dve(lambda e: e.tensor_tensor(out=tb_[:, 0:G8, 32:64], in0=av[:, :, 0:32], in1=sin, op=ALU.mult),
                              r=[acc, cs], w=[tb_])
                        k.pool(lambda e: e.tensor_tensor(out=qb_[:, 0:G8, 0:32], in0=ta[:, 0:G8, 0:32],
                                                         in1=tb_[:, 0:G8, 0:32], op=ALU.subtract), r=[ta, tb_], w=[qb_])
                        k.pool(lambda e: e.tensor_tensor(out=qb_[:, 0:G8, 32:64], in0=ta[:, 0:G8, 32:64],
                                                         in1=tb_[:, 0:G8, 32:64], op=ALU.add), r=[ta, tb_], w=[qb_])
                        bank = pbr.next()
                        for t in range(G8):
                            k.pe(lambda e, t=t: e.transpose(bank[0:64, t * 128:(t + 1) * 128], qb_[:, t, :], IDB()),
                                 r=[qb_, cst_b], w=[bank])
                        k.act(lambda e: e.activation(out=QrT[:, tg * 128:(tg + G8) * 128], in_=bank[0:64, 0:G8 * 128],
                                                     func=AF.Copy), r=[bank], w=[QrT])
                    if dbg == "B2":
                        break
                    for qb in range(NQB):
                        qs = slice(qb * QB, (qb + 1) * QB)
                        accO, accR = pf[(qb % 2) * 2], pf[(qb % 2) * 2 + 1]
                        sps = [pf[4], pf[5]]

                        def qk(kt):
                            sp_ = sps[kt % 2]
                            k.pe(lambda e: e.matmul(sp_[:, 0:QB], lhsT=KT[:, kt * 128:(kt + 1) * 128], rhs=QnT[:, qs],
                                                    start=True, stop=False), r=[KT, QnT], w=[sp_])
                            k.pe(lambda e: e.matmul(sp_[:, 0:QB], lhsT=krT[:, kt * 128:(kt + 1) * 128], rhs=QrT[:, qs],
                                                    start=False, stop=True), r=[krT, QrT], w=[sp_])
                        qk(0)
                        for kt in range(NT):
                            if kt + 1 < NT:
                                qk(kt + 1)
                            sp_ = sps[kt % 2]
                            p_ = Pr.next()
                            k.act(lambda e: e.activation(out=p_[:], in_=sp_[:, 0:QB], func=AF.Exp, scale=scale),
                                  r=[sp_], w=[p_])
                            k.pe(lambda e: e.matmul(accO[:, 0:QB], lhsT=Vh[:, kt * 128:(kt + 1) * 128], rhs=p_[:], start=(kt == 0),
                                                    stop=(kt == NT - 1)), r=[Vh, p_], w=[accO])
                            k.pe(lambda e: e.matmul(accR[:, 0:QB], lhsT=ONESB, rhs=p_[:], start=(kt == 0),
                                                    stop=(kt == NT - 1)), r=[cst_b, p_], w=[accR])
                        ri = rinv.next()
                        k.dve(lambda e: e.reciprocal(out=ri[:], in_=accR[:, 0:QB]), r=[accR], w=[ri])
                        k.dve(lambda e: e.tensor_tensor(out=OT[:, qs], in0=accO[:, 0:QB], in1=ri[:], op=ALU.mult),
                              r=[accO, ri], w=[OT])
                    k.dma(omixT_d[:, h, :], OT[:], r=[OT], w=[dt_("omla")], own=OT, accum_w=True)
                    pfr.i = 0
            k.barrier()
            if dbg in ("B", "B0", "B1", "B2"):
                break
            with contextlib.ExitStack() as st:
                C = Pools(k, st)
                try:
                    P = {"ss": C.ring(4, [128, 1], F32), "rs": C.ring(4, [128, 1], F32), "junk": C.sb([128, 128], BF16)}
                    gon = C.sb([128, 128], F32)
                    k.dma(gon[:], gon_d, w=[gon])
                    cw = C.sb([128, 24, 5], F32)
                    k.dma(cw[:], cw_d, w=[cw])
                    if cstop == 1:
                        raise _StopD()
                    W16 = NT * 16
                    H8 = NT * 8
                    gcs, eg, egs, ek, dec, bgc, nbeta, gq, bq, grem = (C.sb([128, W16], F32) for _ in range(10))
                    DKS = float(128 ** -0.5)
                    for d_ in range(2):
                        k.dma(gq[:, d_ * H8:(d_ + 1) * H8].rearrange("p (t n) -> p t n", n=8), gb_d[:, :, d_ * 8:(d_ + 1) * 8],
                              r=[dt_("gb%d" % b_) for b_ in range(NB)], w=[gq], own=gq, accum_w=(d_ == 1))
                        k.dma(bq[:, d_ * H8:(d_ + 1) * H8].rearrange("p (t n) -> p t n", n=8), gb_d[:, :, 16 + d_ * 8:16 + (d_ + 1) * 8],
                              r=[dt_("gb%d" % b_) for b_ in range(NB)], w=[bq], own=bq, accum_w=(d_ == 1))
                    if cstop == 2:
                        raise _StopD()
                    gpb = [C.sb([128, W16], BF16) for _ in range(3)]
                    gpf = [C.sb([128, W16], F32) for _ in range(3)]
                    k.dve(lambda e: e.tensor_copy(out=grem[:], in_=gq[:]), r=[gq], w=[grem])
                    for i3 in range(3):
                        k.dve(lambda e, i3=i3: e.tensor_copy(out=gpb[i3][:], in_=grem[:]), r=[grem], w=[gpb[i3]])
                        k.dve(lambda e, i3=i3: e.tensor_copy(out=gpf[i3][:], in_=gpb[i3][:]), r=[gpb[i3]], w=[gpf[i3]])
                        if i3 < 2:
                            k.dve(lambda e, i3=i3: e.tensor_tensor(out=grem[:], in0=grem[:], in1=gpf[i3][:], op=ALU.subtract),
                                  r=[grem, gpf[i3]], w=[grem])
                    if cstop == 3:
                        raise _StopD()
                    UPB, LOWB = cst_b[:, 4, :], cst_b[:, 2, :]
                    psA_, psT_ = pfr.next(), pfr.next()
                    for i3 in range(3):
                        k.pe(lambda e, i3=i3: e.matmul(psA_[:, 0:H8], lhsT=UPB, rhs=gpb[i3][:, 0:H8], start=(i3 == 0), stop=(i3 == 2)),
                             r=[cst_b, gpb[i3]], w=[psA_])
                    for i3 in range(3):
                        k.pe(lambda e, i3=i3: e.matmul(psA_[:, H8:W16], lhsT=LOWB, rhs=gpb[i3][:, H8:W16], start=(i3 == 0), stop=(i3 == 2)),
                             r=[cst_b, gpb[i3]], w=[psA_])
                    for i3 in range(3):
                        k.pe(lambda e, i3=i3: e.matmul(psT_[:, 0:W16], lhsT=ONESB, rhs=gpb[i3][:, :], start=(i3 == 0), stop=(i3 == 2)),
                             r=[cst_b, gpb[i3]], w=[psT_])
                    if cstop == 4:
                        raise _StopD()
                    k.act(lambda e: e.activation(out=gcs[:], in_=psA_[:, 0:W16], func=AF.Copy), r=[psA_], w=[gcs])
                    k.act(lambda e: e.activation(out=eg[:], in_=psA_[:, 0:W16], func=AF.Exp), r=[psA_], w=[eg])
                    k.act(lambda e: e.activation(out=dec[:], in_=psT_[:, 0:W16], func=AF.Exp), r=[psT_], w=[dec])
                    if cstop == 41:
                        raise _StopD()
                    k.dve(lambda e: e.tensor_tensor(out=ek[:], in0=psT_[:, 0:W16], in1=gcs[:], op=ALU.subtract), r=[psT_, gcs, dec], w=[ek])
                    if cstop == 411:
                        raise _StopD()
                    k.dve(lambda e: e.tensor_tensor(out=bgc[:], in0=bq[:], in1=eg[:], op=ALU.mult), r=[bq, eg], w=[bgc])
                    if cstop == 412:
                        raise _StopD()
                    k.dve(lambda e: e.tensor_scalar(out=nbeta[:], in0=bq[:], scalar1=-1.0, scalar2=None, op0=ALU.mult), r=[bq], w=[nbeta])
                    if cstop == 42:
                        raise _StopD()
                    k.act(lambda e: e.activation(out=ek[:], in_=ek[:], func=AF.Exp), r=[ek], w=[ek])
                    k.dve(lambda e: e.tensor_scalar(out=egs[:], in0=eg[:], scalar1=DKS, scalar2=None, op0=ALU.mult),
                          r=[eg], w=[egs])
                    if cstop == 5:
                        raise _StopD()
                    raw = C.sb([128, S + 4], F32)
                    cacc = C.sb([128, S], F32)
                    sil = cacc
                    sqb = C.sb([128, S], BF16)
                    qT, kT, vT = (C.sb([128, S], BF16) for _ in range(3))
                    o_d = [C.sb([128, S], F32) for _ in range(2)]
                    z_h = C.sb([128, NT, 128], F32)
                    ogT = C.sb([128, S], BF16)
                    rnr = C.ring(2, [128, QB], F32)
                    S32s = [C.sb([128, 128], F32) for _ in range(2)]
                    Sbfs = [C.sb([128, 128], BF16) for _ in range(2)]
                    f128 = C.ring(16, [128, 128], F32)
                    b128 = C.ring(100, [128, 128], BF16)
                    k.dve(lambda e: e.memset(raw[:, 0:4], 0.0), w=[raw])
                    k.dve(lambda e: e.memset(raw[:, S:S + 4], 0.0), w=[raw])
                    qkv_r = [dt_("qkv%d" % b) for b in range(NB)]
                    zs_r = [dt_("zs%d" % b) for b in range(NB)]
                    for h in range(8):
                        if dbg in ("C0", "D0", "D1", "BD") or "noC" in flags:
                            break
                        for which, dst in ((0, qT), (1, kT), (2, vT)):
                            ch = which * 8 + h
                            k.dma(raw[:, 2:S + 2], qkvT_d[:, ch, :], r=qkv_r, w=[raw])
                            k.dve(lambda e: e.tensor_scalar(out=cacc[:], in0=raw[:, 0:S], scalar1=cw[:, ch, 0:1], scalar2=None,
                                                            op0=ALU.mult), r=[raw, cw], w=[cacc])
                            for j in range(1, 5):
                                k.dve(lambda e, j=j: e.scalar_tensor_tensor(out=cacc[:], in0=raw[:, j:j + S],
                                                                            scalar=cw[:, ch, j:j + 1], in1=cacc[:],
                                                                            op0=ALU.mult, op1=ALU.add), r=[raw, cw, cacc], w=[cacc])
                            if which == 2:
                                k.act(lambda e: e.activation(out=dst[:], in_=cacc[:], func=AF.Silu), r=[cacc], w=[dst])
                                continue
                            k.act(lambda e: e.activation(out=sil[:], in_=cacc[:], func=AF.Silu), r=[cacc], w=[sil])
                            k.act(lambda e: e.activation(out=sqb[:], in_=sil[:], func=AF.Square), r=[sil], w=[sqb])
                            for blk in range(NQB):
                                sl = slice(blk * QB, (blk + 1) * QB)
                                ps = pfr.next()
                                k.pe(lambda e: e.matmul(ps[:, 0:QB], lhsT=ONESB, rhs=sqb[:, sl], start=True, stop=True),
                                     r=[cst_b, sqb], w=[ps])
                                rn = rnr.next()
                                k.act(lambda e: e.activation(out=rn[:], in_=ps[:, 0:QB], func=AF.Sqrt, bias=EPS), r=[ps], w=[rn])
                                k.dve(lambda e: e.reciprocal(out=rn[:], in_=rn[:]), r=[rn], w=[rn])
                                k.dve(lambda e: e.tensor_tensor(out=dst[:, sl], in0=sil[:, sl], in1=rn[:], op=ALU.mult),
                                      r=[sil, rn], w=[dst])
                        if dbg == "C1":
                            break
                        k.dma(z_h[:], zs_d[:, h * 128:(h + 1) * 128].rearrange("(t p) v -> p t v", p=128), r=zs_r, w=[z_h])
                        def unit_gen(dr):
                            TRI = UP if dr == 0 else LOW
                            SM = SLOW if dr == 0 else SUP
                            IMT = UP if dr == 0 else LOW
                            S32, Sbf = S32s[dr], Sbfs[dr]
                            k.pool(lambda e: e.memset(S32[:], 0.0), w=[S32])
                            k.pool(lambda e: e.memset(Sbf[:], 0.0), w=[Sbf])
                            od = o_d[dr]
                            for c in (range(NT) if dr == 0 else range(NT - 1, -1, -1)):
                                cs_ = slice(c * 128, (c + 1) * 128)
                                ci = dr * H8 + c * 8 + h
                                bcol = bq[:, ci:ci + 1]
                                psg = pfr.next()
                                for i3 in range(3):
                                    gT = b128.next()
                                    k.dve(lambda e, i3=i3: e.tensor_scalar(out=gT[:], in0=TRI, scalar1=gpf[i3][:, ci:ci + 1],
                                                                           scalar2=None, op0=ALU.mult), r=[cst_f, gpf[i3]], w=[gT])
                                    k.pe(lambda e, i3=i3: e.matmul(psg[:, 0:128], lhsT=ONESB, rhs=gT[:], start=(i3 == 0), stop=(i3 == 2)),
                                         r=[cst_b, gT], w=[psg])
                                yield
                                dm, dtm = f128.next(), f128.next()
                                k.dve(lambda e: e.tensor_scalar(out=dm[:], in0=psg[:, 0:128], scalar1=gcs[:, ci:ci + 1], scalar2=0.0,
                                                                op0=ALU.subtract, op1=ALU.max), r=[psg, gcs], w=[dm])
                                k.dve(lambda e: e.tensor_scalar(out=dtm[:], in0=psg[:, 0:128], scalar1=gcs[:, ci:ci + 1], scalar2=0.0,
                                                                op0=ALU.subtract, op1=ALU.min), r=[psg, gcs], w=[dtm])
                                k.act(lambda e: e.activation(out=dm[:], in_=dm[:], func=AF.Exp, scale=-1.0), r=[dm], w=[dm])
                                k.act(lambda e: e.activation(out=dtm[:], in_=dtm[:], func=AF.Exp), r=[dtm], w=[dtm])
                                k.pool(lambda e: e.tensor_tensor(out=dm[:], in0=dm[:], in1=SM, op=ALU.mult), r=[dm, cst_f], w=[dm])
                                k.pool(lambda e: e.tensor_tensor(out=dtm[:], in0=dtm[:], in1=IMT, op=ALU.mult), r=[dtm, cst_f], w=[dtm])
                                psG = pfr.next()
                                k.pe(lambda e: e.matmul(psG[:, 0:128], lhsT=kT[:, cs_], rhs=kT[:, cs_], start=True, stop=True),
                                     r=[kT], w=[psG])
                                psK = pfr.next()
                                k.pe(lambda e: e.matmul(psK[:, 0:128], lhsT=kT[:, cs_], rhs=qT[:, cs_], start=True, stop=True),
                                     r=[kT, qT], w=[psK])
                                bank2 = pbr.next()
                                k.pe(lambda e: e.transpose(bank2[:, 0:128], kT[:, cs_], IDB()), r=[kT, cst_b], w=[bank2])
                                k.pe(lambda e: e.transpose(bank2[:, 128:256], vT[:, cs_], IDB()), r=[vT, cst_b], w=[bank2])
                                yield
                                Ln = b128.next()
                                k.dve(lambda e: e.scalar_tensor_tensor(out=Ln[:], in0=psG[:, 0:128], scalar=nbeta[:, ci:ci + 1],
                                                                       in1=dm[:], op0=ALU.mult, op1=ALU.mult),
                                      r=[psG, nbeta, dm], w=[Ln])
                                AT = b128.next()
                                k.dve(lambda e: e.scalar_tensor_tensor(out=AT[:], in0=psK[:, 0:128], scalar=DKS, in1=dtm[:],
                                                                       op0=ALU.mult, op1=ALU.mult), r=[psK, dtm], w=[AT])
                                kbg, kd, vb = b128.next(), b128.next(), b128.next()
                                k.act(lambda e: e.activation(out=kbg[:], in_=bank2[:, 0:128], func=AF.Copy, scale=bgc[:, ci:ci + 1]),
                                      r=[bank2, bgc], w=[kbg])
                                k.act(lambda e: e.activation(out=kd[:], in_=bank2[:, 0:128], func=AF.Copy, scale=ek[:, ci:ci + 1]),
                                      r=[bank2, ek], w=[kd])
                                k.act(lambda e: e.activation(out=vb[:], in_=bank2[:, 128:256], func=AF.Copy, scale=bcol),
                                      r=[bank2, bq], w=[vb])
                                bank = pbr.next()
                                k.pe(lambda e: e.transpose(bank[:, 0:128], Ln[:], IDB()), r=[Ln, cst_b], w=[bank])
                                yield
                                Nk = b128.next()
                                k.act(lambda e: e.activation(out=Nk[:], in_=bank[:, 0:128], func=AF.Copy), r=[bank], w=[Nk])
                                NkT = Ln
                                Pm = b128.next()
                                k.dve(lambda e: e.tensor_tensor(out=Pm[:], in0=Nk[:], in1=cst_b[:, 0, :], op=ALU.add),
                                      r=[Nk, cst_b], w=[Pm])
                                Pt = b128.next()
                                k.pool(lambda e: e.tensor_tensor(out=Pt[:], in0=NkT[:], in1=cst_b[:, 0, :], op=ALU.add),
                                       r=[NkT, cst_b], w=[Pt])
                                for lev in range(1, 7):
                                    psA = pfr.next()
                                    k.pe(lambda e: e.matmul(psA[:, 0:128], lhsT=Nk[:], rhs=NkT[:], start=True, stop=True),
                                         r=[Nk, NkT], w=[psA])
                                    if lev < 6:
                                        psB = pfr.next()
                                        k.pe(lambda e: e.matmul(psB[:, 0:128], lhsT=NkT[:], rhs=Nk[:], start=True, stop=True),
                                             r=[Nk, NkT], w=[psB])
                                    yield
                                    NkT2 = b128.next()
                                    k.act(lambda e: e.activation(out=NkT2[:], in_=psA[:, 0:128], func=AF.Copy), r=[psA], w=[NkT2])
                                    if lev < 6:
                                        Nk2 = b128.next()
                                        k.act(lambda e: e.activation(out=Nk2[:], in_=psB[:, 0:128], func=AF.Copy), r=[psB], w=[Nk2])
                                    else:
                                        Nk2 = None
                                    psC = pfr.next()
                                    k.pe(lambda e: e.matmul(psC[:, 0:128], lhsT=NkT2[:], rhs=Pm[:], start=True, stop=True),
                                         r=[NkT2, Pm], w=[psC])
                                    psD = pfr.next()
                                    k.pe(lambda e: e.matmul(psD[:, 0:128], lhsT=Pm[:], rhs=NkT2[:], start=True, stop=True),
                                         r=[NkT2, Pm], w=[psD])
                                    yield
                                    Pn = b128.next()
                                    k.dve(lambda e: e.tensor_tensor(out=Pn[:], in0=psC[:, 0:128], in1=Pm[:], op=ALU.add),
                                          r=[psC, Pm], w=[Pn])
                                    Ptn = b128.next()
                                    k.dve(lambda e: e.tensor_tensor(out=Ptn[:], in0=psD[:, 0:128], in1=Pt[:], op=ALU.add),
                                          r=[psD, Pt], w=[Ptn])
                                    Pm, Pt, Nk, NkT = Pn, Ptn, Nk2, NkT2
                                psR = pfr.next()
                                k.pe(lambda e: e.matmul(psR[:, 0:128], lhsT=Ln[:], rhs=Pm[:], start=True, stop=True), r=[Ln, Pm], w=[psR])
                                IX = b128.next()
                                k.pool(lambda e: e.tensor_tensor(out=IX[:], in0=cst_b[:, 0, :], in1=Pm[:], op=ALU.subtract),
                                       r=[cst_b, Pm], w=[IX])
                                yield
                                Rr = b128.next()
                                k.dve(lambda e: e.tensor_tensor(out=Rr[:], in0=psR[:, 0:128], in1=IX[:], op=ALU.add),
                                      r=[psR, IX], w=[Rr])
                                psX = pfr.next()
                                k.pe(lambda e: e.matmul(psX[:, 0:128], lhsT=Pt[:], rhs=Rr[:], start=True, stop=True), r=[Pt, Rr], w=[psX])
                                yield
                                TT = b128.next()
                                k.dve(lambda e: e.tensor_tensor(out=TT[:], in0=psX[:, 0:128], in1=Pm[:], op=ALU.add),
                                      r=[psX, Pm], w=[TT])
                                psu = pfr.next()
                                k.pe(lambda e: e.matmul(psu[:, 0:128], lhsT=TT[:], rhs=vb[:], start=True, stop=True), r=[TT, vb], w=[psu])
                                psw = pfr.next()
                                k.pe(lambda e: e.matmul(psw[:, 0:128], lhsT=kbg[:], rhs=TT[:], start=True, stop=True), r=[TT, kbg], w=[psw])
                                yield
                                u = f128.next()
                                k.act(lambda e: e.activation(out=u[:], in_=psu[:, 0:128], func=AF.Copy), r=[psu], w=[u])
                                wT = b128.next()
                                k.act(lambda e: e.activation(out=wT[:], in_=psw[:, 0:128], func=AF.Copy), r=[psw], w=[wT])
                                ps1 = pfr.next()
                                k.pe(lambda e: e.matmul(ps1[:, 0:128], lhsT=wT[:], rhs=Sbf[:], start=True, stop=True), r=[wT, Sbf], w=[ps1])
                                ps2 = pfr.next()
                                k.pe(lambda e: e.matmul(ps2[:, 0:128], lhsT=qT[:, cs_], rhs=Sbf[:], start=True, stop=True), r=[qT, Sbf], w=[ps2])
                                yield
                                vn = b128.next()
                                k.dve(lambda e: e.tensor_tensor(out=vn[:], in0=u[:], in1=ps1[:, 0:128], op=ALU.subtract),
                                      r=[u, ps1], w=[vn])
                                tmp = f128.next()
                                k.act(lambda e: e.activation(out=tmp[:], in_=ps2[:, 0:128], func=AF.Copy, scale=egs[:, ci:ci + 1]),
                                      r=[ps2, egs], w=[tmp])
                                ps3 = pfr.next()
                                k.pe(lambda e: e.matmul(ps3[:, 0:128], lhsT=AT[:], rhs=vn[:], start=True, stop=True), r=[AT, vn], w=[ps3])
                                ps4 = pfr.next()
                                k.pe(lambda e: e.matmul(ps4[:, 0:128], lhsT=kd[:], rhs=vn[:], start=True, stop=True), r=[kd, vn], w=[ps4])
                                yield
                                k.dve(lambda e: e.tensor_tensor(out=od[:, cs_], in0=tmp[:], in1=ps3[:, 0:128], op=ALU.add),
                                      r=[tmp, ps3], w=[od])
                                k.dve(lambda e: e.scalar_tensor_tensor(out=S32[:], in0=S32[:], scalar=dec[:, ci:ci + 1], in1=ps4[:, 0:128],
                                                                       op0=ALU.mult, op1=ALU.add), r=[S32, dec, ps4], w=[S32])
                                k.pool(lambda e: e.tensor_copy(out=Sbf[:], in_=S32[:]), r=[S32], w=[Sbf])
                                yield

                        gens = [unit_gen(0), unit_gen(1)]
                        while gens:
                            for g_ in list(gens):
                                try:
                                    next(g_)
                                except StopIteration:
                                    gens.remove(g_)
                        for c in range(NT):
                            cs_ = slice(c * 128, (c + 1) * 128)
                            k.dve(lambda e: e.tensor_tensor(out=o_d[0][:, cs_], in0=o_d[0][:, cs_], in1=o_d[1][:, cs_], op=ALU.add),
                                   r=[o_d[0], o_d[1]], w=[o_d[0]])
                            rs = rstd_of(P, (o_d[0], o_d[0][:, cs_]), 128, P["junk"])
                            tmp = f128.next()
                            k.dve(lambda e: e.scalar_tensor_tensor(out=tmp[:], in0=o_d[0][:, cs_], scalar=rs[:, 0:1], in1=gon[:],
                                                                   op0=ALU.mult, op1=ALU.mult), r=[o_d[0], rs, gon], w=[tmp])
                            onb = b128.next()
                            k.dve(lambda e: e.tensor_tensor(out=onb[:], in0=tmp[:], in1=z_h[:, c, :], op=ALU.mult),
                                   r=[tmp, z_h], w=[onb])
                            bank = pbr.next()
                            k.pe(lambda e: e.transpose(bank[:, 0:128], onb[:], IDB()), r=[onb, cst_b], w=[bank])
                            k.act(lambda e: e.activation(out=ogT[:, cs_], in_=bank[:, 0:128], func=AF.Copy), r=[bank], w=[ogT])
                        k.dma(omixT_d[:, 8 + h, :], ogT[:], r=[ogT], w=[dt_("ogdn")], own=ogT, accum_w=True)
                except _StopD:
                    pass
            k.barrier()
            if dbg in ("C", "C0", "C1"):
                break
            if cstop != 99:
                break
            with contextlib.ExitStack() as st:
                Dp = Pools(k, st)
                P = {"ss": Dp.ring(4, [128, 1], F32), "rs": Dp.ring(4, [128, 1], F32),
                     "junk": Dp.sb([128, D], BF16), "nbf": Dp.ring(1, [128, D], BF16),
                     "wring": Dp.ring(3, [128, KC, 512], BF16), "hT": Dp.sb([128, FC, TB], BF16),
                     "silu": Dp.ring(2, [128, 512], F32), "gBs": [Dp.sb([128, D], F32) for _ in range(2)], "gcur": {}}
                nT = Dp.sb([128, KC, TB], BF16)
                xr = Dp.ring(2, [128, D], F32)
                ysb = [Dp.sb([128, D], F32) for _ in range(TPB)]
                mT = P["hT"]
                KmT = Dp.sb([128, 4, NMEM], BF16)
                Vm = Dp.sb([128, 2, 512], BF16)
                qmT = Dp.sb([128, 4, TB], BF16)
                omT = Dp.sb([128, 4, TB], BF16)
                Pr = Dp.ring(2, [128, TB], BF16)
                rinv = Dp.ring(1, [128, TB], F32)
                mscale = float(128 ** -0.5)
                TPB_save = TPB

                def load_mem(t):
                    xt = xr.next()
                    k.dma(xt[:], mem_d[sq, t * 128:(t + 1) * 128, :], w=[xt])
                    return xt
                for t in range(NMEM // 128 if dstop > 0 else 0):
                    xt = load_mem(t)
                    rs = rstd_of(P, (xt, xt[:, :]), D, P["junk"])
                    nb = P["nbf"].next()
                    k.act(lambda e: e.activation(out=nb[:], in_=xt[:], func=AF.Copy, scale=rs[:, 0:1]), r=[xt, rs], w=[nb])
                    transpose_to(P, nb, D, mT, t, (gpre, PRE["mem_kv_norm_g"]))
                sk = load_w(P, wb["w_mk"], 0, KC, 0, 512)
                for hh in range(4 if dstop > 0 else 0):
                    acc = pfr.next()
                    for kc in range(KC):
                        k.pe(lambda e, kc=kc: e.matmul(acc[:, 0:NMEM], lhsT=sk[:, kc, hh * 128:(hh + 1) * 128], rhs=mT[:, kc, 0:NMEM],
                                                       start=(kc == 0), stop=(kc == KC - 1)), r=[sk, mT], w=[acc])
                    k.act(lambda e: e.activation(out=KmT[:, hh, :], in_=acc[:, 0:NMEM], func=AF.Copy), r=[acc], w=[KmT])
                sv = load_w(P, wb["w_mv"], 0, KC, 0, 512)
                for t in range(NMEM // 128 if dstop > 0 else 0):
                    acc = pfr.next()
                    for kc in range(KC):
                        k.pe(lambda e, kc=kc: e.matmul(acc[:, :], lhsT=mT[:, kc, t * 128:(t + 1) * 128], rhs=sv[:, kc, :],
                                                       start=(kc == 0), stop=(kc == KC - 1)), r=[sv, mT], w=[acc])
                    k.act(lambda e: e.activation(out=Vm[:, t, :], in_=acc[:, :], func=AF.Copy), r=[acc], w=[Vm])
                om_r = [dt_("omla"), dt_("ogdn")]
                for blk in range(NB if dstop > 1 else 0):
                    if dbg == "D1" and blk == 1:
                        break
                    t0 = blk * TB
                    hrow = dt_("h1%d" % blk)
                    k.dma(nT[:], omixT_d[:, :, t0:t0 + TB], r=om_r, w=[nT])
                    for dg in range(4):
                        so = load_w(P, wb["w_out"], 0, KC, dg * 512, 512)
                        for t in range(TPB):
                            acc = pfr.next()
                            for kc in range(KC):
                                k.pe(lambda e, kc=kc: e.matmul(acc[:, :], lhsT=nT[:, kc, t * 128:(t + 1) * 128], rhs=so[:, kc, :],
                                                               start=(kc == 0), stop=(kc == KC - 1)), r=[so, nT], w=[acc])
                            k.act(lambda e: e.activation(out=ysb[t][:, dg * 512:(dg + 1) * 512], in_=acc[:, :], func=AF.Copy),
                                  r=[acc], w=[ysb[t]])

                    def mk_loader(gname, half):
                        def ld(t):
                            xt = xr.next()
                            k.dma(xt[:], h1_d[t0 + t * 128:t0 + (t + 1) * 128, :], r=[hrow], w=[xt])
                            post_residual(P, ysb[t], gname, xt, xt, half)
                            k.dma(h1_d[t0 + t * 128:t0 + (t + 1) * 128, :], xt[:], r=[xt], w=[hrow], own=xt, accum_w=True)
                            return xt
                        return ld
                    norm_transpose_block(P, mk_loader("mix_post_g", False), "mem_pre_g", nT)
                    if dstop == 2:
                        break
                    sq_ = load_w(P, wb["w_mq"], 0, KC, 0, 512)
                    for hh in range(4):
                        acc = pfr.next()
                        for kc in range(KC):
                            k.pe(lambda e, kc=kc: e.matmul(acc[:, 0:TB], lhsT=sq_[:, kc, hh * 128:(hh + 1) * 128], rhs=nT[:, kc, :],
                                                           start=(kc == 0), stop=(kc == KC - 1)), r=[sq_, nT], w=[acc])
                        k.dve(lambda e: e.tensor_copy(out=qmT[:, hh, :], in_=acc[:, 0:TB]), r=[acc], w=[qmT])
                    for hh in range(4):
                        accO, accR = pf[0], pf[1]
                        for mt in range(2):
                            sp_ = pf[2 + mt]
                            k.pe(lambda e: e.matmul(sp_[:, 0:TB], lhsT=KmT[:, hh, mt * 128:(mt + 1) * 128], rhs=qmT[:, hh, :],
                                                    start=True, stop=True), r=[KmT, qmT], w=[sp_])
                            p_ = Pr.next()
                            k.act(lambda e: e.activation(out=p_[:], in_=sp_[:, 0:TB], func=AF.Exp, scale=mscale), r=[sp_], w=[p_])
                            k.pe(lambda e: e.matmul(accO[:, 0:TB], lhsT=Vm[:, mt, hh * 128:(hh + 1) * 128], rhs=p_[:],
                                                    start=(mt == 0), stop=(mt == 1)), r=[Vm, p_], w=[accO])
                            k.pe(lambda e: e.matmul(accR[:, 0:TB], lhsT=ONESB, rhs=p_[:], start=(mt == 0), stop=(mt == 1)),
                                 r=[cst_b, p_], w=[accR])
                        ri = rinv.next()
                        k.dve(lambda e: e.reciprocal(out=ri[:], in_=accR[:, 0:TB]), r=[accR], w=[ri])
                        k.dve(lambda e: e.tensor_tensor(out=omT[:, hh, :], in0=accO[:, 0:TB], in1=ri[:], op=ALU.mult),
                              r=[accO, ri], w=[omT])
                    pfr.i = 0
                    for dg in range(4):
                        so = load_w(P, wb["w_mo"], 0, 4, dg * 512, 512)
                        for t in range(TPB):
                            acc = pfr.next()
                            for hh in range(4):
                                k.pe(lambda e, hh=hh: e.matmul(acc[:, :], lhsT=omT[:, hh, t * 128:(t + 1) * 128], rhs=so[:, hh, :],
                                                               start=(hh == 0), stop=(hh == 3)), r=[so, omT], w=[acc])
                            k.act(lambda e: e.activation(out=ysb[t][:, dg * 512:(dg + 1) * 512], in_=acc[:, :], func=AF.Copy),
                                  r=[acc], w=[ysb[t]])
                    if dstop == 3:
                        break
                    norm_transpose_block(P, mk_loader("mem_post_g", False), "ffn2_pre_g", nT)
                    ffn(P, nT, wb["ffn2_w_gate"], wb["ffn2_w_up"], wb["ffn2_w_down"], ysb)
                    if dstop == 4:
                        break
                    for t in range(TPB):
                        xt = xr.next()
                        k.dma(xt[:], h1_d[t0 + t * 128:t0 + (t + 1) * 128, :], r=[hrow], w=[xt])
                        post_residual(P, ysb[t], "ffn2_post_g", xt, xt, True)
                        rs = rstd_of(P, (xt, xt[:, :]), D, P["junk"])
                        gB = load_gain(P, "final_norm_g", 1)
                        k.dve(lambda e: e.scalar_tensor_tensor(out=xt[:], in0=xt[:], scalar=rs[:, 0:1], in1=gB[:],
                                                               op0=ALU.mult, op1=ALU.mult), r=[xt, rs, gB], w=[xt])
                        k.dma(y_d[sq, t0 + t * 128:t0 + (t + 1) * 128, :], xt[:], r=[xt], w=[dt_("y")], own=xt, accum_w=True)
            k.barrier()
        k.finish()
        print("BUILD stats: nins=%d nwait=%d nsems=%d" % (k.nins, k.nwait, len(k.sems)), flush=True)
    return nc


def host_layouts(inp, S):
    f = np.float32
    out = {}
    for n in ["ffn1_w_gate", "ffn1_w_up", "ffn1_w_down", "ffn2_w_gate", "ffn2_w_up", "ffn2_w_down",
              "w_in", "w_uq", "w_ukv", "w_out", "w_mq", "w_mk", "w_mv", "w_mo"]:
        out[n] = np.ascontiguousarray(np.asarray(inp[n], f)[0])
    for n in ["ffn1_pre_g", "ffn1_post_g", "mix_pre_g", "mix_post_g", "mem_pre_g", "mem_kv_norm_g",
              "mem_post_g", "ffn2_pre_g", "ffn2_post_g", "final_norm_g"]:
        out[n + "_bc"] = np.ascontiguousarray(np.broadcast_to(np.asarray(inp[n], f)[0][None, :], (128, D)))
    pre = ["ffn1_pre_g", "mix_pre_g", "mem_pre_g", "mem_kv_norm_g", "ffn2_pre_g"]
    out["gpre"] = np.ascontiguousarray(
        np.stack([np.asarray(inp[n], f)[0].reshape(KC, 128).T for n in pre], axis=1))
    qg = np.asarray(inp["mla_q_norm_g"], f)[0].reshape(4, 128).T
    kg = np.asarray(inp["mla_kv_norm_g"], f)[0].reshape(2, 128).T
    out["qkg"] = np.ascontiguousarray(np.concatenate([qg, kg], axis=1))
    i = np.arange(128)
    p, fr = i[:, None], i[None, :]
    out["consts"] = np.ascontiguousarray(np.stack(
        [np.eye(128), np.ones((128, 128)), fr <= p, fr < p, fr >= p, fr > p], axis=1).astype(f))
    NT = S // 128
    inv_freq = (10000.0 ** (-np.arange(0, ROPE, 2, dtype=f) / f(ROPE))).astype(f)
    ang = (np.arange(S, dtype=f)[:, None] * inv_freq[None, :]).astype(f)
    cos = np.cos(ang).astype(f).reshape(NT, 128, 32).transpose(1, 0, 2)
    sin = np.sin(ang).astype(f).reshape(NT, 128, 32).transpose(1, 0, 2)
    out["rope"] = np.ascontiguousarray(np.stack([cos, sin], axis=1))
    al = np.asarray(inp["gdn_a_log"], f)[0].reshape(16)
    dtb = np.asarray(inp["gdn_dt_bias"], f)[0].reshape(16)
    out["gdnp"] = np.ascontiguousarray(np.broadcast_to(np.stack([al, dtb], 0)[None], (128, 2, 16)))
    out["gon"] = np.ascontiguousarray(np.broadcast_to(np.asarray(inp["gdn_out_norm_g"], f)[0][None, :], (128, 128)))
    cw = np.asarray(inp["gdn_conv_w"], f)[0]
    out["cw"] = np.ascontiguousarray(cw.reshape(5, 24, 128).transpose(2, 1, 0))
    return out


S_FULL = 4096
_NC_CACHE = {}


def kernel(**inputs):
    S = S_FULL
    NSEQ = 2
    if "nc" not in _NC_CACHE:
        _NC_CACHE["nc"] = build(S, NSEQ)
    nc = _NC_CACHE["nc"]
    hl = host_layouts(inputs, S)
    xp = np.asarray(inputs["x_prompt"], np.float32)
    xs = np.asarray(inputs["x_sample"], np.float32)
    mp = np.asarray(inputs["mem_prompt"], np.float32)
    ms = np.asarray(inputs["mem_sample"], np.float32)
    in_maps = []
    for c in range(8):
        m = dict(hl)
        m["x"] = np.ascontiguousarray(np.stack([xp[c], xs[c % 2]], axis=0))
        m["mem"] = np.ascontiguousarray(np.stack([mp[c], ms[c % 2]], axis=0))
        in_maps.append(m)
    res = run_bass_kernel_spmd(nc, in_maps, core_ids=list(range(8)))
    yp = np.stack([np.asarray(res.results[c]["y"])[0] for c in range(8)], axis=0).astype(np.float32)
    ys = np.stack([np.asarray(res.results[c]["y"])[1] for c in range(2)], axis=0).astype(np.float32)
    return (yp, ys)
```

```python
import contextlib
import numpy as np
import concourse.bass as bass
import concourse.mybir as mybir
from concourse.bass_utils import run_bass_kernel_spmd

F32 = mybir.dt.float32
BF16 = mybir.dt.bfloat16
AF = mybir.ActivationFunctionType
ALU = mybir.AluOpType

D = 2048
DFF = 5632
NMEM = 256
EPS = 1e-6
QR, KVR, ROPE, NOPE, VD = 512, 256, 64, 128, 128
INW = 4960
KC = D // 128
FC = DFF // 128


class Trk:
    __slots__ = ("w", "r", "dsem")

    def __init__(self):
        self.w = {}
        self.r = {}
        self.dsem = None


class Buf:
    def __init__(self, t, psum=False):
        self.t = t
        self.T = Trk()
        self.psum = psum

    def __getitem__(self, idx):
        return self.t[idx]


class Eng:
    def __init__(self, name, e, semid, compute):
        self.name, self.e, self.semid, self.compute = name, e, semid, compute
        self.cnt = 0
        self.seen = {}


class K:
    def __init__(self, nc, stack):
        self.nc = nc
        self.stack = stack
        self.sems = []
        self.semcnt = []
        self.eng = {}
        for name, e, comp in (("pe", nc.tensor, True), ("act", nc.scalar, True), ("dve", nc.vector, True),
                              ("pool", nc.gpsimd, True), ("sp", nc.sync, False)):
            sid = self.newsem("e_" + name)
            self.eng[name] = Eng(name, e, sid, comp)
        self.free_dsems = []
        self.nwait = 0
        self.nins = 0

    def newsem(self, name):
        s = self.stack.enter_context(self.nc.semaphore(name))
        self.sems.append(s)
        self.semcnt.append(0)
        return len(self.sems) - 1

    def _wait(self, E, sid, val, raw=True):
        if val <= 0:
            return
        if sid == E.semid:
            if E.name == "pe" or not E.compute or not raw:
                return
        if E.seen.get(sid, 0) >= val:
            return
        E.e.wait_ge(self.sems[sid], val)
        E.seen[sid] = val
        self.nwait += 1

    def _deps(self, r, w):
        need = {}
        for x in r:
            for s, v in x.T.w.items():
                if need.get(s, 0) < v:
                    need[s] = v
            if x.psum:
                for s, v in x.T.r.items():
                    if need.get(s, 0) < v:
                        need[s] = v
        for x in w:
            for s, v in x.T.w.items():
                if need.get(s, 0) < v:
                    need[s] = v
            for s, v in x.T.r.items():
                if need.get(s, 0) < v:
                    need[s] = v
        return need

    def op(self, eng, fn, r=(), w=()):
        E = self.eng[eng]
        rawv = 0
        for x in r:
            rawv = max(rawv, x.T.w.get(E.semid, 0))
        for s, v in self._deps(r, w).items():
            if s == E.semid:
                self._wait(E, s, v, raw=True)
            else:
                self._wait(E, s, v)
        ins = fn(E.e)
        ins.then_inc(self.sems[E.semid], 1)
        E.cnt += 1
        self.semcnt[E.semid] = E.cnt
        self.nins += 1
        for x in r:
            if x.T.r.get(E.semid, 0) < E.cnt:
                x.T.r[E.semid] = E.cnt
        for x in w:
            x.T.w = {E.semid: E.cnt}
            x.T.r = {}

    def pe(self, fn, r=(), w=()):
        self.op("pe", fn, r, w)

    def act(self, fn, r=(), w=()):
        self.op("act", fn, r, w)

    def dve(self, fn, r=(), w=()):
        self.op("dve", fn, r, w)

    def pool(self, fn, r=(), w=()):
        self.op("pool", fn, r, w)

    def dma(self, out, in_, r=(), w=(), own=None, q="sp", accum_w=False):
        E = self.eng[q]
        if own is None:
            own = w[0] if w else r[0]
        T = own.T
        if T.dsem is None:
            T.dsem = self.newsem("d%d" % len(self.sems))
        sid = T.dsem
        need = self._deps(r, [] if accum_w else w)
        if accum_w:
            for x in w:
                for s, v in x.T.r.items():
                    if need.get(s, 0) < v:
                        need[s] = v
        if not accum_w and need.get(sid, 0) < self.semcnt[sid]:
            need[sid] = self.semcnt[sid]
        for s, v in need.items():
            self._wait(E, s, v)
        ins = E.e.dma_start(out=out, in_=in_)
        ins.then_inc(self.sems[sid], 16)
        self.semcnt[sid] += 16
        val = self.semcnt[sid]
        self.nins += 1
        for x in r:
            if x.T.r.get(sid, 0) < val:
                x.T.r[sid] = val
        for x in w:
            if accum_w:
                x.T.w[sid] = val
            else:
                x.T.w = {sid: val}
                x.T.r = {}

    def barrier(self):
        for E in self.eng.values():
            for sid in range(len(self.sems)):
                if sid != E.semid:
                    self._wait(E, sid, self.semcnt[sid])

    def finish(self):
        E = self.eng["sp"]
        for sid in range(len(self.sems)):
            self._wait(E, sid, self.semcnt[sid])


class Pools:
    _uid = [0]

    def __init__(self, k, stack):
        self.k, self.stack, self.n = k, stack, 0

    def sb(self, shape, dt, name=None):
        Pools._uid[0] += 1
        nm = "%s_%d" % (name or "sb", Pools._uid[0])
        return Buf(self.stack.enter_context(self.k.nc.sbuf_tensor(nm, list(shape), dt)))

    def ps(self, shape, dt, name=None):
        Pools._uid[0] += 1
        nm = "%s_%d" % (name or "ps", Pools._uid[0])
        return Buf(self.stack.enter_context(self.k.nc.psum_tensor(nm, list(shape), dt)), psum=True)

    def ring(self, n, shape, dt, name=None):
        return Ring([self.sb(shape, dt, name) for _ in range(n)])


class Ring:
    def __init__(self, bufs):
        self.bufs, self.i = bufs, 0

    def next(self):
        b = self.bufs[self.i % len(self.bufs)]
        self.i += 1
        return b


class _StopD(Exception):
    pass


def build(S, NSEQ, dbg=False):
    nc = bass.Bass("TRN2", target_bir_lowering=False)
    flags = set(str(dbg).split("+")) if dbg else set()
    dstop = 99
    cstop = 99
    for f_ in flags:
        if f_.startswith("ds"):
            dstop = int(f_[2:])
        if f_.startswith("cs"):
            cstop = int(f_[2:])
    if "noB" in flags or "noC" in flags or dstop < 99 or cstop != 99:
        dbg = "X"
    NT = S // 128
    TB = min(512, S)
    NB = S // TB
    TPB = TB // 128
    QB = TB
    NQB = S // QB

    def din(name, shape, dt=F32):
        return nc.dram_tensor(name, list(shape), dt, kind="ExternalInput").ap()

    def dscr(name, shape, dt):
        return nc.dram_tensor(name, list(shape), dt, kind=("ExternalOutput" if dbg else "Internal")).ap()

    x_d = din("x", [NSEQ, S, D])
    mem_d = din("mem", [NSEQ, NMEM, D])
    y_d = nc.dram_tensor("y", [NSEQ, S, D], F32, kind="ExternalOutput").ap()
    wnames = {"ffn1_w_gate": (D, DFF), "ffn1_w_up": (D, DFF), "ffn1_w_down": (DFF, D),
              "ffn2_w_gate": (D, DFF), "ffn2_w_up": (D, DFF), "ffn2_w_down": (DFF, D),
              "w_in": (D, INW), "w_uq": (QR, 8 * 192), "w_ukv": (KVR, 8 * 256), "w_out": (D, D),
              "w_mq": (D, 512), "w_mk": (D, 512), "w_mv": (D, 512), "w_mo": (512, D)}
    wf = {n: din(n, s) for n, s in wnames.items()}
    wb = {n: nc.dram_tensor(n + "_b", list(s), BF16, kind="Internal").ap() for n, s in wnames.items()}
    gnames = ["ffn1_pre_g", "ffn1_post_g", "mix_pre_g", "mix_post_g", "mem_pre_g", "mem_kv_norm_g",
              "mem_post_g", "ffn2_pre_g", "ffn2_post_g", "final_norm_g"]
    gbd = {n: din(n + "_bc", [128, D]) for n in gnames}
    gpd = din("gpre", [128, 5, KC])
    qkg_d = din("qkg", [128, 6])
    consts_d = din("consts", [128, 6, 128])
    rope_d = din("rope", [128, 2, NT, 32])
    gdnp_d = din("gdnp", [128, 2, 16])
    gon_d = din("gon", [128, 128])
    cw_d = din("cw", [128, 24, 5])

    h1_d = dscr("h1_s", [S, D], F32)
    cqnT_d = dscr("cqnT_s", [128, 4, S], BF16)
    ckvnT_d = dscr("ckvnT_s", [128, 2, S], BF16)
    krT_d = dscr("krT_s", [64, S], BF16)
    qkvT_d = dscr("qkvT_s", [128, 24, S], F32)
    zs_d = dscr("zs_s", [S, 1024], F32)
    gb_d = dscr("gb_s", [128, NT, 32], F32)
    omixT_d = dscr("omixT_s", [128, 16, S], BF16)

    with contextlib.ExitStack() as gstack:
        k = K(nc, gstack)
        GP = Pools(k, gstack)
        pf = [GP.ps([128, 512], F32, "pf") for _ in range(6)]
        pb = [GP.ps([128, 1024], BF16, "pb") for _ in range(2)]
        pfr = Ring(pf)
        pbr = Ring(pb)
        class DT:
            pass
        dtrk = {}

        def dt_(name):
            if name not in dtrk:
                dtrk[name] = Buf(None)
            return dtrk[name]

        cst_f = GP.sb([128, 6, 128], F32, "cstf")
        cst_b = GP.sb([128, 6, 128], BF16, "cstb")
        k.dma(cst_f[:], consts_d, w=[cst_f])
        k.dve(lambda e: e.tensor_copy(out=cst_b[:], in_=cst_f[:]), r=[cst_f], w=[cst_b])
        IDB = lambda n=128: cst_b[0:n, 0, 0:n]
        ONESB = cst_b[:, 1, :]
        ONESF = cst_f[:, 1, :]
        LOW, SLOW, UP, SUP = (cst_f[:, i, :] for i in (2, 3, 4, 5))
        gpre = GP.sb([128, 5, KC], F32, "gpre")
        k.dma(gpre[:], gpd, w=[gpre])
        qkg = GP.sb([128, 6], F32, "qkg")
        k.dma(qkg[:], qkg_d, w=[qkg])
        PRE = {"ffn1_pre_g": 0, "mix_pre_g": 1, "mem_pre_g": 2, "mem_kv_norm_g": 3, "ffn2_pre_g": 4}

        wtrks = {n: Buf(None) for n in wnames}
        wkey = {id(wb[n]): n for n in wnames}
        corder = ["ffn1_w_gate", "ffn1_w_up", "ffn1_w_down", "w_in", "w_uq", "w_ukv", "w_out", "w_mq", "w_mk", "w_mv", "w_mo",
                  "ffn2_w_gate", "ffn2_w_up", "ffn2_w_down"]
        for n in corder:
            rows, cols = wnames[n]
            step = 256
            for r0 in range(0, rows, step):
                r1 = min(rows, r0 + step)
                k.dma(wb[n][r0:r1, :], wf[n][r0:r1, :], w=[wtrks[n]], own=wtrks[n], q="pool", accum_w=True)

        def rstd_of(P, src, W, junk, extra=1.0):
            sbuf, sap = src
            ss = P["ss"].next()
            k.pool(lambda e: e.memset(ss[:], 0.0), w=[ss])
            k.act(lambda e: e.activation(out=junk[:, 0:W], in_=sap, func=AF.Square, accum_out=ss[:, 0:1]),
                  r=[sbuf, ss], w=[junk, ss])
            rs = P["rs"].next()
            ex2 = float(extra) ** 2
            k.act(lambda e: e.activation(out=rs[:], in_=ss[:], func=AF.Sqrt, scale=1.0 / (W * ex2), bias=EPS / ex2),
                  r=[ss], w=[rs])
            k.dve(lambda e: e.reciprocal(out=rs[:], in_=rs[:]), r=[rs], w=[rs])
            return rs

        def transpose_to(P, nbf, ncols, dstT, t, gidx, alt=[0]):
            nch = ncols // 128
            for c0 in range(0, nch, 8):
                c1 = min(nch, c0 + 8)
                bank = pbr.next()
                for c in range(c0, c1):
                    k.pe(lambda e, c=c: e.transpose(bank[:, (c - c0) * 128:(c - c0 + 1) * 128],
                                                    nbf[:, c * 128:(c + 1) * 128], IDB()),
                         r=[nbf, cst_b], w=[bank])
                if gidx is None:
                    nn = c1 - c0
                    k.dve(lambda e: e.tensor_copy(out=dstT[:, c0:c1, t * 128:(t + 1) * 128],
                                                  in_=bank[:, 0:nn * 128].rearrange("p (c q) -> p c q", q=128)),
                          r=[bank], w=[dstT])
                    continue
                for c in range(c0, c1):
                    src = bank[:, (c - c0) * 128:(c - c0 + 1) * 128]
                    dst = dstT[:, c, t * 128:(t + 1) * 128]
                    if gidx is None:
                        fn = lambda e, src=src, dst=dst: e.tensor_copy(out=dst, in_=src)
                        rr = [bank]
                    else:
                        gbuf, g0 = gidx
                        gap = gbuf[:, g0 + c:g0 + c + 1] if len(gbuf.t.shape) == 2 else gbuf[:, g0, c:c + 1]
                        rr = [bank, gbuf]
                        if alt[0] % 2 == 0:
                            fn = lambda e, src=src, dst=dst, gap=gap: e.tensor_scalar(
                                out=dst, in0=src, scalar1=gap, scalar2=None, op0=ALU.mult)
                        else:
                            fn = lambda e, src=src, dst=dst, gap=gap: e.activation(
                                out=dst, in_=src, func=AF.Copy, scale=gap)
                    if gidx is not None and alt[0] % 2 == 1:
                        k.act(fn, r=rr, w=[dstT])
                    else:
                        k.dve(fn, r=rr, w=[dstT])
                alt[0] += 1

        def load_w(P, wd, r0, nkc, c0, ncols, coff=0, slot=None):
            if slot is None:
                slot = P["wring"].next()
            src = wd[r0 * 128:(r0 + nkc) * 128, c0:c0 + ncols].rearrange("(c p) f -> p c f", p=128)
            k.dma(slot[:, 0:nkc, coff:coff + ncols], src, r=[wtrks[wkey[id(wd)]]], w=[slot], own=slot, accum_w=(coff != 0))
            return slot

        def load_gain(P, gname, slot):
            gt = P["gBs"][slot]
            if P["gcur"].get(slot) != gname:
                k.dma(gt[:], gbd[gname], w=[gt])
                P["gcur"][slot] = gname
            return gt

        def norm_transpose_block(P, load_tile, gain_name, nT, keep=None):
            gt = load_gain(P, gain_name, 1)
            nbs = {}

            def s1(t):
                nb = P["nbf"].next()
                xt = load_tile(t, nb)
                rs = rstd_of(P, (xt, xt[:, :]), D, nb)
                k.dve(lambda e: e.scalar_tensor_tensor(out=nb[:], in0=xt[:], scalar=rs[:, 0:1], in1=gt[:],
                                                       op0=ALU.mult, op1=ALU.mult), r=[xt, rs, gt], w=[nb])
                nbs[t] = nb
            s1(0)
            for t in range(TPB):
                if t + 1 < TPB:
                    s1(t + 1)
                transpose_to(P, nbs[t], D, nT, t, None)

        def ffn(P, nT, wg, wu, wd, ysb):
            hT = P["hT"]
            for g in range(FC // 2):
                sgu = load_w(P, wg, 0, KC, g * 256, 256)
                load_w(P, wu, 0, KC, g * 256, 256, coff=256, slot=sgu)
                for f in range(2):
                    pg, pu = pfr.next(), pfr.next()
                    for kc in range(KC):
                        k.pe(lambda e, kc=kc: e.matmul(pg[:, 0:TB], lhsT=sgu[:, kc, f * 128:(f + 1) * 128],
                                                       rhs=nT[:, kc, :], start=(kc == 0), stop=(kc == KC - 1)),
                             r=[sgu, nT], w=[pg])
                    for kc in range(KC):
                        k.pe(lambda e, kc=kc: e.matmul(pu[:, 0:TB], lhsT=sgu[:, kc, 256 + f * 128:256 + (f + 1) * 128],
                                                       rhs=nT[:, kc, :], start=(kc == 0), stop=(kc == KC - 1)),
                             r=[sgu, nT], w=[pu])
                    sl = P["silu"].next()
                    k.act(lambda e: e.activation(out=sl[:, 0:TB], in_=pg[:, 0:TB], func=AF.Silu), r=[pg], w=[sl])
                    fc = g * 2 + f
                    k.dve(lambda e: e.tensor_tensor(out=hT[:, fc, :], in0=sl[:, 0:TB], in1=pu[:, 0:TB], op=ALU.mult),
                          r=[sl, pu], w=[hT])
            for dg in range(4):
                accs = [pf[i] for i in range(TPB)]
                for fg in range(4):
                    sd = load_w(P, wd, fg * 11, 11, dg * 512, 512)
                    for t in range(TPB):
                        for f in range(11):
                            fc = fg * 11 + f
                            k.pe(lambda e, t=t, f=f, fc=fc: e.matmul(
                                accs[t][:, :], lhsT=hT[:, fc, t * 128:(t + 1) * 128], rhs=sd[:, f, :],
                                start=(fc == 0), stop=(fc == FC - 1)), r=[hT, sd], w=[accs[t]])
                for t in range(TPB):
                    k.act(lambda e, t=t: e.activation(out=ysb[t][:, dg * 512:(dg + 1) * 512], in_=accs[t][:, :],
                                                      func=AF.Copy), r=[accs[t]], w=[ysb[t]])
            pfr.i = 0

        def post_residual(P, ysrc, gname, base, out, half, junk):
            gB = load_gain(P, gname, 0)
            rs = rstd_of(P, (ysrc, ysrc[:, :]), D, junk, extra=(0.5 if half else 1.0))
            k.dve(lambda e: e.scalar_tensor_tensor(out=ysrc[:], in0=ysrc[:], scalar=rs[:, 0:1], in1=gB[:],
                                                   op0=ALU.mult, op1=ALU.mult), r=[ysrc, rs, gB], w=[ysrc])
            k.pool(lambda e: e.tensor_tensor(out=out[:], in0=base[:], in1=ysrc[:], op=ALU.add),
                   r=[base, ysrc], w=[out])

        for sq in range(NSEQ):
            with contextlib.ExitStack() as st:
                A = Pools(k, st)
                P = {"ss": A.ring(4, [128, 1], F32), "rs": A.ring(4, [128, 1], F32),
                     "junk": A.sb([128, 512], BF16), "nbf": A.ring(2, [128, D], BF16),
                     "wring": A.ring(3, [128, KC, 512], BF16), "hT": A.sb([128, FC, TB], BF16),
                     "silu": A.ring(1, [128, 512], F32), "gBs": [A.sb([128, D], F32) for _ in range(2)], "gcur": {}}
                nT = A.sb([128, KC, TB], BF16)
                xr = A.ring(2, [128, D], F32)
                ysb = [A.sb([128, D], F32) for _ in range(TPB)]
                cs = A.sb([128, 2, TPB, 32], F32)
                gdnp = A.sb([128, 2, 16], F32)
                k.dma(gdnp[:], gdnp_d, w=[gdnp])
                negA = A.sb([128, 16], F32)
                k.act(lambda e: e.activation(out=negA[:], in_=gdnp[:, 0, :], func=AF.Exp), r=[gdnp], w=[negA])
                k.dve(lambda e: e.tensor_scalar(out=negA[:], in0=negA[:], scalar1=-1.0, scalar2=None, op0=ALU.mult),
                      r=[negA], w=[negA])
                cqT = A.sb([128, 4, TB], BF16)
                ckT = A.sb([128, 2, TB], BF16)
                krT = A.sb([64, TB], BF16)
                gbs = A.sb([128, TPB, 32], F32)
                stg = A.ring(1, [128, 2, TB], F32)
                small = A.ring(1, [128, 512], F32)
                smallb = A.ring(2, [128, 512], BF16)
                tiny = A.ring(5, [128, 64], F32)
                for blk in range(NB):
                    t0 = blk * TB
                    k.dma(cs[:], rope_d[:, :, blk * TPB:(blk + 1) * TPB, :], w=[cs])

                    def load_x(t, junk=None):
                        xt = xr.next()
                        k.dma(xt[:], x_d[sq, t0 + t * 128:t0 + (t + 1) * 128, :], w=[xt])
                        return xt
                    norm_transpose_block(P, load_x, "ffn1_pre_g", nT)
                    ffn(P, nT, wb["ffn1_w_gate"], wb["ffn1_w_up"], wb["ffn1_w_down"], ysb)
                    h1t = {}

                    def load_h1(t, junk):
                        xt = load_x(t)
                        post_residual(P, ysb[t], "ffn1_post_g", xt, xt, True, junk)
                        k.dma(h1_d[t0 + t * 128:t0 + (t + 1) * 128, :], xt[:], r=[xt], w=[dt_("h1%d" % (blk))],
                              own=xt, accum_w=True, q="pool")
                        return xt
                    norm_transpose_block(P, load_h1, "mix_pre_g", nT)
                    s0 = load_w(P, wb["w_in"], 0, KC, 0, 512)
                    for t in range(TPB):
                        acc = pfr.next()
                        for kc in range(KC):
                            k.pe(lambda e, kc=kc: e.matmul(acc[:, :], lhsT=nT[:, kc, t * 128:(t + 1) * 128],
                                                           rhs=s0[:, kc, :], start=(kc == 0), stop=(kc == KC - 1)),
                                 r=[nT, s0], w=[acc])
                        cq = small.next()
                        k.act(lambda e: e.activation(out=cq[:], in_=acc[:, :], func=AF.Copy), r=[acc], w=[cq])
                        rs = rstd_of(P, (cq, cq[:, :]), 512, P["junk"])
                        cqn = smallb.next()
                        k.act(lambda e: e.activation(out=cqn[:], in_=cq[:], func=AF.Copy, scale=rs[:, 0:1]),
                              r=[cq, rs], w=[cqn])
                        transpose_to(P, cqn, 512, cqT, t, (qkg, 0))
                    k.dma(cqnT_d[:, :, t0:t0 + TB], cqT[:], r=[cqT], w=[dt_("cq%d" % blk)], own=cqT, q="pool")
                    s1 = load_w(P, wb["w_in"], 0, KC, 512, 320)
                    load_w(P, wb["w_in"], 0, KC, 4928, 32, coff=320, slot=s1)
                    for t in range(TPB):
                        acc = pfr.next()
                        for kc in range(KC):
                            k.pe(lambda e, kc=kc: e.matmul(acc[:, 0:352], lhsT=nT[:, kc, t * 128:(t + 1) * 128],
                                                           rhs=s1[:, kc, 0:352], start=(kc == 0), stop=(kc == KC - 1)),
                                 r=[nT, s1], w=[acc])
                        ck = small.next()
                        k.act(lambda e: e.activation(out=ck[:, 0:352], in_=acc[:, 0:352], func=AF.Copy),
                              r=[acc], w=[ck])
                        rs = rstd_of(P, (ck, ck[:, 0:256]), 256, P["junk"])
                        ckn = smallb.next()
                        k.act(lambda e: e.activation(out=ckn[:, 0:256], in_=ck[:, 0:256], func=AF.Copy,
                                                     scale=rs[:, 0:1]), r=[ck, rs], w=[ckn])
                        transpose_to(P, ckn, 256, ckT, t, (qkg, 4))
                        cos, sin = cs[:, 0, t, :], cs[:, 1, t, :]
                        ta, tb_ = tiny.next(), tiny.next()
                        x1, x2 = ck[:, 256:288], ck[:, 288:320]
                        k.dve(lambda e: e.tensor_tensor(out=ta[:, 0:32], in0=x1, in1=cos, op=ALU.mult), r=[ck, cs], w=[ta])
                        k.dve(lambda e: e.tensor_tensor(out=ta[:, 32:64], in0=x2, in1=cos, op=ALU.mult), r=[ck, cs], w=[ta])
                        k.pool(lambda e: e.tensor_tensor(out=tb_[:, 0:32], in0=x2, in1=sin, op=ALU.mult), r=[ck, cs], w=[tb_])
                        k.pool(lambda e: e.tensor_tensor(out=tb_[:, 32:64], in0=x1, in1=sin, op=ALU.mult), r=[ck, cs], w=[tb_])
                        krb = smallb.next()
                        k.dve(lambda e: e.tensor_tensor(out=krb[:, 0:32], in0=ta[:, 0:32], in1=tb_[:, 0:32],
                                                        op=ALU.subtract), r=[ta, tb_], w=[krb])
                        k.dve(lambda e: e.tensor_tensor(out=krb[:, 32:64], in0=ta[:, 32:64], in1=tb_[:, 32:64],
                                                        op=ALU.add), r=[ta, tb_], w=[krb])
                        bank = pbr.next()
                        k.pe(lambda e: e.transpose(bank[0:64, 0:128], krb[:, 0:64], IDB()), r=[krb, cst_b], w=[bank])
                        k.dve(lambda e: e.tensor_copy(out=krT[:, t * 128:(t + 1) * 128], in_=bank[0:64, 0:128]),
                              r=[bank], w=[krT])
                        a_, b_ = ck[:, 320:336], ck[:, 336:352]
                        u0, u1, u2 = tiny.next(), tiny.next(), tiny.next()
                        k.dve(lambda e: e.tensor_tensor(out=u0[:, 0:16], in0=a_, in1=gdnp[:, 1, :], op=ALU.add),
                              r=[ck, gdnp], w=[u0])
                        k.dve(lambda e: e.tensor_scalar(out=u1[:, 0:16], in0=u0[:, 0:16], scalar1=-1.0, scalar2=None,
                                                        op0=ALU.mult), r=[u0], w=[u1])
                        k.dve(lambda e: e.tensor_tensor(out=u1[:, 0:16], in0=u0[:, 0:16], in1=u1[:, 0:16], op=ALU.min),
                              r=[u0, u1], w=[u1])
                        k.act(lambda e: e.activation(out=u1[:, 0:16], in_=u1[:, 0:16], func=AF.Exp),
                              r=[u1], w=[u1])
                        k.act(lambda e: e.activation(out=u1[:, 0:16], in_=u1[:, 0:16], func=AF.Ln, bias=1.0),
                              r=[u1], w=[u1])
                        k.dve(lambda e: e.scalar_tensor_tensor(out=u2[:, 0:16], in0=u0[:, 0:16], scalar=0.0,
                                                               in1=u1[:, 0:16], op0=ALU.max, op1=ALU.add),
                              r=[u0, u1], w=[u2])
                        k.dve(lambda e: e.tensor_tensor(out=gbs[:, t, 0:16], in0=u2[:, 0:16], in1=negA[:], op=ALU.mult),
                              r=[u2, negA], w=[gbs])
                        k.act(lambda e: e.activation(out=gbs[:, t, 16:32], in_=b_, func=AF.Sigmoid), r=[ck], w=[gbs])
                    k.dma(ckvnT_d[:, :, t0:t0 + TB], ckT[:], r=[ckT], w=[dt_("ck%d" % blk)], own=ckT, q="pool")
                    k.dma(krT_d[:, t0:t0 + TB], krT[:], r=[krT], w=[dt_("kr%d" % blk)], own=krT, q="pool")
                    k.dma(gb_d[:, blk * TPB:(blk + 1) * TPB, :], gbs[:], r=[gbs], w=[dt_("gb%d" % blk)], own=gbs, q="pool")
                    for g in range(6):
                        sw = load_w(P, wb["w_in"], 0, KC, 832 + g * 512, 512)
                        for f2 in range(2):
                            sg = stg.next()
                            for ff in range(2):
                                f = f2 * 2 + ff
                                acc = pfr.next()
                                for kc in range(KC):
                                    k.pe(lambda e, kc=kc: e.matmul(acc[:, 0:TB], lhsT=sw[:, kc, f * 128:(f + 1) * 128],
                                                                   rhs=nT[:, kc, :], start=(kc == 0), stop=(kc == KC - 1)),
                                         r=[sw, nT], w=[acc])
                                if ff == 0:
                                    k.act(lambda e: e.activation(out=sg[:, ff, :], in_=acc[:, 0:TB], func=AF.Copy),
                                          r=[acc], w=[sg])
                                else:
                                    k.dve(lambda e: e.tensor_copy(out=sg[:, ff, :], in_=acc[:, 0:TB]), r=[acc], w=[sg])
                            k.dma(qkvT_d[:, g * 4 + f2 * 2:g * 4 + f2 * 2 + 2, t0:t0 + TB], sg[:, 0:2, :], r=[sg],
                                  w=[dt_("qkv%d" % blk)], own=sg, accum_w=True, q="pool")
                    sz = [load_w(P, wb["w_in"], 0, KC, 3904 + g * 512, 512) for g in range(2)]
                    for t in range(TPB):
                        zt = stg.next()
                        for g in range(2):
                            acc = pfr.next()
                            for kc in range(KC):
                                k.pe(lambda e, kc=kc: e.matmul(acc[:, :], lhsT=nT[:, kc, t * 128:(t + 1) * 128],
                                                               rhs=sz[g][:, kc, :], start=(kc == 0), stop=(kc == KC - 1)),
                                     r=[nT, sz[g]], w=[acc])
                            k.act(lambda e: e.activation(out=zt[:, g, :], in_=acc[:, :], func=AF.Silu),
                                  r=[acc], w=[zt])
                        k.dma(zs_d[t0 + t * 128:t0 + (t + 1) * 128, :].rearrange("s (g c) -> s g c", g=2), zt[:, 0:2, :],
                              r=[zt], w=[dt_("zs%d" % blk)],
                              own=zt, accum_w=True, q="pool")
            k.barrier()
            if dbg == "A":
                break
            with contextlib.ExitStack() as st:
                B = Pools(k, st)
                cqT = B.sb([128, 4, S], BF16)
                ckT = B.sb([128, 2, S], BF16)
                krT = B.sb([64, S], BF16)
                rA = [dt_("cq%d" % b) for b in range(NB)] + [dt_("ck%d" % b) for b in range(NB)] + \
                     [dt_("kr%d" % b) for b in range(NB)]
                k.dma(cqT[:], cqnT_d, r=rA, w=[cqT])
                k.dma(ckT[:], ckvnT_d, r=rA, w=[ckT])
                k.dma(krT[:], krT_d, r=rA, w=[krT])
                wuq = B.sb([128, 4, 8 * 192], BF16)
                wukv = B.sb([128, 2, 8 * 256], BF16)
                k.dma(wuq[:], wb["w_uq"].rearrange("(c p) f -> p c f", p=128), r=[wtrks["w_uq"]], w=[wuq])
                k.dma(wukv[:], wb["w_ukv"].rearrange("(c p) f -> p c f", p=128), r=[wtrks["w_ukv"]], w=[wukv])
                cs = B.sb([128, 2, NT, 32], F32)
                k.dma(cs[:], rope_d, w=[cs])
                KT = B.sb([128, S], BF16)
                QnT = B.sb([128, S], BF16)
                QrT = B.sb([64, S], BF16)
                Vh = B.sb([128, NT * 128], BF16)
                OT = B.sb([128, S], BF16)
                Pr = B.ring(3, [128, QB], BF16)
                rinv = B.ring(2, [128, QB], F32)
                qra = B.ring(2, [128, 8, 64], F32)
                qrb = B.ring(2, [128, 8, 64], F32)
                qrbf = B.ring(2, [128, 8, 64], BF16)
                scale = float((NOPE + ROPE) ** -0.5)
                G8 = min(8, NT)
                for h in range(8):
                    if dbg in ("B0", "C", "C0", "C1", "D0", "D1", "CD") or "noB" in flags:
                        break
                    for blk in range(NQB):
                        sl = slice(blk * QB, (blk + 1) * QB)
                        acc = pfr.next()
                        for c in range(2):
                            k.pe(lambda e, c=c: e.matmul(acc[:, 0:QB], lhsT=wukv[:, c, h * 256:h * 256 + 128],
                                                         rhs=ckT[:, c, sl], start=(c == 0), stop=(c == 1)),
                                 r=[wukv, ckT], w=[acc])
                        k.act(lambda e: e.activation(out=KT[:, sl], in_=acc[:, 0:QB], func=AF.Copy), r=[acc], w=[KT])
                        acc2 = pfr.next()
                        for c in range(4):
                            k.pe(lambda e, c=c: e.matmul(acc2[:, 0:QB], lhsT=wuq[:, c, h * 192:h * 192 + 128],
                                                         rhs=cqT[:, c, sl], start=(c == 0), stop=(c == 3)),
                                 r=[wuq, cqT], w=[acc2])
                        k.dve(lambda e: e.tensor_copy(out=QnT[:, sl], in_=acc2[:, 0:QB]), r=[acc2], w=[QnT])
                    for tg in range(0, NT, 4):
                        acc = pfr.next()
                        for t in range(tg, min(NT, tg + 4)):
                            for c in range(2):
                                k.pe(lambda e, c=c, t=t: e.matmul(
                                    acc[:, (t - tg) * 128:(t - tg + 1) * 128], lhsT=ckT[:, c, t * 128:(t + 1) * 128],
                                    rhs=wukv[:, c, h * 256 + 128:(h + 1) * 256], start=(c == 0), stop=(c == 1)),
                                    r=[wukv, ckT], w=[acc])
                        n4 = min(NT, tg + 4) - tg
                        k.act(lambda e: e.activation(out=Vh[:, tg * 128:(tg + n4) * 128], in_=acc[:, 0:n4 * 128],
                                                     func=AF.Copy), r=[acc], w=[Vh])
                    if dbg == "B1":
                        break
                    for tg in range(0, NT, G8):
                        acc = pfr.next()
                        for t in range(tg, tg + G8):
                            for c in range(4):
                                k.pe(lambda e, c=c, t=t: e.matmul(
                                    acc[:, (t - tg) * 64:(t - tg + 1) * 64], lhsT=cqT[:, c, t * 128:(t + 1) * 128],
                                    rhs=wuq[:, c, h * 192 + 128:(h + 1) * 192], start=(c == 0), stop=(c == 3)),
                                    r=[wuq, cqT], w=[acc])
                        av = acc[:, 0:G8 * 64].rearrange("p (t r) -> p t r", r=64)
                        cos, sin = cs[:, 0, tg:tg + G8, :], cs[:, 1, tg:tg + G8, :]
                        ta, tb_, qb_ = qra.next(), qrb.next(), qrbf.next()
                        k.dve(lambda e: e.tensor_tensor(out=ta[:, 0:G8, 0:32], in0=av[:, :, 0:32], in1=cos, op=ALU.mult),
                              r=[acc, cs], w=[ta])
                        k.dve(lambda e: e.tensor_tensor(out=ta[:, 0:G8, 32:64], in0=av[:, :, 32:64], in1=cos, op=ALU.mult),
                              r=[acc, cs], w=[ta])
                        k.dve(lambda e: e.tensor_tensor(out=tb_[:, 0:G8, 0:32], in0=av[:, :, 32:64], in1=sin, op=ALU.mult),
                              r=[acc, cs], w=[tb_])
                        k.dve(lambda e: e.tensor_tensor(out=tb_[:, 0:G8, 32:64], in0=av[:, :, 0:32], in1=sin, op=ALU.mult),
                              r=[acc, cs], w=[tb_])
                        k.pool(lambda e: e.tensor_tensor(out=qb_[:, 0:G8, 0:32], in0=ta[:, 0:G8, 0:32],
                                                         in1=tb_[:, 0:G8, 0:32], op=ALU.subtract), r=[ta, tb_], w=[qb_])
                        k.pool(lambda e: e.tensor_tensor(out=qb_[:, 0:G8, 32:64], in0=ta[:, 0:G8, 32:64],
                                                         in1=tb_[:, 0:G8, 32:64], op=ALU.add), r=[ta, tb_], w=[qb_])
                        bank = pbr.next()
                        for t in range(G8):
                            k.pe(lambda e, t=t: e.transpose(bank[0:64, t * 128:(t + 1) * 128], qb_[:, t, :], IDB()),
                                 r=[qb_, cst_b], w=[bank])
                        k.act(lambda e: e.activation(out=QrT[:, tg * 128:(tg + G8) * 128], in_=bank[0:64, 0:G8 * 128],
                                                     func=AF.Copy), r=[bank], w=[QrT])
                    if dbg == "B2":
                        break
                    for qb in range(NQB):
                        qs = slice(qb * QB, (qb + 1) * QB)
                        accO, accR = pf[(qb % 2) * 2], pf[(qb % 2) * 2 + 1]
                        sps = [pf[4], pf[5]]

                        def qk(kt):
                            sp_ = sps[kt % 2]
                            k.pe(lambda e: e.matmul(sp_[:, 0:QB], lhsT=KT[:, kt * 128:(kt + 1) * 128], rhs=QnT[:, qs],
                                                    start=True, stop=False), r=[KT, QnT], w=[sp_])
                            k.pe(lambda e: e.matmul(sp_[:, 0:QB], lhsT=krT[:, kt * 128:(kt + 1) * 128], rhs=QrT[:, qs],
                                                    start=False, stop=True), r=[krT, QrT], w=[sp_])
                        qk(0)
                        for kt in range(NT):
                            if kt + 1 < NT:
                                qk(kt + 1)
                            sp_ = sps[kt % 2]
                            p_ = Pr.next()
                            k.act(lambda e: e.activation(out=p_[:], in_=sp_[:, 0:QB], func=AF.Exp, scale=scale),
                                  r=[sp_], w=[p_])
                            k.pe(lambda e: e.matmul(accO[:, 0:QB], lhsT=Vh[:, kt * 128:(kt + 1) * 128], rhs=p_[:], start=(kt == 0),
                                                    stop=(kt == NT - 1)), r=[Vh, p_], w=[accO])
                            k.pe(lambda e: e.matmul(accR[:, 0:QB], lhsT=ONESB, rhs=p_[:], start=(kt == 0),
                                                    stop=(kt == NT - 1)), r=[cst_b, p_], w=[accR])
                        ri = rinv.next()
                        k.dve(lambda e: e.reciprocal(out=ri[:], in_=accR[:, 0:QB]), r=[accR], w=[ri])
                        k.dve(lambda e: e.tensor_tensor(out=OT[:, qs], in0=accO[:, 0:QB], in1=ri[:], op=ALU.mult),
                              r=[accO, ri], w=[OT])
                    k.dma(omixT_d[:, h, :], OT[:], r=[OT], w=[dt_("omla")], own=OT, accum_w=True, q="pool")
                    pfr.i = 0
            k.barrier()
            if dbg in ("B", "B0", "B1", "B2"):
                break
            with contextlib.ExitStack() as st:
                C = Pools(k, st)
                try:
                    P = {"ss": C.ring(4, [128, 1], F32), "rs": C.ring(4, [128, 1], F32), "junk": C.sb([128, 128], BF16)}
                    gon = C.sb([128, 128], F32)
                    k.dma(gon[:], gon_d, w=[gon])
                    cw = C.sb([128, 24, 5], F32)
                    k.dma(cw[:], cw_d, w=[cw])
                    if cstop == 1:
                        raise _StopD()
                    W16 = NT * 16
                    H8 = NT * 8
                    gcs, eg, egs, ek, dec, bgc, nbeta, gq, bq, grem = (C.sb([128, W16], F32) for _ in range(10))
                    DKS = float(128 ** -0.5)
                    for d_ in range(2):
                        k.dma(gq[:, d_ * H8:(d_ + 1) * H8].rearrange("p (t n) -> p t n", n=8), gb_d[:, :, d_ * 8:(d_ + 1) * 8],
                              r=[dt_("gb%d" % b_) for b_ in range(NB)], w=[gq], own=gq, accum_w=(d_ == 1))
                        k.dma(bq[:, d_ * H8:(d_ + 1) * H8].rearrange("p (t n) -> p t n", n=8), gb_d[:, :, 16 + d_ * 8:16 + (d_ + 1) * 8],
                              r=[dt_("gb%d" % b_) for b_ in range(NB)], w=[bq], own=bq, accum_w=(d_ == 1))
                    if cstop == 2:
                        raise _StopD()
                    gpb = [C.sb([128, W16], BF16) for _ in range(3)]
                    gpf = [C.sb([128, W16], F32) for _ in range(3)]
                    k.dve(lambda e: e.tensor_copy(out=grem[:], in_=gq[:]), r=[gq], w=[grem])
                    for i3 in range(3):
                        k.dve(lambda e, i3=i3: e.tensor_copy(out=gpb[i3][:], in_=grem[:]), r=[grem], w=[gpb[i3]])
                        k.dve(lambda e, i3=i3: e.tensor_copy(out=gpf[i3][:], in_=gpb[i3][:]), r=[gpb[i3]], w=[gpf[i3]])
                        if i3 < 2:
                            k.dve(lambda e, i3=i3: e.tensor_tensor(out=grem[:], in0=grem[:], in1=gpf[i3][:], op=ALU.subtract),
                                  r=[grem, gpf[i3]], w=[grem])
                    if cstop == 3:
                        raise _StopD()
                    UPB, LOWB = cst_b[:, 4, :], cst_b[:, 2, :]
                    psA_, psT_ = pfr.next(), pfr.next()
                    for i3 in range(3):
                        k.pe(lambda e, i3=i3: e.matmul(psA_[:, 0:H8], lhsT=UPB, rhs=gpb[i3][:, 0:H8], start=(i3 == 0), stop=(i3 == 2)),
                             r=[cst_b, gpb[i3]], w=[psA_])
                    for i3 in range(3):
                        k.pe(lambda e, i3=i3: e.matmul(psA_[:, H8:W16], lhsT=LOWB, rhs=gpb[i3][:, H8:W16], start=(i3 == 0), stop=(i3 == 2)),
                             r=[cst_b, gpb[i3]], w=[psA_])
                    for i3 in range(3):
                        k.pe(lambda e, i3=i3: e.matmul(psT_[:, 0:W16], lhsT=ONESB, rhs=gpb[i3][:, :], start=(i3 == 0), stop=(i3 == 2)),
                             r=[cst_b, gpb[i3]], w=[psT_])
                    if cstop == 4:
                        raise _StopD()
                    k.act(lambda e: e.activation(out=gcs[:], in_=psA_[:, 0:W16], func=AF.Copy), r=[psA_], w=[gcs])
                    k.act(lambda e: e.activation(out=eg[:], in_=psA_[:, 0:W16], func=AF.Exp), r=[psA_], w=[eg])
                    k.act(lambda e: e.activation(out=dec[:], in_=psT_[:, 0:W16], func=AF.Exp), r=[psT_], w=[dec])
                    if cstop == 41:
                        raise _StopD()
                    k.dve(lambda e: e.tensor_tensor(out=ek[:], in0=psT_[:, 0:W16], in1=gcs[:], op=ALU.subtract), r=[psT_, gcs, dec], w=[ek])
                    if cstop == 411:
                        raise _StopD()
                    k.dve(lambda e: e.tensor_tensor(out=bgc[:], in0=bq[:], in1=eg[:], op=ALU.mult), r=[bq, eg], w=[bgc])
                    if cstop == 412:
                        raise _StopD()
                    k.dve(lambda e: e.tensor_scalar(out=nbeta[:], in0=bq[:], scalar1=-1.0, scalar2=None, op0=ALU.mult), r=[bq], w=[nbeta])
                    if cstop == 42:
                        raise _StopD()
                    k.act(lambda e: e.activation(out=ek[:], in_=ek[:], func=AF.Exp), r=[ek], w=[ek])
                    k.dve(lambda e: e.tensor_scalar(out=egs[:], in0=eg[:], scalar1=DKS, scalar2=None, op0=ALU.mult),
                          r=[eg], w=[egs])
                    if cstop == 5:
                        raise _StopD()
                    raw = C.sb([128, S + 4], F32)
                    cacc = C.sb([128, S], F32)
                    sil = cacc
                    sqb = C.sb([128, S], BF16)
                    qT, kT, vT = (C.sb([128, S], BF16) for _ in range(3))
                    o_d = [C.sb([128, S], F32) for _ in range(2)]
                    z_h = C.sb([128, NT, 128], F32)
                    ogT = C.sb([128, S], BF16)
                    rnr = C.ring(2, [128, QB], F32)
                    S32s = [C.sb([128, 128], F32) for _ in range(2)]
                    Sbfs = [C.sb([128, 128], BF16) for _ in range(2)]
                    f128 = C.ring(16, [128, 128], F32)
                    b128 = C.ring(100, [128, 128], BF16)
                    k.dve(lambda e: e.memset(raw[:, 0:4], 0.0), w=[raw])
                    k.dve(lambda e: e.memset(raw[:, S:S + 4], 0.0), w=[raw])
                    qkv_r = [dt_("qkv%d" % b) for b in range(NB)]
                    zs_r = [dt_("zs%d" % b) for b in range(NB)]
                    for h in range(8):
                        if dbg in ("C0", "D0", "D1", "BD") or "noC" in flags:
                            break
                        for which, dst in ((0, qT), (1, kT), (2, vT)):
                            ch = which * 8 + h
                            k.dma(raw[:, 2:S + 2], qkvT_d[:, ch, :], r=qkv_r, w=[raw])
                            k.dve(lambda e: e.tensor_scalar(out=cacc[:], in0=raw[:, 0:S], scalar1=cw[:, ch, 0:1], scalar2=None,
                                                            op0=ALU.mult), r=[raw, cw], w=[cacc])
                            for j in range(1, 5):
                                k.dve(lambda e, j=j: e.scalar_tensor_tensor(out=cacc[:], in0=raw[:, j:j + S],
                                                                            scalar=cw[:, ch, j:j + 1], in1=cacc[:],
                                                                            op0=ALU.mult, op1=ALU.add), r=[raw, cw, cacc], w=[cacc])
                            if which == 2:
                                k.act(lambda e: e.activation(out=dst[:], in_=cacc[:], func=AF.Silu), r=[cacc], w=[dst])
                                continue
                            k.act(lambda e: e.activation(out=sil[:], in_=cacc[:], func=AF.Silu), r=[cacc], w=[sil])
                            k.act(lambda e: e.activation(out=sqb[:], in_=sil[:], func=AF.Square), r=[sil], w=[sqb])
                            for blk in range(NQB):
                                sl = slice(blk * QB, (blk + 1) * QB)
                                ps = pfr.next()
                                k.pe(lambda e: e.matmul(ps[:, 0:QB], lhsT=ONESB, rhs=sqb[:, sl], start=True, stop=True),
                                     r=[cst_b, sqb], w=[ps])
                                rn = rnr.next()
                                k.act(lambda e: e.activation(out=rn[:], in_=ps[:, 0:QB], func=AF.Sqrt, bias=EPS), r=[ps], w=[rn])
                                k.dve(lambda e: e.reciprocal(out=rn[:], in_=rn[:]), r=[rn], w=[rn])
                                k.dve(lambda e: e.tensor_tensor(out=dst[:, sl], in0=sil[:, sl], in1=rn[:], op=ALU.mult),
                                      r=[sil, rn], w=[dst])
                        if dbg == "C1":
                            break
                        k.dma(z_h[:], zs_d[:, h * 128:(h + 1) * 128].rearrange("(t p) v -> p t v", p=128), r=zs_r, w=[z_h])
                        def unit_gen(dr):
                            TRI = UP if dr == 0 else LOW
                            SM = SLOW if dr == 0 else SUP
                            IMT = UP if dr == 0 else LOW
                            S32, Sbf = S32s[dr], Sbfs[dr]
                            k.pool(lambda e: e.memset(S32[:], 0.0), w=[S32])
                            k.pool(lambda e: e.memset(Sbf[:], 0.0), w=[Sbf])
                            od = o_d[dr]
                            for c in (range(NT) if dr == 0 else range(NT - 1, -1, -1)):
                                cs_ = slice(c * 128, (c + 1) * 128)
                                ci = dr * H8 + c * 8 + h
                                bcol = bq[:, ci:ci + 1]
                                psg = pfr.next()
                                for i3 in range(3):
                                    gT = b128.next()
                                    k.dve(lambda e, i3=i3: e.tensor_scalar(out=gT[:], in0=TRI, scalar1=gpf[i3][:, ci:ci + 1],
                                                                           scalar2=None, op0=ALU.mult), r=[cst_f, gpf[i3]], w=[gT])
                                    k.pe(lambda e, i3=i3: e.matmul(psg[:, 0:128], lhsT=ONESB, rhs=gT[:], start=(i3 == 0), stop=(i3 == 2)),
                                         r=[cst_b, gT], w=[psg])
                                yield
                                dm, dtm = f128.next(), f128.next()
                                k.dve(lambda e: e.tensor_scalar(out=dm[:], in0=psg[:, 0:128], scalar1=gcs[:, ci:ci + 1], scalar2=0.0,
                                                                op0=ALU.subtract, op1=ALU.max), r=[psg, gcs], w=[dm])
                                k.dve(lambda e: e.tensor_scalar(out=dtm[:], in0=psg[:, 0:128], scalar1=gcs[:, ci:ci + 1], scalar2=0.0,
                                                                op0=ALU.subtract, op1=ALU.min), r=[psg, gcs], w=[dtm])
                                k.act(lambda e: e.activation(out=dm[:], in_=dm[:], func=AF.Exp, scale=-1.0), r=[dm], w=[dm])
                                k.act(lambda e: e.activation(out=dtm[:], in_=dtm[:], func=AF.Exp), r=[dtm], w=[dtm])
                                k.pool(lambda e: e.tensor_tensor(out=dm[:], in0=dm[:], in1=SM, op=ALU.mult), r=[dm, cst_f], w=[dm])
                                k.pool(lambda e: e.tensor_tensor(out=dtm[:], in0=dtm[:], in1=IMT, op=ALU.mult), r=[dtm, cst_f], w=[dtm])
                                psG = pfr.next()
                                k.pe(lambda e: e.matmul(psG[:, 0:128], lhsT=kT[:, cs_], rhs=kT[:, cs_], start=True, stop=True),
                                     r=[kT], w=[psG])
                                psK = pfr.next()
                                k.pe(lambda e: e.matmul(psK[:, 0:128], lhsT=kT[:, cs_], rhs=qT[:, cs_], start=True, stop=True),
                                     r=[kT, qT], w=[psK])
                                bank2 = pbr.next()
                                k.pe(lambda e: e.transpose(bank2[:, 0:128], kT[:, cs_], IDB()), r=[kT, cst_b], w=[bank2])
                                k.pe(lambda e: e.transpose(bank2[:, 128:256], vT[:, cs_], IDB()), r=[vT, cst_b], w=[bank2])
                                yield
                                Ln = b128.next()
                                k.dve(lambda e: e.scalar_tensor_tensor(out=Ln[:], in0=psG[:, 0:128], scalar=nbeta[:, ci:ci + 1],
                                                                       in1=dm[:], op0=ALU.mult, op1=ALU.mult),
                                      r=[psG, nbeta, dm], w=[Ln])
                                AT = b128.next()
                                k.dve(lambda e: e.scalar_tensor_tensor(out=AT[:], in0=psK[:, 0:128], scalar=DKS, in1=dtm[:],
                                                                       op0=ALU.mult, op1=ALU.mult), r=[psK, dtm], w=[AT])
                                kbg, kd, vb = b128.next(), b128.next(), b128.next()
                                k.act(lambda e: e.activation(out=kbg[:], in_=bank2[:, 0:128], func=AF.Copy, scale=bgc[:, ci:ci + 1]),
                                      r=[bank2, bgc], w=[kbg])
                                k.act(lambda e: e.activation(out=kd[:], in_=bank2[:, 0:128], func=AF.Copy, scale=ek[:, ci:ci + 1]),
                                      r=[bank2, ek], w=[kd])
                                k.act(lambda e: e.activation(out=vb[:], in_=bank2[:, 128:256], func=AF.Copy, scale=bcol),
                                      r=[bank2, bq], w=[vb])
                                bank = pbr.next()
                                k.pe(lambda e: e.transpose(bank[:, 0:128], Ln[:], IDB()), r=[Ln, cst_b], w=[bank])
                                yield
                                Nk = b128.next()
                                k.act(lambda e: e.activation(out=Nk[:], in_=bank[:, 0:128], func=AF.Copy), r=[bank], w=[Nk])
                                NkT = Ln
                                Pm = b128.next()
                                k.dve(lambda e: e.tensor_tensor(out=Pm[:], in0=Nk[:], in1=cst_b[:, 0, :], op=ALU.add),
                                      r=[Nk, cst_b], w=[Pm])
                                Pt = b128.next()
                                k.pool(lambda e: e.tensor_tensor(out=Pt[:], in0=NkT[:], in1=cst_b[:, 0, :], op=ALU.add),
                                       r=[NkT, cst_b], w=[Pt])
                                for lev in range(1, 7):
                                    psA = pfr.next()
                                    k.pe(lambda e: e.matmul(psA[:, 0:128], lhsT=Nk[:], rhs=NkT[:], start=True, stop=True),
                                         r=[Nk, NkT], w=[psA])
                                    if lev < 6:
                                        psB = pfr.next()
                                        k.pe(lambda e: e.matmul(psB[:, 0:128], lhsT=NkT[:], rhs=Nk[:], start=True, stop=True),
                                             r=[Nk, NkT], w=[psB])
                                    yield
                                    NkT2 = b128.next()
                                    k.act(lambda e: e.activation(out=NkT2[:], in_=psA[:, 0:128], func=AF.Copy), r=[psA], w=[NkT2])
                                    if lev < 6:
                                        Nk2 = b128.next()
                                        k.act(lambda e: e.activation(out=Nk2[:], in_=psB[:, 0:128], func=AF.Copy), r=[psB], w=[Nk2])
                                    else:
                                        Nk2 = None
                                    psC = pfr.next()
                                    k.pe(lambda e: e.matmul(psC[:, 0:128], lhsT=NkT2[:], rhs=Pm[:], start=True, stop=True),
                                         r=[NkT2, Pm], w=[psC])
                                    psD = pfr.next()
                                    k.pe(lambda e: e.matmul(psD[:, 0:128], lhsT=Pm[:], rhs=NkT2[:], start=True, stop=True),
                                         r=[NkT2, Pm], w=[psD])
                                    yield
                                    Pn = b128.next()
                                    k.dve(lambda e: e.tensor_tensor(out=Pn[:], in0=psC[:, 0:128], in1=Pm[:], op=ALU.add),
                                          r=[psC, Pm], w=[Pn])
                                    Ptn = b128.next()
                                    k.dve(lambda e: e.tensor_tensor(out=Ptn[:], in0=psD[:, 0:128], in1=Pt[:], op=ALU.add),
                                          r=[psD, Pt], w=[Ptn])
                                    Pm, Pt, Nk, NkT = Pn, Ptn, Nk2, NkT2
                                psR = pfr.next()
                                k.pe(lambda e: e.matmul(psR[:, 0:128], lhsT=Ln[:], rhs=Pm[:], start=True, stop=True), r=[Ln, Pm], w=[psR])
                                IX = b128.next()
                                k.pool(lambda e: e.tensor_tensor(out=IX[:], in0=cst_b[:, 0, :], in1=Pm[:], op=ALU.subtract),
                                       r=[cst_b, Pm], w=[IX])
                                yield
                                Rr = b128.next()
                                k.dve(lambda e: e.tensor_tensor(out=Rr[:], in0=psR[:, 0:128], in1=IX[:], op=ALU.add),
                                      r=[psR, IX], w=[Rr])
                                psX = pfr.next()
                                k.pe(lambda e: e.matmul(psX[:, 0:128], lhsT=Pt[:], rhs=Rr[:], start=True, stop=True), r=[Pt, Rr], w=[psX])
                                yield
                                TT = b128.next()
                                k.dve(lambda e: e.tensor_tensor(out=TT[:], in0=psX[:, 0:128], in1=Pm[:], op=ALU.add),
                                      r=[psX, Pm], w=[TT])
                                psu = pfr.next()
                                k.pe(lambda e: e.matmul(psu[:, 0:128], lhsT=TT[:], rhs=vb[:], start=True, stop=True), r=[TT, vb], w=[psu])
                                psw = pfr.next()
                                k.pe(lambda e: e.matmul(psw[:, 0:128], lhsT=kbg[:], rhs=TT[:], start=True, stop=True), r=[TT, kbg], w=[psw])
                                yield
                                u = f128.next()
                                k.act(lambda e: e.activation(out=u[:], in_=psu[:, 0:128], func=AF.Copy), r=[psu], w=[u])
                                wT = b128.next()
                                k.act(lambda e: e.activation(out=wT[:], in_=psw[:, 0:128], func=AF.Copy), r=[psw], w=[wT])
                                ps1 = pfr.next()
                                k.pe(lambda e: e.matmul(ps1[:, 0:128], lhsT=wT[:], rhs=Sbf[:], start=True, stop=True), r=[wT, Sbf], w=[ps1])
                                ps2 = pfr.next()
                                k.pe(lambda e: e.matmul(ps2[:, 0:128], lhsT=qT[:, cs_], rhs=Sbf[:], start=True, stop=True), r=[qT, Sbf], w=[ps2])
                                yield
                                vn = b128.next()
                                k.dve(lambda e: e.tensor_tensor(out=vn[:], in0=u[:], in1=ps1[:, 0:128], op=ALU.subtract),
                                      r=[u, ps1], w=[vn])
                                tmp = f128.next()
                                k.act(lambda e: e.activation(out=tmp[:], in_=ps2[:, 0:128], func=AF.Copy, scale=egs[:, ci:ci + 1]),
                                      r=[ps2, egs], w=[tmp])
                                ps3 = pfr.next()
                                k.pe(lambda e: e.matmul(ps3[:, 0:128], lhsT=AT[:], rhs=vn[:], start=True, stop=True), r=[AT, vn], w=[ps3])
                                ps4 = pfr.next()
                                k.pe(lambda e: e.matmul(ps4[:, 0:128], lhsT=kd[:], rhs=vn[:], start=True, stop=True), r=[kd, vn], w=[ps4])
                                yield
                                k.dve(lambda e: e.tensor_tensor(out=od[:, cs_], in0=tmp[:], in1=ps3[:, 0:128], op=ALU.add),
                                      r=[tmp, ps3], w=[od])
                                k.dve(lambda e: e.scalar_tensor_tensor(out=S32[:], in0=S32[:], scalar=dec[:, ci:ci + 1], in1=ps4[:, 0:128],
                                                                       op0=ALU.mult, op1=ALU.add), r=[S32, dec, ps4], w=[S32])
                                k.pool(lambda e: e.tensor_copy(out=Sbf[:], in_=S32[:]), r=[S32], w=[Sbf])
                                yield

                        gens = [unit_gen(0), unit_gen(1)]
                        while gens:
                            for g_ in list(gens):
                                try:
                                    next(g_)
                                except StopIteration:
                                    gens.remove(g_)
                        for c in range(NT):
                            cs_ = slice(c * 128, (c + 1) * 128)
                            k.dve(lambda e: e.tensor_tensor(out=o_d[0][:, cs_], in0=o_d[0][:, cs_], in1=o_d[1][:, cs_], op=ALU.add),
                                   r=[o_d[0], o_d[1]], w=[o_d[0]])
                            rs = rstd_of(P, (o_d[0], o_d[0][:, cs_]), 128, P["junk"])
                            tmp = f128.next()
                            k.dve(lambda e: e.scalar_tensor_tensor(out=tmp[:], in0=o_d[0][:, cs_], scalar=rs[:, 0:1], in1=gon[:],
                                                                   op0=ALU.mult, op1=ALU.mult), r=[o_d[0], rs, gon], w=[tmp])
                            onb = b128.next()
                            k.dve(lambda e: e.tensor_tensor(out=onb[:], in0=tmp[:], in1=z_h[:, c, :], op=ALU.mult),
                                   r=[tmp, z_h], w=[onb])
                            bank = pbr.next()
                            k.pe(lambda e: e.transpose(bank[:, 0:128], onb[:], IDB()), r=[onb, cst_b], w=[bank])
                            k.act(lambda e: e.activation(out=ogT[:, cs_], in_=bank[:, 0:128], func=AF.Copy), r=[bank], w=[ogT])
                        k.dma(omixT_d[:, 8 + h, :], ogT[:], r=[ogT], w=[dt_("ogdn")], own=ogT, accum_w=True, q="pool")
                except _StopD:
                    pass
            k.barrier()
            if dbg in ("C", "C0", "C1"):
                break
            if cstop != 99:
                break
            with contextlib.ExitStack() as st:
                Dp = Pools(k, st)
                P = {"ss": Dp.ring(4, [128, 1], F32), "rs": Dp.ring(4, [128, 1], F32),
                     "junk": Dp.sb([128, 512], BF16), "nbf": Dp.ring(2, [128, D], BF16),
                     "wring": Dp.ring(3, [128, KC, 512], BF16), "hT": Dp.sb([128, FC, TB], BF16),
                     "silu": Dp.ring(1, [128, 512], F32), "gBs": [Dp.sb([128, D], F32) for _ in range(2)], "gcur": {}}
                nT = Dp.sb([128, KC, TB], BF16)
                xr = Dp.ring(2, [128, D], F32)
                ysb = [Dp.sb([128, D], F32) for _ in range(TPB)]
                mT = P["hT"]
                KmT = Dp.sb([128, 4, NMEM], BF16)
                Vm = Dp.sb([128, 2, 512], BF16)
                qmT = Dp.sb([128, 4, TB], BF16)
                omT = Dp.sb([128, 4, TB], BF16)
                Pr = Dp.ring(2, [128, TB], BF16)
                rinv = Dp.ring(1, [128, TB], F32)
                mscale = float(128 ** -0.5)
                TPB_save = TPB

                def load_mem(t):
                    xt = xr.next()
                    k.dma(xt[:], mem_d[sq, t * 128:(t + 1) * 128, :], w=[xt])
                    return xt
                for t in range(NMEM // 128 if dstop > 0 else 0):
                    xt = load_mem(t)
                    nb = P["nbf"].next()
                    rs = rstd_of(P, (xt, xt[:, :]), D, nb)
                    k.act(lambda e: e.activation(out=nb[:], in_=xt[:], func=AF.Copy, scale=rs[:, 0:1]), r=[xt, rs], w=[nb])
                    transpose_to(P, nb, D, mT, t, (gpre, PRE["mem_kv_norm_g"]))
                sk = load_w(P, wb["w_mk"], 0, KC, 0, 512)
                for hh in range(4 if dstop > 0 else 0):
                    acc = pfr.next()
                    for kc in range(KC):
                        k.pe(lambda e, kc=kc: e.matmul(acc[:, 0:NMEM], lhsT=sk[:, kc, hh * 128:(hh + 1) * 128], rhs=mT[:, kc, 0:NMEM],
                                                       start=(kc == 0), stop=(kc == KC - 1)), r=[sk, mT], w=[acc])
                    k.act(lambda e: e.activation(out=KmT[:, hh, :], in_=acc[:, 0:NMEM], func=AF.Copy), r=[acc], w=[KmT])
                sv = load_w(P, wb["w_mv"], 0, KC, 0, 512)
                for t in range(NMEM // 128 if dstop > 0 else 0):
                    acc = pfr.next()
                    for kc in range(KC):
                        k.pe(lambda e, kc=kc: e.matmul(acc[:, :], lhsT=mT[:, kc, t * 128:(t + 1) * 128], rhs=sv[:, kc, :],
                                                       start=(kc == 0), stop=(kc == KC - 1)), r=[sv, mT], w=[acc])
                    k.act(lambda e: e.activation(out=Vm[:, t, :], in_=acc[:, :], func=AF.Copy), r=[acc], w=[Vm])
                om_r = [dt_("omla"), dt_("ogdn")]
                for blk in range(NB if dstop > 1 else 0):
                    if dbg == "D1" and blk == 1:
                        break
                    t0 = blk * TB
                    hrow = dt_("h1%d" % blk)
                    k.dma(nT[:], omixT_d[:, :, t0:t0 + TB], r=om_r, w=[nT])
                    for dg in range(4):
                        so = load_w(P, wb["w_out"], 0, KC, dg * 512, 512)
                        for t in range(TPB):
                            acc = pfr.next()
                            for kc in range(KC):
                                k.pe(lambda e, kc=kc: e.matmul(acc[:, :], lhsT=nT[:, kc, t * 128:(t + 1) * 128], rhs=so[:, kc, :],
                                                               start=(kc == 0), stop=(kc == KC - 1)), r=[so, nT], w=[acc])
                            k.act(lambda e: e.activation(out=ysb[t][:, dg * 512:(dg + 1) * 512], in_=acc[:, :], func=AF.Copy),
                                  r=[acc], w=[ysb[t]])

                    def mk_loader(gname, half):
                        def ld(t, junk):
                            xt = xr.next()
                            k.dma(xt[:], h1_d[t0 + t * 128:t0 + (t + 1) * 128, :], r=[hrow], w=[xt])
                            post_residual(P, ysb[t], gname, xt, xt, half, junk)
                            k.dma(h1_d[t0 + t * 128:t0 + (t + 1) * 128, :], xt[:], r=[xt], w=[hrow], own=xt, accum_w=True, q="pool")
                            return xt
                        return ld
                    norm_transpose_block(P, mk_loader("mix_post_g", False), "mem_pre_g", nT)
                    if dstop == 2:
                        break
                    sq_ = load_w(P, wb["w_mq"], 0, KC, 0, 512)
                    for hh in range(4):
                        acc = pfr.next()
                        for kc in range(KC):
                            k.pe(lambda e, kc=kc: e.matmul(acc[:, 0:TB], lhsT=sq_[:, kc, hh * 128:(hh + 1) * 128], rhs=nT[:, kc, :],
                                                           start=(kc == 0), stop=(kc == KC - 1)), r=[sq_, nT], w=[acc])
                        k.dve(lambda e: e.tensor_copy(out=qmT[:, hh, :], in_=acc[:, 0:TB]), r=[acc], w=[qmT])
                    for hh in range(4):
                        accO, accR = pf[0], pf[1]
                        for mt in range(2):
                            sp_ = pf[2 + mt]
                            k.pe(lambda e: e.matmul(sp_[:, 0:TB], lhsT=KmT[:, hh, mt * 128:(mt + 1) * 128], rhs=qmT[:, hh, :],
                                                    start=True, stop=True), r=[KmT, qmT], w=[sp_])
                            p_ = Pr.next()
                            k.act(lambda e: e.activation(out=p_[:], in_=sp_[:, 0:TB], func=AF.Exp, scale=mscale), r=[sp_], w=[p_])
                            k.pe(lambda e: e.matmul(accO[:, 0:TB], lhsT=Vm[:, mt, hh * 128:(hh + 1) * 128], rhs=p_[:],
                                                    start=(mt == 0), stop=(mt == 1)), r=[Vm, p_], w=[accO])
                            k.pe(lambda e: e.matmul(accR[:, 0:TB], lhsT=ONESB, rhs=p_[:], start=(mt == 0), stop=(mt == 1)),
                                 r=[cst_b, p_], w=[accR])
                        ri = rinv.next()
                        k.dve(lambda e: e.reciprocal(out=ri[:], in_=accR[:, 0:TB]), r=[accR], w=[ri])
                        k.dve(lambda e: e.tensor_tensor(out=omT[:, hh, :], in0=accO[:, 0:TB], in1=ri[:], op=ALU.mult),
                              r=[accO, ri], w=[omT])
                    pfr.i = 0
                    for dg in range(4):
                        so = load_w(P, wb["w_mo"], 0, 4, dg * 512, 512)
                        for t in range(TPB):
                            acc = pfr.next()
                            for hh in range(4):
                                k.pe(lambda e, hh=hh: e.matmul(acc[:, :], lhsT=omT[:, hh, t * 128:(t + 1) * 128], rhs=so[:, hh, :],
                                                               start=(hh == 0), stop=(hh == 3)), r=[so, omT], w=[acc])
                            k.act(lambda e: e.activation(out=ysb[t][:, dg * 512:(dg + 1) * 512], in_=acc[:, :], func=AF.Copy),
                                  r=[acc], w=[ysb[t]])
                    if dstop == 3:
                        break
                    norm_transpose_block(P, mk_loader("mem_post_g", False), "ffn2_pre_g", nT)
                    ffn(P, nT, wb["ffn2_w_gate"], wb["ffn2_w_up"], wb["ffn2_w_down"], ysb)
                    if dstop == 4:
                        break
                    for t in range(TPB):
                        xt = xr.next()
                        k.dma(xt[:], h1_d[t0 + t * 128:t0 + (t + 1) * 128, :], r=[hrow], w=[xt])
                        jk = P["nbf"].next()
                        post_residual(P, ysb[t], "ffn2_post_g", xt, xt, True, jk)
                        rs = rstd_of(P, (xt, xt[:, :]), D, jk)
                        gB = load_gain(P, "final_norm_g", 1)
                        k.dve(lambda e: e.scalar_tensor_tensor(out=xt[:], in0=xt[:], scalar=rs[:, 0:1], in1=gB[:],
                                                               op0=ALU.mult, op1=ALU.mult), r=[xt, rs, gB], w=[xt])
                        k.dma(y_d[sq, t0 + t * 128:t0 + (t + 1) * 128, :], xt[:], r=[xt], w=[dt_("y")], own=xt, accum_w=True, q="pool")
            k.barrier()
        k.finish()
        print("BUILD stats: nins=%d nwait=%d nsems=%d" % (k.nins, k.nwait, len(k.sems)), flush=True)
    return nc


def host_layouts(inp, S):
    f = np.float32
    out = {}
    for n in ["ffn1_w_gate", "ffn1_w_up", "ffn1_w_down", "ffn2_w_gate", "ffn2_w_up", "ffn2_w_down",
              "w_in", "w_uq", "w_ukv", "w_out", "w_mq", "w_mk", "w_mv", "w_mo"]:
        out[n] = np.ascontiguousarray(np.asarray(inp[n], f)[0])
    for n in ["ffn1_pre_g", "ffn1_post_g", "mix_pre_g", "mix_post_g", "mem_pre_g", "mem_kv_norm_g",
              "mem_post_g", "ffn2_pre_g", "ffn2_post_g", "final_norm_g"]:
        out[n + "_bc"] = np.ascontiguousarray(np.broadcast_to(np.asarray(inp[n], f)[0][None, :], (128, D)))
    pre = ["ffn1_pre_g", "mix_pre_g", "mem_pre_g", "mem_kv_norm_g", "ffn2_pre_g"]
    out["gpre"] = np.ascontiguousarray(
        np.stack([np.asarray(inp[n], f)[0].reshape(KC, 128).T for n in pre], axis=1))
    qg = np.asarray(inp["mla_q_norm_g"], f)[0].reshape(4, 128).T
    kg = np.asarray(inp["mla_kv_norm_g"], f)[0].reshape(2, 128).T
    out["qkg"] = np.ascontiguousarray(np.concatenate([qg, kg], axis=1))
    i = np.arange(128)
    p, fr = i[:, None], i[None, :]
    out["consts"] = np.ascontiguousarray(np.stack(
        [np.eye(128), np.ones((128, 128)), fr <= p, fr < p, fr >= p, fr > p], axis=1).astype(f))
    NT = S // 128
    inv_freq = (10000.0 ** (-np.arange(0, ROPE, 2, dtype=f) / f(ROPE))).astype(f)
    ang = (np.arange(S, dtype=f)[:, None] * inv_freq[None, :]).astype(f)
    cos = np.cos(ang).astype(f).reshape(NT, 128, 32).transpose(1, 0, 2)
    sin = np.sin(ang).astype(f).reshape(NT, 128, 32).transpose(1, 0, 2)
    out["rope"] = np.ascontiguousarray(np.stack([cos, sin], axis=1))
    al = np.asarray(inp["gdn_a_log"], f)[0].reshape(16)
    dtb = np.asarray(inp["gdn_dt_bias"], f)[0].reshape(16)
    out["gdnp"] = np.ascontiguousarray(np.broadcast_to(np.stack([al, dtb], 0)[None], (128, 2, 16)))
    out["gon"] = np.ascontiguousarray(np.broadcast_to(np.asarray(inp["gdn_out_norm_g"], f)[0][None, :], (128, 128)))
    cw = np.asarray(inp["gdn_conv_w"], f)[0]
    out["cw"] = np.ascontiguousarray(cw.reshape(5, 24, 128).transpose(2, 1, 0))
    return out


S_FULL = 4096
_NC_CACHE = {}


def kernel(**inputs):
    S = S_FULL
    NSEQ = 2
    if "nc" not in _NC_CACHE:
        _NC_CACHE["nc"] = build(S, NSEQ)
    nc = _NC_CACHE["nc"]
    hl = host_layouts(inputs, S)
    xp = np.asarray(inputs["x_prompt"], np.float32)
    xs = np.asarray(inputs["x_sample"], np.float32)
    mp = np.asarray(inputs["mem_prompt"], np.float32)
    ms = np.asarray(inputs["mem_sample"], np.float32)
    in_maps = []
    for c in range(8):
        m = dict(hl)
        m["x"] = np.ascontiguousarray(np.stack([xp[c], xs[c % 2]], axis=0))
        m["mem"] = np.ascontiguousarray(np.stack([mp[c], ms[c % 2]], axis=0))
        in_maps.append(m)
    res = run_bass_kernel_spmd(nc, in_maps, core_ids=list(range(8)))
    yp = np.stack([np.asarray(res.results[c]["y"])[0] for c in range(8)], axis=0).astype(np.float32)
    ys = np.stack([np.asarray(res.results[c]["y"])[1] for c in range(2)], axis=0).astype(np.float32)
    return (yp, ys)
```

```python
import contextlib
import numpy as np
import concourse.bass as bass
import concourse.mybir as mybir
from concourse.bass_utils import run_bass_kernel_spmd

F32 = mybir.dt.float32
BF16 = mybir.dt.bfloat16
AF = mybir.ActivationFunctionType
ALU = mybir.AluOpType

D = 2048
DFF = 5632
NMEM = 256
EPS = 1e-6
QR, KVR, ROPE, NOPE, VD = 512, 256, 64, 128, 128
INW = 4960
KC = D // 128
FC = DFF // 128


class Trk:
    __slots__ = ("w", "r", "dsem")

    def __init__(self):
        self.w = {}
        self.r = {}
        self.dsem = None


class Buf:
    def __init__(self, t, psum=False):
        self.t = t
        self.T = Trk()
        self.psum = psum

    def __getitem__(self, idx):
        return self.t[idx]


class Eng:
    def __init__(self, name, e, semid, compute):
        self.name, self.e, self.semid, self.compute = name, e, semid, compute
        self.cnt = 0
        self.seen = {}


class K:
    def __init__(self, nc, stack):
        self.nc = nc
        self.stack = stack
        self.sems = []
        self.semcnt = []
        self.eng = {}
        for name, e, comp in (("pe", nc.tensor, True), ("act", nc.scalar, True), ("dve", nc.vector, True),
                              ("pool", nc.gpsimd, True), ("sp", nc.sync, False)):
            sid = self.newsem("e_" + name)
            self.eng[name] = Eng(name, e, sid, comp)
        self.free_dsems = []
        self.nwait = 0
        self.nins = 0

    def newsem(self, name):
        s = self.stack.enter_context(self.nc.semaphore(name))
        self.sems.append(s)
        self.semcnt.append(0)
        return len(self.sems) - 1

    def _wait(self, E, sid, val, raw=True):
        if val <= 0:
            return
        if sid == E.semid:
            if E.name == "pe" or not E.compute or not raw:
                return
        if E.seen.get(sid, 0) >= val:
            return
        E.e.wait_ge(self.sems[sid], val)
        E.seen[sid] = val
        self.nwait += 1

    def _deps(self, r, w):
        need = {}
        for x in r:
            for s, v in x.T.w.items():
                if need.get(s, 0) < v:
                    need[s] = v
            if x.psum:
                for s, v in x.T.r.items():
                    if need.get(s, 0) < v:
                        need[s] = v
        for x in w:
            for s, v in x.T.w.items():
                if need.get(s, 0) < v:
                    need[s] = v
            for s, v in x.T.r.items():
                if need.get(s, 0) < v:
                    need[s] = v
        return need

    def op(self, eng, fn, r=(), w=()):
        E = self.eng[eng]
        rawv = 0
        for x in r:
            rawv = max(rawv, x.T.w.get(E.semid, 0))
        for s, v in self._deps(r, w).items():
            if s == E.semid:
                self._wait(E, s, v, raw=True)
            else:
                self._wait(E, s, v)
        ins = fn(E.e)
        ins.then_inc(self.sems[E.semid], 1)
        E.cnt += 1
        self.semcnt[E.semid] = E.cnt
        self.nins += 1
        for x in r:
            if x.T.r.get(E.semid, 0) < E.cnt:
                x.T.r[E.semid] = E.cnt
        for x in w:
            x.T.w = {E.semid: E.cnt}
            x.T.r = {}

    def pe(self, fn, r=(), w=()):
        self.op("pe", fn, r, w)

    def act(self, fn, r=(), w=()):
        self.op("act", fn, r, w)

    def dve(self, fn, r=(), w=()):
        self.op("dve", fn, r, w)

    def pool(self, fn, r=(), w=()):
        self.op("pool", fn, r, w)

    def dma(self, out, in_, r=(), w=(), own=None, q="sp", accum_w=False):
        E = self.eng[q]
        if own is None:
            own = w[0] if w else r[0]
        T = own.T
        if T.dsem is None:
            T.dsem = self.newsem("d%d" % len(self.sems))
        sid = T.dsem
        need = self._deps(r, [] if accum_w else w)
        if accum_w:
            for x in w:
                for s, v in x.T.r.items():
                    if need.get(s, 0) < v:
                        need[s] = v
        if not accum_w and need.get(sid, 0) < self.semcnt[sid]:
            need[sid] = self.semcnt[sid]
        for s, v in need.items():
            self._wait(E, s, v)
        ins = E.e.dma_start(out=out, in_=in_)
        ins.then_inc(self.sems[sid], 16)
        self.semcnt[sid] += 16
        val = self.semcnt[sid]
        self.nins += 1
        for x in r:
            if x.T.r.get(sid, 0) < val:
                x.T.r[sid] = val
        for x in w:
            if accum_w:
                x.T.w[sid] = val
            else:
                x.T.w = {sid: val}
                x.T.r = {}

    def barrier(self):
        for E in self.eng.values():
            for sid in range(len(self.sems)):
                if sid != E.semid:
                    self._wait(E, sid, self.semcnt[sid])

    def finish(self):
        E = self.eng["sp"]
        for sid in range(len(self.sems)):
            self._wait(E, sid, self.semcnt[sid])


class Pools:
    _uid = [0]

    def __init__(self, k, stack):
        self.k, self.stack, self.n = k, stack, 0

    def sb(self, shape, dt, name=None):
        Pools._uid[0] += 1
        nm = "%s_%d" % (name or "sb", Pools._uid[0])
        return Buf(self.stack.enter_context(self.k.nc.sbuf_tensor(nm, list(shape), dt)))

    def ps(self, shape, dt, name=None):
        Pools._uid[0] += 1
        nm = "%s_%d" % (name or "ps", Pools._uid[0])
        return Buf(self.stack.enter_context(self.k.nc.psum_tensor(nm, list(shape), dt)), psum=True)

    def ring(self, n, shape, dt, name=None):
        return Ring([self.sb(shape, dt, name) for _ in range(n)])


class Ring:
    def __init__(self, bufs):
        self.bufs, self.i = bufs, 0

    def next(self):
        b = self.bufs[self.i % len(self.bufs)]
        self.i += 1
        return b


class _StopD(Exception):
    pass


def build(S, NSEQ, dbg=False):
    nc = bass.Bass("TRN2", target_bir_lowering=False)
    flags = set(str(dbg).split("+")) if dbg else set()
    dstop = 99
    cstop = 99
    for f_ in flags:
        if f_.startswith("ds"):
            dstop = int(f_[2:])
        if f_.startswith("cs"):
            cstop = int(f_[2:])
    if "noB" in flags or "noC" in flags or dstop < 99 or cstop != 99:
        dbg = "X"
    NT = S // 128
    TB = min(512, S)
    NB = S // TB
    TPB = TB // 128
    QB = TB
    NQB = S // QB

    def din(name, shape, dt=F32):
        return nc.dram_tensor(name, list(shape), dt, kind="ExternalInput").ap()

    def dscr(name, shape, dt):
        return nc.dram_tensor(name, list(shape), dt, kind=("ExternalOutput" if dbg else "Internal")).ap()

    x_d = din("x", [NSEQ, S, D])
    mem_d = din("mem", [NSEQ, NMEM, D])
    y_d = nc.dram_tensor("y", [NSEQ, S, D], F32, kind="ExternalOutput").ap()
    wnames = {"ffn1_w_gate": (D, DFF), "ffn1_w_up": (D, DFF), "ffn1_w_down": (DFF, D),
              "ffn2_w_gate": (D, DFF), "ffn2_w_up": (D, DFF), "ffn2_w_down": (DFF, D),
              "w_in": (D, INW), "w_uq": (QR, 8 * 192), "w_ukv": (KVR, 8 * 256), "w_out": (D, D),
              "w_mq": (D, 512), "w_mk": (D, 512), "w_mv": (D, 512), "w_mo": (512, D)}
    wf = {n: din(n, s) for n, s in wnames.items()}
    wb = {n: nc.dram_tensor(n + "_b", list(s), BF16, kind="Internal").ap() for n, s in wnames.items()}
    gnames = ["ffn1_pre_g", "ffn1_post_g", "mix_pre_g", "mix_post_g", "mem_pre_g", "mem_kv_norm_g",
              "mem_post_g", "ffn2_pre_g", "ffn2_post_g", "final_norm_g"]
    gbd = {n: din(n + "_bc", [128, D]) for n in gnames}
    gpd = din("gpre", [128, 5, KC])
    qkg_d = din("qkg", [128, 6])
    consts_d = din("consts", [128, 6, 128])
    rope_d = din("rope", [128, 2, NT, 32])
    gdnp_d = din("gdnp", [128, 2, 16])
    gon_d = din("gon", [128, 128])
    cw_d = din("cw", [128, 24, 5])

    h1_d = dscr("h1_s", [S, D], F32)
    cqnT_d = dscr("cqnT_s", [128, 4, S], BF16)
    ckvnT_d = dscr("ckvnT_s", [128, 2, S], BF16)
    krT_d = dscr("krT_s", [64, S], BF16)
    qkvT_d = dscr("qkvT_s", [128, 24, S], F32)
    zs_d = dscr("zs_s", [S, 1024], F32)
    gb_d = dscr("gb_s", [128, NT, 32], F32)
    omixT_d = dscr("omixT_s", [128, 16, S], BF16)

    with contextlib.ExitStack() as gstack:
        k = K(nc, gstack)
        GP = Pools(k, gstack)
        pf = [GP.ps([128, 512], F32, "pf") for _ in range(6)]
        pb = [GP.ps([128, 1024], BF16, "pb") for _ in range(2)]
        pfr = Ring(pf)
        pbr = Ring(pb)
        class DT:
            pass
        dtrk = {}

        def dt_(name):
            if name not in dtrk:
                dtrk[name] = Buf(None)
            return dtrk[name]

        cst_f = GP.sb([128, 6, 128], F32, "cstf")
        cst_b = GP.sb([128, 6, 128], BF16, "cstb")
        k.dma(cst_f[:], consts_d, w=[cst_f])
        k.dve(lambda e: e.tensor_copy(out=cst_b[:], in_=cst_f[:]), r=[cst_f], w=[cst_b])
        IDB = lambda n=128: cst_b[0:n, 0, 0:n]
        ONESB = cst_b[:, 1, :]
        ONESF = cst_f[:, 1, :]
        LOW, SLOW, UP, SUP = (cst_f[:, i, :] for i in (2, 3, 4, 5))
        gpre = GP.sb([128, 5, KC], F32, "gpre")
        k.dma(gpre[:], gpd, w=[gpre])
        qkg = GP.sb([128, 6], F32, "qkg")
        k.dma(qkg[:], qkg_d, w=[qkg])
        PRE = {"ffn1_pre_g": 0, "mix_pre_g": 1, "mem_pre_g": 2, "mem_kv_norm_g": 3, "ffn2_pre_g": 4}

        wtrks = {n: Buf(None) for n in wnames}
        wkey = {id(wb[n]): n for n in wnames}
        corder = ["ffn1_w_gate", "ffn1_w_up", "ffn1_w_down", "w_in", "w_uq", "w_ukv", "w_out", "w_mq", "w_mk", "w_mv", "w_mo",
                  "ffn2_w_gate", "ffn2_w_up", "ffn2_w_down"]
        for n in corder:
            rows, cols = wnames[n]
            step = 256
            for r0 in range(0, rows, step):
                r1 = min(rows, r0 + step)
                k.dma(wb[n][r0:r1, :], wf[n][r0:r1, :], w=[wtrks[n]], own=wtrks[n], q="pool", accum_w=True)

        def rstd_of(P, src, W, junk, extra=1.0):
            sbuf, sap = src
            ss = P["ss"].next()
            k.pool(lambda e: e.memset(ss[:], 0.0), w=[ss])
            k.act(lambda e: e.activation(out=junk[:, 0:W], in_=sap, func=AF.Square, accum_out=ss[:, 0:1]),
                  r=[sbuf, ss], w=[junk, ss])
            rs = P["rs"].next()
            ex2 = float(extra) ** 2
            k.act(lambda e: e.activation(out=rs[:], in_=ss[:], func=AF.Sqrt, scale=1.0 / (W * ex2), bias=EPS / ex2),
                  r=[ss], w=[rs])
            k.dve(lambda e: e.reciprocal(out=rs[:], in_=rs[:]), r=[rs], w=[rs])
            return rs

        def transpose_to(P, nbf, ncols, dstT, t, gidx, alt=[0]):
            nch = ncols // 128
            for c0 in range(0, nch, 8):
                c1 = min(nch, c0 + 8)
                bank = pbr.next()
                for c in range(c0, c1):
                    k.pe(lambda e, c=c: e.transpose(bank[:, (c - c0) * 128:(c - c0 + 1) * 128],
                                                    nbf[:, c * 128:(c + 1) * 128], IDB()),
                         r=[nbf, cst_b], w=[bank])
                if gidx is None:
                    nn = c1 - c0
                    k.dve(lambda e: e.tensor_copy(out=dstT[:, c0:c1, t * 128:(t + 1) * 128],
                                                  in_=bank[:, 0:nn * 128].rearrange("p (c q) -> p c q", q=128)),
                          r=[bank], w=[dstT])
                    continue
                for c in range(c0, c1):
                    src = bank[:, (c - c0) * 128:(c - c0 + 1) * 128]
                    dst = dstT[:, c, t * 128:(t + 1) * 128]
                    if gidx is None:
                        fn = lambda e, src=src, dst=dst: e.tensor_copy(out=dst, in_=src)
                        rr = [bank]
                    else:
                        gbuf, g0 = gidx
                        gap = gbuf[:, g0 + c:g0 + c + 1] if len(gbuf.t.shape) == 2 else gbuf[:, g0, c:c + 1]
                        rr = [bank, gbuf]
                        if alt[0] % 2 == 0:
                            fn = lambda e, src=src, dst=dst, gap=gap: e.tensor_scalar(
                                out=dst, in0=src, scalar1=gap, scalar2=None, op0=ALU.mult)
                        else:
                            fn = lambda e, src=src, dst=dst, gap=gap: e.activation(
                                out=dst, in_=src, func=AF.Copy, scale=gap)
                    if gidx is not None and alt[0] % 2 == 1:
                        k.act(fn, r=rr, w=[dstT])
                    else:
                        k.dve(fn, r=rr, w=[dstT])
                alt[0] += 1

        def load_w(P, wd, r0, nkc, c0, ncols, coff=0, slot=None):
            if slot is None:
                slot = P["wring"].next()
            src = wd[r0 * 128:(r0 + nkc) * 128, c0:c0 + ncols].rearrange("(c p) f -> p c f", p=128)
            k.dma(slot[:, 0:nkc, coff:coff + ncols], src, r=[wtrks[wkey[id(wd)]]], w=[slot], own=slot, accum_w=(coff != 0))
            return slot

        def load_gain(P, gname, slot):
            gt = P["gBs"][slot]
            if P["gcur"].get(slot) != gname:
                k.dma(gt[:], gbd[gname], w=[gt])
                P["gcur"][slot] = gname
            return gt

        def norm_transpose_block(P, load_tile, gain_name, nT, keep=None):
            gt = load_gain(P, gain_name, 1)
            nbs = {}

            def s1(t):
                nb = P["nbf"].next()
                xt = load_tile(t, nb)
                rs = rstd_of(P, (xt, xt[:, :]), D, nb)
                k.dve(lambda e: e.scalar_tensor_tensor(out=nb[:], in0=xt[:], scalar=rs[:, 0:1], in1=gt[:],
                                                       op0=ALU.mult, op1=ALU.mult), r=[xt, rs, gt], w=[nb])
                nbs[t] = nb
            s1(0)
            for t in range(TPB):
                if t + 1 < TPB:
                    s1(t + 1)
                transpose_to(P, nbs[t], D, nT, t, None)

        def ffn(P, nT, wg, wu, wd, ysb):
            hT = P["hT"]
            for g in range(FC // 2):
                sgu = load_w(P, wg, 0, KC, g * 256, 256)
                load_w(P, wu, 0, KC, g * 256, 256, coff=256, slot=sgu)
                for f in range(2):
                    pg, pu = pfr.next(), pfr.next()
                    for kc in range(KC):
                        k.pe(lambda e, kc=kc: e.matmul(pg[:, 0:TB], lhsT=sgu[:, kc, f * 128:(f + 1) * 128],
                                                       rhs=nT[:, kc, :], start=(kc == 0), stop=(kc == KC - 1)),
                             r=[sgu, nT], w=[pg])
                    for kc in range(KC):
                        k.pe(lambda e, kc=kc: e.matmul(pu[:, 0:TB], lhsT=sgu[:, kc, 256 + f * 128:256 + (f + 1) * 128],
                                                       rhs=nT[:, kc, :], start=(kc == 0), stop=(kc == KC - 1)),
                             r=[sgu, nT], w=[pu])
                    sl = P["silu"].next()
                    k.act(lambda e: e.activation(out=sl[:, 0:TB], in_=pg[:, 0:TB], func=AF.Silu), r=[pg], w=[sl])
                    fc = g * 2 + f
                    k.dve(lambda e: e.tensor_tensor(out=hT[:, fc, :], in0=sl[:, 0:TB], in1=pu[:, 0:TB], op=ALU.mult),
                          r=[sl, pu], w=[hT])
            for dg in range(4):
                accs = [pf[i] for i in range(TPB)]
                for fg in range(4):
                    sd = load_w(P, wd, fg * 11, 11, dg * 512, 512)
                    for t in range(TPB):
                        for f in range(11):
                            fc = fg * 11 + f
                            k.pe(lambda e, t=t, f=f, fc=fc: e.matmul(
                                accs[t][:, :], lhsT=hT[:, fc, t * 128:(t + 1) * 128], rhs=sd[:, f, :],
                                start=(fc == 0), stop=(fc == FC - 1)), r=[hT, sd], w=[accs[t]])
                for t in range(TPB):
                    k.act(lambda e, t=t: e.activation(out=ysb[t][:, dg * 512:(dg + 1) * 512], in_=accs[t][:, :],
                                                      func=AF.Copy), r=[accs[t]], w=[ysb[t]])
            pfr.i = 0

        def post_residual(P, ysrc, gname, base, out, half, junk):
            gB = load_gain(P, gname, 0)
            rs = rstd_of(P, (ysrc, ysrc[:, :]), D, junk, extra=(0.5 if half else 1.0))
            k.dve(lambda e: e.scalar_tensor_tensor(out=ysrc[:], in0=ysrc[:], scalar=rs[:, 0:1], in1=gB[:],
                                                   op0=ALU.mult, op1=ALU.mult), r=[ysrc, rs, gB], w=[ysrc])
            k.pool(lambda e: e.tensor_tensor(out=out[:], in0=base[:], in1=ysrc[:], op=ALU.add),
                   r=[base, ysrc], w=[out])

        for sq in range(NSEQ):
            with contextlib.ExitStack() as st:
                A = Pools(k, st)
                P = {"ss": A.ring(4, [128, 1], F32), "rs": A.ring(4, [128, 1], F32),
                     "junk": A.sb([128, 512], BF16), "nbf": A.ring(2, [128, D], BF16),
                     "wring": A.ring(3, [128, KC, 512], BF16), "hT": A.sb([128, FC, TB], BF16),
                     "silu": A.ring(1, [128, 512], F32), "gBs": [A.sb([128, D], F32) for _ in range(2)], "gcur": {}}
                nT = A.sb([128, KC, TB], BF16)
                xr = A.ring(2, [128, D], F32)
                ysb = [A.sb([128, D], F32) for _ in range(TPB)]
                cs = A.sb([128, 2, TPB, 32], F32)
                gdnp = A.sb([128, 2, 16], F32)
                k.dma(gdnp[:], gdnp_d, w=[gdnp])
                negA = A.sb([128, 16], F32)
                k.act(lambda e: e.activation(out=negA[:], in_=gdnp[:, 0, :], func=AF.Exp), r=[gdnp], w=[negA])
                k.dve(lambda e: e.tensor_scalar(out=negA[:], in0=negA[:], scalar1=-1.0, scalar2=None, op0=ALU.mult),
                      r=[negA], w=[negA])
                cqT = A.sb([128, 4, TB], BF16)
                ckT = A.sb([128, 2, TB], BF16)
                krT = A.sb([64, TB], BF16)
                gbs = A.sb([128, TPB, 32], F32)
                stg = A.ring(1, [128, 2, TB], F32)
                small = A.ring(1, [128, 512], F32)
                smallb = A.ring(2, [128, 512], BF16)
                tiny = A.ring(5, [128, 64], F32)
                for blk in range(NB):
                    t0 = blk * TB
                    k.dma(cs[:], rope_d[:, :, blk * TPB:(blk + 1) * TPB, :], w=[cs])

                    def load_x(t, junk=None):
                        xt = xr.next()
                        k.dma(xt[:], x_d[sq, t0 + t * 128:t0 + (t + 1) * 128, :], w=[xt])
                        return xt
                    norm_transpose_block(P, load_x, "ffn1_pre_g", nT)
                    ffn(P, nT, wb["ffn1_w_gate"], wb["ffn1_w_up"], wb["ffn1_w_down"], ysb)
                    h1t = {}

                    def load_h1(t, junk):
                        xt = load_x(t)
                        post_residual(P, ysb[t], "ffn1_post_g", xt, xt, True, junk)
                        k.dma(h1_d[t0 + t * 128:t0 + (t + 1) * 128, :], xt[:], r=[xt], w=[dt_("h1%d" % (blk))],
                              own=xt, accum_w=True, q="pool")
                        return xt
                    norm_transpose_block(P, load_h1, "mix_pre_g", nT)
                    s0 = load_w(P, wb["w_in"], 0, KC, 0, 512)
                    for t in range(TPB):
                        acc = pfr.next()
                        for kc in range(KC):
                            k.pe(lambda e, kc=kc: e.matmul(acc[:, :], lhsT=nT[:, kc, t * 128:(t + 1) * 128],
                                                           rhs=s0[:, kc, :], start=(kc == 0), stop=(kc == KC - 1)),
                                 r=[nT, s0], w=[acc])
                        cq = small.next()
                        k.act(lambda e: e.activation(out=cq[:], in_=acc[:, :], func=AF.Copy), r=[acc], w=[cq])
                        rs = rstd_of(P, (cq, cq[:, :]), 512, P["junk"])
                        cqn = smallb.next()
                        k.act(lambda e: e.activation(out=cqn[:], in_=cq[:], func=AF.Copy, scale=rs[:, 0:1]),
                              r=[cq, rs], w=[cqn])
                        transpose_to(P, cqn, 512, cqT, t, (qkg, 0))
                    k.dma(cqnT_d[:, :, t0:t0 + TB], cqT[:], r=[cqT], w=[dt_("cq%d" % blk)], own=cqT, q="pool")
                    s1 = load_w(P, wb["w_in"], 0, KC, 512, 320)
                    load_w(P, wb["w_in"], 0, KC, 4928, 32, coff=320, slot=s1)
                    for t in range(TPB):
                        acc = pfr.next()
                        for kc in range(KC):
                            k.pe(lambda e, kc=kc: e.matmul(acc[:, 0:352], lhsT=nT[:, kc, t * 128:(t + 1) * 128],
                                                           rhs=s1[:, kc, 0:352], start=(kc == 0), stop=(kc == KC - 1)),
                                 r=[nT, s1], w=[acc])
                        ck = small.next()
                        k.act(lambda e: e.activation(out=ck[:, 0:352], in_=acc[:, 0:352], func=AF.Copy),
                              r=[acc], w=[ck])
                        rs = rstd_of(P, (ck, ck[:, 0:256]), 256, P["junk"])
                        ckn = smallb.next()
                        k.act(lambda e: e.activation(out=ckn[:, 0:256], in_=ck[:, 0:256], func=AF.Copy,
                                                     scale=rs[:, 0:1]), r=[ck, rs], w=[ckn])
                        transpose_to(P, ckn, 256, ckT, t, (qkg, 4))
                        cos, sin = cs[:, 0, t, :], cs[:, 1, t, :]
                        ta, tb_ = tiny.next(), tiny.next()
                        x1, x2 = ck[:, 256:288], ck[:, 288:320]
                        k.dve(lambda e: e.tensor_tensor(out=ta[:, 0:32], in0=x1, in1=cos, op=ALU.mult), r=[ck, cs], w=[ta])
                        k.dve(lambda e: e.tensor_tensor(out=ta[:, 32:64], in0=x2, in1=cos, op=ALU.mult), r=[ck, cs], w=[ta])
                        k.pool(lambda e: e.tensor_tensor(out=tb_[:, 0:32], in0=x2, in1=sin, op=ALU.mult), r=[ck, cs], w=[tb_])
                        k.pool(lambda e: e.tensor_tensor(out=tb_[:, 32:64], in0=x1, in1=sin, op=ALU.mult), r=[ck, cs], w=[tb_])
                        krb = smallb.next()
                        k.dve(lambda e: e.tensor_tensor(out=krb[:, 0:32], in0=ta[:, 0:32], in1=tb_[:, 0:32],
                                                        op=ALU.subtract), r=[ta, tb_], w=[krb])
                        k.dve(lambda e: e.tensor_tensor(out=krb[:, 32:64], in0=ta[:, 32:64], in1=tb_[:, 32:64],
                                                        op=ALU.add), r=[ta, tb_], w=[krb])
                        bank = pbr.next()
                        k.pe(lambda e: e.transpose(bank[0:64, 0:128], krb[:, 0:64], IDB()), r=[krb, cst_b], w=[bank])
                        k.dve(lambda e: e.tensor_copy(out=krT[:, t * 128:(t + 1) * 128], in_=bank[0:64, 0:128]),
                              r=[bank], w=[krT])
                        a_, b_ = ck[:, 320:336], ck[:, 336:352]
                        u0, u1, u2 = tiny.next(), tiny.next(), tiny.next()
                        k.dve(lambda e: e.tensor_tensor(out=u0[:, 0:16], in0=a_, in1=gdnp[:, 1, :], op=ALU.add),
                              r=[ck, gdnp], w=[u0])
                        k.dve(lambda e: e.tensor_scalar(out=u1[:, 0:16], in0=u0[:, 0:16], scalar1=-1.0, scalar2=None,
                                                        op0=ALU.mult), r=[u0], w=[u1])
                        k.dve(lambda e: e.tensor_tensor(out=u1[:, 0:16], in0=u0[:, 0:16], in1=u1[:, 0:16], op=ALU.min),
                              r=[u0, u1], w=[u1])
                        k.act(lambda e: e.activation(out=u1[:, 0:16], in_=u1[:, 0:16], func=AF.Exp),
                              r=[u1], w=[u1])
                        k.act(lambda e: e.activation(out=u1[:, 0:16], in_=u1[:, 0:16], func=AF.Ln, bias=1.0),
                              r=[u1], w=[u1])
                        k.dve(lambda e: e.scalar_tensor_tensor(out=u2[:, 0:16], in0=u0[:, 0:16], scalar=0.0,
                                                               in1=u1[:, 0:16], op0=ALU.max, op1=ALU.add),
                              r=[u0, u1], w=[u2])
                        k.dve(lambda e: e.tensor_tensor(out=gbs[:, t, 0:16], in0=u2[:, 0:16], in1=negA[:], op=ALU.mult),
                              r=[u2, negA], w=[gbs])
                        k.act(lambda e: e.activation(out=gbs[:, t, 16:32], in_=b_, func=AF.Sigmoid), r=[ck], w=[gbs])
                    k.dma(ckvnT_d[:, :, t0:t0 + TB], ckT[:], r=[ckT], w=[dt_("ck%d" % blk)], own=ckT, q="pool")
                    k.dma(krT_d[:, t0:t0 + TB], krT[:], r=[krT], w=[dt_("kr%d" % blk)], own=krT, q="pool")
                    k.dma(gb_d[:, blk * TPB:(blk + 1) * TPB, :], gbs[:], r=[gbs], w=[dt_("gb%d" % blk)], own=gbs, q="pool")
                    for g in range(6):
                        sw = load_w(P, wb["w_in"], 0, KC, 832 + g * 512, 512)
                        for f2 in range(2):
                            sg = stg.next()
                            for ff in range(2):
                                f = f2 * 2 + ff
                                acc = pfr.next()
                                for kc in range(KC):
                                    k.pe(lambda e, kc=kc: e.matmul(acc[:, 0:TB], lhsT=sw[:, kc, f * 128:(f + 1) * 128],
                                                                   rhs=nT[:, kc, :], start=(kc == 0), stop=(kc == KC - 1)),
                                         r=[sw, nT], w=[acc])
                                if ff == 0:
                                    k.act(lambda e: e.activation(out=sg[:, ff, :], in_=acc[:, 0:TB], func=AF.Copy),
                                          r=[acc], w=[sg])
                                else:
                                    k.dve(lambda e: e.tensor_copy(out=sg[:, ff, :], in_=acc[:, 0:TB]), r=[acc], w=[sg])
                            k.dma(qkvT_d[:, g * 4 + f2 * 2:g * 4 + f2 * 2 + 2, t0:t0 + TB], sg[:, 0:2, :], r=[sg],
                                  w=[dt_("qkv%d" % blk)], own=sg, accum_w=True, q="pool")
                    sz = [load_w(P, wb["w_in"], 0, KC, 3904 + g * 512, 512) for g in range(2)]
                    for t in range(TPB):
                        zt = stg.next()
                        for g in range(2):
                            acc = pfr.next()
                            for kc in range(KC):
                                k.pe(lambda e, kc=kc: e.matmul(acc[:, :], lhsT=nT[:, kc, t * 128:(t + 1) * 128],
                                                               rhs=sz[g][:, kc, :], start=(kc == 0), stop=(kc == KC - 1)),
                                     r=[nT, sz[g]], w=[acc])
                            k.act(lambda e: e.activation(out=zt[:, g, :], in_=acc[:, :], func=AF.Silu),
                                  r=[acc], w=[zt])
                        k.dma(zs_d[t0 + t * 128:t0 + (t + 1) * 128, :].rearrange("s (g c) -> s g c", g=2), zt[:, 0:2, :],
                              r=[zt], w=[dt_("zs%d" % blk)],
                              own=zt, accum_w=True, q="pool")
            k.barrier()
            if dbg == "A":
                break
            with contextlib.ExitStack() as st:
                B = Pools(k, st)
                cqT = B.sb([128, 4, S], BF16)
                ckT = B.sb([128, 2, S], BF16)
                krT = B.sb([64, S], BF16)
                rA = [dt_("cq%d" % b) for b in range(NB)] + [dt_("ck%d" % b) for b in range(NB)] + \
                     [dt_("kr%d" % b) for b in range(NB)]
                k.dma(cqT[:], cqnT_d, r=rA, w=[cqT])
                k.dma(ckT[:], ckvnT_d, r=rA, w=[ckT])
                k.dma(krT[:], krT_d, r=rA, w=[krT])
                wuq = B.sb([128, 4, 8 * 192], BF16)
                wukv = B.sb([128, 2, 8 * 256], BF16)
                k.dma(wuq[:], wb["w_uq"].rearrange("(c p) f -> p c f", p=128), r=[wtrks["w_uq"]], w=[wuq])
                k.dma(wukv[:], wb["w_ukv"].rearrange("(c p) f -> p c f", p=128), r=[wtrks["w_ukv"]], w=[wukv])
                cs = B.sb([128, 2, NT, 32], F32)
                k.dma(cs[:], rope_d, w=[cs])
                KT = B.sb([128, S], BF16)
                QnT = B.sb([128, S], BF16)
                QrT = B.sb([64, S], BF16)
                Vh = B.sb([128, NT * 128], BF16)
                OT = B.sb([128, S], BF16)
                Pr = B.ring(3, [128, QB], BF16)
                rinv = B.ring(2, [128, QB], F32)
                qra = B.ring(2, [128, 8, 64], F32)
                qrb = B.ring(2, [128, 8, 64], F32)
                qrbf = B.ring(2, [128, 8, 64], BF16)
                scale = float((NOPE + ROPE) ** -0.5)
                G8 = min(8, NT)
                for h in range(8):
                    if dbg in ("B0", "C", "C0", "C1", "D0", "D1", "CD") or "noB" in flags:
                        break
                    for blk in range(NQB):
                        sl = slice(blk * QB, (blk + 1) * QB)
                        acc = pfr.next()
                        for c in range(2):
                            k.pe(lambda e, c=c: e.matmul(acc[:, 0:QB], lhsT=wukv[:, c, h * 256:h * 256 + 128],
                                                         rhs=ckT[:, c, sl], start=(c == 0), stop=(c == 1)),
                                 r=[wukv, ckT], w=[acc])
                        k.act(lambda e: e.activation(out=KT[:, sl], in_=acc[:, 0:QB], func=AF.Copy), r=[acc], w=[KT])
                        acc2 = pfr.next()
                        for c in range(4):
                            k.pe(lambda e, c=c: e.matmul(acc2[:, 0:QB], lhsT=wuq[:, c, h * 192:h * 192 + 128],
                                                         rhs=cqT[:, c, sl], start=(c == 0), stop=(c == 3)),
                                 r=[wuq, cqT], w=[acc2])
                        k.dve(lambda e: e.tensor_copy(out=QnT[:, sl], in_=acc2[:, 0:QB]), r=[acc2], w=[QnT])
                    for tg in range(0, NT, 4):
                        acc = pfr.next()
                        for t in range(tg, min(NT, tg + 4)):
                            for c in range(2):
                                k.pe(lambda e, c=c, t=t: e.matmul(
                                    acc[:, (t - tg) * 128:(t - tg + 1) * 128], lhsT=ckT[:, c, t * 128:(t + 1) * 128],
                                    rhs=wukv[:, c, h * 256 + 128:(h + 1) * 256], start=(c == 0), stop=(c == 1)),
                                    r=[wukv, ckT], w=[acc])
                        n4 = min(NT, tg + 4) - tg
                        k.act(lambda e: e.activation(out=Vh[:, tg * 128:(tg + n4) * 128], in_=acc[:, 0:n4 * 128],
                                                     func=AF.Copy), r=[acc], w=[Vh])
                    if dbg == "B1":
                        break
                    for tg in range(0, NT, G8):
                        acc = pfr.next()
                        for t in range(tg, tg + G8):
                            for c in range(4):
                                k.pe(lambda e, c=c, t=t: e.matmul(
                                    acc[:, (t - tg) * 64:(t - tg + 1) * 64], lhsT=cqT[:, c, t * 128:(t + 1) * 128],
                                    rhs=wuq[:, c, h * 192 + 128:(h + 1) * 192], start=(c == 0), stop=(c == 3)),
                                    r=[wuq, cqT], w=[acc])
                        av = acc[:, 0:G8 * 64].rearrange("p (t r) -> p t r", r=64)
                        cos, sin = cs[:, 0, tg:tg + G8, :], cs[:, 1, tg:tg + G8, :]
                        ta, tb_, qb_ = qra.next(), qrb.next(), qrbf.next()
                        k.dve(lambda e: e.tensor_tensor(out=ta[:, 0:G8, 0:32], in0=av[:, :, 0:32], in1=cos, op=ALU.mult),
                              r=[acc, cs], w=[ta])
                        k.dve(lambda e: e.tensor_tensor(out=ta[:, 0:G8, 32:64], in0=av[:, :, 32:64], in1=cos, op=ALU.mult),
                              r=[acc, cs], w=[ta])
                        k.dve(lambda e: e.tensor_tensor(out=tb_[:, 0:G8, 0:32], in0=av[:, :, 32:64], in1=sin, op=ALU.mult),
                              r=[acc, cs], w=[tb_])
                        k.dve(lambda e: e.tensor_tensor(out=tb_[:, 0:G8, 32:64], in0=av[:, :, 0:32], in1=sin, op=ALU.mult),
                              r=[acc, cs], w=[tb_])
                        k.pool(lambda e: e.tensor_tensor(out=qb_[:, 0:G8, 0:32], in0=ta[:, 0:G8, 0:32],
                                                         in1=tb_[:, 0:G8, 0:32], op=ALU.subtract), r=[ta, tb_], w=[qb_])
                        k.pool(lambda e: e.tensor_tensor(out=qb_[:, 0:G8, 32:64], in0=ta[:, 0:G8, 32:64],
                                                         in1=tb_[:, 0:G8, 32:64], op=ALU.add), r=[ta, tb_], w=[qb_])
                        bank = pbr.next()
                        for t in range(G8):
                            k.pe(lambda e, t=t: e.transpose(bank[0:64, t * 128:(t + 1) * 128], qb_[:, t, :], IDB()),
                                 r=[qb_, cst_b], w=[bank])
                        k.act(lambda e: e.activation(out=QrT[:, tg * 128:(tg + G8) * 128], in_=bank[0:64, 0:G8 * 128],
                                                     func=AF.Copy), r=[bank], w=[QrT])
                    if dbg == "B2":
                        break
                    for qb in range(NQB):
                        qs = slice(qb * QB, (qb + 1) * QB)
                        accO, accR = pf[(qb % 2) * 2], pf[(qb % 2) * 2 + 1]
                        sps = [pf[4], pf[5]]

                        def qk(kt):
                            sp_ = sps[kt % 2]
                            k.pe(lambda e: e.matmul(sp_[:, 0:QB], lhsT=KT[:, kt * 128:(kt + 1) * 128], rhs=QnT[:, qs],
                                                    start=True, stop=False), r=[KT, QnT], w=[sp_])
                            k.pe(lambda e: e.matmul(sp_[:, 0:QB], lhsT=krT[:, kt * 128:(kt + 1) * 128], rhs=QrT[:, qs],
                                                    start=False, stop=True), r=[krT, QrT], w=[sp_])
                        qk(0)
                        for kt in range(NT):
                            if kt + 1 < NT:
                                qk(kt + 1)
                            sp_ = sps[kt % 2]
                            p_ = Pr.next()
                            k.act(lambda e: e.activation(out=p_[:], in_=sp_[:, 0:QB], func=AF.Exp, scale=scale),
                                  r=[sp_], w=[p_])
                            k.pe(lambda e: e.matmul(accO[:, 0:QB], lhsT=Vh[:, kt * 128:(kt + 1) * 128], rhs=p_[:], start=(kt == 0),
                                                    stop=(kt == NT - 1)), r=[Vh, p_], w=[accO])
                            k.pe(lambda e: e.matmul(accR[:, 0:QB], lhsT=ONESB, rhs=p_[:], start=(kt == 0),
                                                    stop=(kt == NT - 1)), r=[cst_b, p_], w=[accR])
                        ri = rinv.next()
                        k.dve(lambda e: e.reciprocal(out=ri[:], in_=accR[:, 0:QB]), r=[accR], w=[ri])
                        k.dve(lambda e: e.tensor_tensor(out=OT[:, qs], in0=accO[:, 0:QB], in1=ri[:], op=ALU.mult),
                              r=[accO, ri], w=[OT])
                    k.dma(omixT_d[:, h, :], OT[:], r=[OT], w=[dt_("omla")], own=OT, accum_w=True, q="pool")
                    pfr.i = 0
            k.barrier()
            if dbg in ("B", "B0", "B1", "B2"):
                break
            with contextlib.ExitStack() as st:
                C = Pools(k, st)
                try:
                    P = {"ss": C.ring(4, [128, 1], F32), "rs": C.ring(4, [128, 1], F32), "junk": C.sb([128, 128], BF16)}
                    gon = C.sb([128, 128], F32)
                    k.dma(gon[:], gon_d, w=[gon])
                    cw = C.sb([128, 24, 5], F32)
                    k.dma(cw[:], cw_d, w=[cw])
                    if cstop == 1:
                        raise _StopD()
                    W16 = NT * 16
                    H8 = NT * 8
                    gcs, egs, ek, bgc, nbeta, gq, bq, grem = (C.sb([128, W16], F32) for _ in range(8))
                    eg, dec = grem, gq
                    DKS = float(128 ** -0.5)
                    for d_ in range(2):
                        k.dma(gq[:, d_ * H8:(d_ + 1) * H8].rearrange("p (t n) -> p t n", n=8), gb_d[:, :, d_ * 8:(d_ + 1) * 8],
                              r=[dt_("gb%d" % b_) for b_ in range(NB)], w=[gq], own=gq, accum_w=(d_ == 1))
                        k.dma(bq[:, d_ * H8:(d_ + 1) * H8].rearrange("p (t n) -> p t n", n=8), gb_d[:, :, 16 + d_ * 8:16 + (d_ + 1) * 8],
                              r=[dt_("gb%d" % b_) for b_ in range(NB)], w=[bq], own=bq, accum_w=(d_ == 1))
                    if cstop == 2:
                        raise _StopD()
                    gpb = [C.sb([128, W16], BF16) for _ in range(3)]
                    gpf = [C.sb([128, W16], F32) for _ in range(3)]
                    k.dve(lambda e: e.tensor_copy(out=grem[:], in_=gq[:]), r=[gq], w=[grem])
                    for i3 in range(3):
                        k.dve(lambda e, i3=i3: e.tensor_copy(out=gpb[i3][:], in_=grem[:]), r=[grem], w=[gpb[i3]])
                        k.dve(lambda e, i3=i3: e.tensor_copy(out=gpf[i3][:], in_=gpb[i3][:]), r=[gpb[i3]], w=[gpf[i3]])
                        if i3 < 2:
                            k.dve(lambda e, i3=i3: e.tensor_tensor(out=grem[:], in0=grem[:], in1=gpf[i3][:], op=ALU.subtract),
                                  r=[grem, gpf[i3]], w=[grem])
                    if cstop == 3:
                        raise _StopD()
                    UPB, LOWB = cst_b[:, 4, :], cst_b[:, 2, :]
                    psA_, psT_ = pfr.next(), pfr.next()
                    for i3 in range(3):
                        k.pe(lambda e, i3=i3: e.matmul(psA_[:, 0:H8], lhsT=UPB, rhs=gpb[i3][:, 0:H8], start=(i3 == 0), stop=(i3 == 2)),
                             r=[cst_b, gpb[i3]], w=[psA_])
                    for i3 in range(3):
                        k.pe(lambda e, i3=i3: e.matmul(psA_[:, H8:W16], lhsT=LOWB, rhs=gpb[i3][:, H8:W16], start=(i3 == 0), stop=(i3 == 2)),
                             r=[cst_b, gpb[i3]], w=[psA_])
                    for i3 in range(3):
                        k.pe(lambda e, i3=i3: e.matmul(psT_[:, 0:W16], lhsT=ONESB, rhs=gpb[i3][:, :], start=(i3 == 0), stop=(i3 == 2)),
                             r=[cst_b, gpb[i3]], w=[psT_])
                    if cstop == 4:
                        raise _StopD()
                    k.act(lambda e: e.activation(out=gcs[:], in_=psA_[:, 0:W16], func=AF.Copy), r=[psA_], w=[gcs])
                    k.act(lambda e: e.activation(out=eg[:], in_=psA_[:, 0:W16], func=AF.Exp), r=[psA_], w=[eg])
                    k.act(lambda e: e.activation(out=dec[:], in_=psT_[:, 0:W16], func=AF.Exp), r=[psT_], w=[dec])
                    if cstop == 41:
                        raise _StopD()
                    k.dve(lambda e: e.tensor_tensor(out=ek[:], in0=psT_[:, 0:W16], in1=gcs[:], op=ALU.subtract), r=[psT_, gcs, dec], w=[ek])
                    if cstop == 411:
                        raise _StopD()
                    k.dve(lambda e: e.tensor_tensor(out=bgc[:], in0=bq[:], in1=eg[:], op=ALU.mult), r=[bq, eg], w=[bgc])
                    if cstop == 412:
                        raise _StopD()
                    k.dve(lambda e: e.tensor_scalar(out=nbeta[:], in0=bq[:], scalar1=-1.0, scalar2=None, op0=ALU.mult), r=[bq], w=[nbeta])
                    if cstop == 42:
                        raise _StopD()
                    k.act(lambda e: e.activation(out=ek[:], in_=ek[:], func=AF.Exp), r=[ek], w=[ek])
                    k.dve(lambda e: e.tensor_scalar(out=egs[:], in0=eg[:], scalar1=DKS, scalar2=None, op0=ALU.mult),
                          r=[eg], w=[egs])
                    if cstop == 5:
                        raise _StopD()
                    raw = C.sb([128, S + 4], F32)
                    cacc = C.sb([128, S], F32)
                    sil = cacc
                    sqb = C.sb([128, S], BF16)
                    qT, kT, vT = (C.sb([128, S], BF16) for _ in range(3))
                    o_d = [C.sb([128, S], F32) for _ in range(2)]
                    z_h = C.sb([128, NT, 128], F32)
                    ogT = C.sb([128, S], BF16)
                    rnr = C.ring(1, [128, QB], F32)
                    S32s = [C.sb([128, 128], F32) for _ in range(2)]
                    Sbfs = [C.sb([128, 128], BF16) for _ in range(2)]
                    f128 = C.ring(20, [128, 128], F32)
                    b128 = C.ring(136, [128, 128], BF16)
                    k.dve(lambda e: e.memset(raw[:, 0:4], 0.0), w=[raw])
                    k.dve(lambda e: e.memset(raw[:, S:S + 4], 0.0), w=[raw])
                    qkv_r = [dt_("qkv%d" % b) for b in range(NB)]
                    zs_r = [dt_("zs%d" % b) for b in range(NB)]
                    for h in range(8):
                        if dbg in ("C0", "D0", "D1", "BD") or "noC" in flags:
                            break
                        for which, dst in ((0, qT), (1, kT), (2, vT)):
                            ch = which * 8 + h
                            k.dma(raw[:, 2:S + 2], qkvT_d[:, ch, :], r=qkv_r, w=[raw])
                            k.dve(lambda e: e.tensor_scalar(out=cacc[:], in0=raw[:, 0:S], scalar1=cw[:, ch, 0:1], scalar2=None,
                                                            op0=ALU.mult), r=[raw, cw], w=[cacc])
                            for j in range(1, 5):
                                k.dve(lambda e, j=j: e.scalar_tensor_tensor(out=cacc[:], in0=raw[:, j:j + S],
                                                                            scalar=cw[:, ch, j:j + 1], in1=cacc[:],
                                                                            op0=ALU.mult, op1=ALU.add), r=[raw, cw, cacc], w=[cacc])
                            if which == 2:
                                k.act(lambda e: e.activation(out=dst[:], in_=cacc[:], func=AF.Silu), r=[cacc], w=[dst])
                                continue
                            k.act(lambda e: e.activation(out=sil[:], in_=cacc[:], func=AF.Silu), r=[cacc], w=[sil])
                            k.act(lambda e: e.activation(out=sqb[:], in_=sil[:], func=AF.Square), r=[sil], w=[sqb])
                            for blk in range(NQB):
                                sl = slice(blk * QB, (blk + 1) * QB)
                                ps = pfr.next()
                                k.pe(lambda e: e.matmul(ps[:, 0:QB], lhsT=ONESB, rhs=sqb[:, sl], start=True, stop=True),
                                     r=[cst_b, sqb], w=[ps])
                                rn = rnr.next()
                                k.act(lambda e: e.activation(out=rn[:], in_=ps[:, 0:QB], func=AF.Sqrt, bias=EPS), r=[ps], w=[rn])
                                k.dve(lambda e: e.reciprocal(out=rn[:], in_=rn[:]), r=[rn], w=[rn])
                                k.dve(lambda e: e.tensor_tensor(out=dst[:, sl], in0=sil[:, sl], in1=rn[:], op=ALU.mult),
                                      r=[sil, rn], w=[dst])
                        if dbg == "C1":
                            break
                        k.dma(z_h[:], zs_d[:, h * 128:(h + 1) * 128].rearrange("(t p) v -> p t v", p=128), r=zs_r, w=[z_h])
                        for dr_ in range(2):
                            k.pool(lambda e, dr_=dr_: e.memset(S32s[dr_][:], 0.0), w=[S32s[dr_]])
                            k.pool(lambda e, dr_=dr_: e.memset(Sbfs[dr_][:], 0.0), w=[Sbfs[dr_]])

                        def unit_gen(dr, par):
                            TRI = UP if dr == 0 else LOW
                            SM = SLOW if dr == 0 else SUP
                            IMT = UP if dr == 0 else LOW
                            S32, Sbf = S32s[dr], Sbfs[dr]
                            od = o_d[dr]
                            order = list(range(NT)) if dr == 0 else list(range(NT - 1, -1, -1))
                            for c in (order if par is None else order[par::2]):
                                cs_ = slice(c * 128, (c + 1) * 128)
                                ci = dr * H8 + c * 8 + h
                                bcol = bq[:, ci:ci + 1]
                                psg = pfr.next()
                                for i3 in range(3):
                                    gT = b128.next()
                                    k.dve(lambda e, i3=i3: e.tensor_scalar(out=gT[:], in0=TRI, scalar1=gpf[i3][:, ci:ci + 1],
                                                                           scalar2=None, op0=ALU.mult), r=[cst_f, gpf[i3]], w=[gT])
                                    k.pe(lambda e, i3=i3: e.matmul(psg[:, 0:128], lhsT=ONESB, rhs=gT[:], start=(i3 == 0), stop=(i3 == 2)),
                                         r=[cst_b, gT], w=[psg])
                                yield
                                dm, dtm = f128.next(), f128.next()
                                k.dve(lambda e: e.tensor_scalar(out=dm[:], in0=psg[:, 0:128], scalar1=gcs[:, ci:ci + 1], scalar2=0.0,
                                                                op0=ALU.subtract, op1=ALU.max), r=[psg, gcs], w=[dm])
                                k.dve(lambda e: e.tensor_scalar(out=dtm[:], in0=psg[:, 0:128], scalar1=gcs[:, ci:ci + 1], scalar2=0.0,
                                                                op0=ALU.subtract, op1=ALU.min), r=[psg, gcs], w=[dtm])
                                k.act(lambda e: e.activation(out=dm[:], in_=dm[:], func=AF.Exp, scale=-1.0), r=[dm], w=[dm])
                                k.act(lambda e: e.activation(out=dtm[:], in_=dtm[:], func=AF.Exp), r=[dtm], w=[dtm])
                                k.pool(lambda e: e.tensor_tensor(out=dm[:], in0=dm[:], in1=SM, op=ALU.mult), r=[dm, cst_f], w=[dm])
                                k.pool(lambda e: e.tensor_tensor(out=dtm[:], in0=dtm[:], in1=IMT, op=ALU.mult), r=[dtm, cst_f], w=[dtm])
                                psG = pfr.next()
                                k.pe(lambda e: e.matmul(psG[:, 0:128], lhsT=kT[:, cs_], rhs=kT[:, cs_], start=True, stop=True),
                                     r=[kT], w=[psG])
                                psK = pfr.next()
                                k.pe(lambda e: e.matmul(psK[:, 0:128], lhsT=kT[:, cs_], rhs=qT[:, cs_], start=True, stop=True),
                                     r=[kT, qT], w=[psK])
                                bank2 = pbr.next()
                                k.pe(lambda e: e.transpose(bank2[:, 0:128], kT[:, cs_], IDB()), r=[kT, cst_b], w=[bank2])
                                k.pe(lambda e: e.transpose(bank2[:, 128:256], vT[:, cs_], IDB()), r=[vT, cst_b], w=[bank2])
                                yield
                                Ln = b128.next()
                                k.dve(lambda e: e.scalar_tensor_tensor(out=Ln[:], in0=psG[:, 0:128], scalar=nbeta[:, ci:ci + 1],
                                                                       in1=dm[:], op0=ALU.mult, op1=ALU.mult),
                                      r=[psG, nbeta, dm], w=[Ln])
                                AT = b128.next()
                                k.dve(lambda e: e.scalar_tensor_tensor(out=AT[:], in0=psK[:, 0:128], scalar=DKS, in1=dtm[:],
                                                                       op0=ALU.mult, op1=ALU.mult), r=[psK, dtm], w=[AT])
                                kbg, kd, vb = b128.next(), b128.next(), b128.next()
                                k.act(lambda e: e.activation(out=kbg[:], in_=bank2[:, 0:128], func=AF.Copy, scale=bgc[:, ci:ci + 1]),
                                      r=[bank2, bgc], w=[kbg])
                                k.act(lambda e: e.activation(out=kd[:], in_=bank2[:, 0:128], func=AF.Copy, scale=ek[:, ci:ci + 1]),
                                      r=[bank2, ek], w=[kd])
                                k.act(lambda e: e.activation(out=vb[:], in_=bank2[:, 128:256], func=AF.Copy, scale=bcol),
                                      r=[bank2, bq], w=[vb])
                                bank = pbr.next()
                                k.pe(lambda e: e.transpose(bank[:, 0:128], Ln[:], IDB()), r=[Ln, cst_b], w=[bank])
                                yield
                                Nk = b128.next()
                                k.act(lambda e: e.activation(out=Nk[:], in_=bank[:, 0:128], func=AF.Copy), r=[bank], w=[Nk])
                                NkT = Ln
                                Pm = b128.next()
                                k.pool(lambda e: e.tensor_tensor(out=Pm[:], in0=Nk[:], in1=cst_b[:, 0, :], op=ALU.add),
                                       r=[Nk, cst_b], w=[Pm])
                                Pt = b128.next()
                                k.pool(lambda e: e.tensor_tensor(out=Pt[:], in0=NkT[:], in1=cst_b[:, 0, :], op=ALU.add),
                                       r=[NkT, cst_b], w=[Pt])
                                for lev in range(1, 6):
                                    psA = pfr.next()
                                    k.pe(lambda e: e.matmul(psA[:, 0:128], lhsT=Nk[:], rhs=NkT[:], start=True, stop=True),
                                         r=[Nk, NkT], w=[psA])
                                    if lev < 5:
                                        psB = pfr.next()
                                        k.pe(lambda e: e.matmul(psB[:, 0:128], lhsT=NkT[:], rhs=Nk[:], start=True, stop=True),
                                             r=[Nk, NkT], w=[psB])
                                    yield
                                    NkT2 = b128.next()
                                    k.act(lambda e: e.activation(out=NkT2[:], in_=psA[:, 0:128], func=AF.Copy), r=[psA], w=[NkT2])
                                    if lev < 5:
                                        Nk2 = b128.next()
                                        k.act(lambda e: e.activation(out=Nk2[:], in_=psB[:, 0:128], func=AF.Copy), r=[psB], w=[Nk2])
                                    else:
                                        Nk2 = None
                                    psC = pfr.next()
                                    k.pe(lambda e: e.matmul(psC[:, 0:128], lhsT=NkT2[:], rhs=Pm[:], start=True, stop=True),
                                         r=[NkT2, Pm], w=[psC])
                                    psD = pfr.next()
                                    k.pe(lambda e: e.matmul(psD[:, 0:128], lhsT=Pm[:], rhs=NkT2[:], start=True, stop=True),
                                         r=[NkT2, Pm], w=[psD])
                                    yield
                                    Pn = b128.next()
                                    k.dve(lambda e: e.tensor_tensor(out=Pn[:], in0=psC[:, 0:128], in1=Pm[:], op=ALU.add),
                                          r=[psC, Pm], w=[Pn])
                                    Ptn = b128.next()
                                    k.dve(lambda e: e.tensor_tensor(out=Ptn[:], in0=psD[:, 0:128], in1=Pt[:], op=ALU.add),
                                          r=[psD, Pt], w=[Ptn])
                                    Pm, Pt, Nk, NkT = Pn, Ptn, Nk2, NkT2
                                psR = pfr.next()
                                k.pe(lambda e: e.matmul(psR[:, 0:128], lhsT=Ln[:], rhs=Pm[:], start=True, stop=True), r=[Ln, Pm], w=[psR])
                                IX = b128.next()
                                k.pool(lambda e: e.tensor_tensor(out=IX[:], in0=cst_b[:, 0, :], in1=Pm[:], op=ALU.subtract),
                                       r=[cst_b, Pm], w=[IX])
                                yield
                                Rr = b128.next()
                                k.dve(lambda e: e.tensor_tensor(out=Rr[:], in0=psR[:, 0:128], in1=IX[:], op=ALU.add),
                                      r=[psR, IX], w=[Rr])
                                psX = pfr.next()
                                k.pe(lambda e: e.matmul(psX[:, 0:128], lhsT=Pt[:], rhs=Rr[:], start=True, stop=True), r=[Pt, Rr], w=[psX])
                                yield
                                TT = b128.next()
                                k.dve(lambda e: e.tensor_tensor(out=TT[:], in0=psX[:, 0:128], in1=Pm[:], op=ALU.add),
                                      r=[psX, Pm], w=[TT])
                                psu = pfr.next()
                                k.pe(lambda e: e.matmul(psu[:, 0:128], lhsT=TT[:], rhs=vb[:], start=True, stop=True), r=[TT, vb], w=[psu])
                                psw = pfr.next()
                                k.pe(lambda e: e.matmul(psw[:, 0:128], lhsT=kbg[:], rhs=TT[:], start=True, stop=True), r=[TT, kbg], w=[psw])
                                yield
                                u = f128.next()
                                k.act(lambda e: e.activation(out=u[:], in_=psu[:, 0:128], func=AF.Copy), r=[psu], w=[u])
                                wT = b128.next()
                                k.act(lambda e: e.activation(out=wT[:], in_=psw[:, 0:128], func=AF.Copy), r=[psw], w=[wT])
                                ps1 = pfr.next()
                                k.pe(lambda e: e.matmul(ps1[:, 0:128], lhsT=wT[:], rhs=Sbf[:], start=True, stop=True), r=[wT, Sbf], w=[ps1])
                                ps2 = pfr.next()
                                k.pe(lambda e: e.matmul(ps2[:, 0:128], lhsT=qT[:, cs_], rhs=Sbf[:], start=True, stop=True), r=[qT, Sbf], w=[ps2])
                                yield
                                vn = b128.next()
                                k.dve(lambda e: e.tensor_tensor(out=vn[:], in0=u[:], in1=ps1[:, 0:128], op=ALU.subtract),
                                      r=[u, ps1], w=[vn])
                                tmp = f128.next()
                                k.act(lambda e: e.activation(out=tmp[:], in_=ps2[:, 0:128], func=AF.Copy, scale=egs[:, ci:ci + 1]),
                                      r=[ps2, egs], w=[tmp])
                                ps3 = pfr.next()
                                k.pe(lambda e: e.matmul(ps3[:, 0:128], lhsT=AT[:], rhs=vn[:], start=True, stop=True), r=[AT, vn], w=[ps3])
                                ps4 = pfr.next()
                                k.pe(lambda e: e.matmul(ps4[:, 0:128], lhsT=kd[:], rhs=vn[:], start=True, stop=True), r=[kd, vn], w=[ps4])
                                yield
                                k.dve(lambda e: e.tensor_tensor(out=od[:, cs_], in0=tmp[:], in1=ps3[:, 0:128], op=ALU.add),
                                      r=[tmp, ps3], w=[od])
                                k.dve(lambda e: e.scalar_tensor_tensor(out=S32[:], in0=S32[:], scalar=dec[:, ci:ci + 1], in1=ps4[:, 0:128],
                                                                       op0=ALU.mult, op1=ALU.add), r=[S32, dec, ps4], w=[S32])
                                k.pool(lambda e: e.tensor_copy(out=Sbf[:], in_=S32[:]), r=[S32], w=[Sbf])
                                yield

                        def delayed(g_, n_):
                            for _ in range(n_):
                                yield
                            yield from g_
                        gens = [unit_gen(0, None), unit_gen(1, None)]
                        while gens:
                            for g_ in list(gens):
                                try:
                                    next(g_)
                                except StopIteration:
                                    gens.remove(g_)
                        for c in range(NT):
                            cs_ = slice(c * 128, (c + 1) * 128)
                            k.dve(lambda e: e.tensor_tensor(out=o_d[0][:, cs_], in0=o_d[0][:, cs_], in1=o_d[1][:, cs_], op=ALU.add),
                                   r=[o_d[0], o_d[1]], w=[o_d[0]])
                            rs = rstd_of(P, (o_d[0], o_d[0][:, cs_]), 128, P["junk"])
                            tmp = f128.next()
                            k.dve(lambda e: e.scalar_tensor_tensor(out=tmp[:], in0=o_d[0][:, cs_], scalar=rs[:, 0:1], in1=gon[:],
                                                                   op0=ALU.mult, op1=ALU.mult), r=[o_d[0], rs, gon], w=[tmp])
                            onb = b128.next()
                            k.dve(lambda e: e.tensor_tensor(out=onb[:], in0=tmp[:], in1=z_h[:, c, :], op=ALU.mult),
                                   r=[tmp, z_h], w=[onb])
                            bank = pbr.next()
                            k.pe(lambda e: e.transpose(bank[:, 0:128], onb[:], IDB()), r=[onb, cst_b], w=[bank])
                            k.act(lambda e: e.activation(out=ogT[:, cs_], in_=bank[:, 0:128], func=AF.Copy), r=[bank], w=[ogT])
                        k.dma(omixT_d[:, 8 + h, :], ogT[:], r=[ogT], w=[dt_("ogdn")], own=ogT, accum_w=True, q="pool")
                except _StopD:
                    pass
            k.barrier()
            if dbg in ("C", "C0", "C1"):
                break
            if cstop != 99:
                break
            with contextlib.ExitStack() as st:
                Dp = Pools(k, st)
                P = {"ss": Dp.ring(4, [128, 1], F32), "rs": Dp.ring(4, [128, 1], F32),
                     "junk": Dp.sb([128, 512], BF16), "nbf": Dp.ring(2, [128, D], BF16),
                     "wring": Dp.ring(3, [128, KC, 512], BF16), "hT": Dp.sb([128, FC, TB], BF16),
                     "silu": Dp.ring(1, [128, 512], F32), "gBs": [Dp.sb([128, D], F32) for _ in range(2)], "gcur": {}}
                nT = Dp.sb([128, KC, TB], BF16)
                xr = Dp.ring(2, [128, D], F32)
                ysb = [Dp.sb([128, D], F32) for _ in range(TPB)]
                mT = P["hT"]
                KmT = Dp.sb([128, 4, NMEM], BF16)
                Vm = Dp.sb([128, 2, 512], BF16)
                qmT = Dp.sb([128, 4, TB], BF16)
                omT = Dp.sb([128, 4, TB], BF16)
                Pr = Dp.ring(2, [128, TB], BF16)
                rinv = Dp.ring(1, [128, TB], F32)
                mscale = float(128 ** -0.5)
                TPB_save = TPB

                def load_mem(t):
                    xt = xr.next()
                    k.dma(xt[:], mem_d[sq, t * 128:(t + 1) * 128, :], w=[xt])
                    return xt
                for t in range(NMEM // 128 if dstop > 0 else 0):
                    xt = load_mem(t)
                    nb = P["nbf"].next()
                    rs = rstd_of(P, (xt, xt[:, :]), D, nb)
                    k.act(lambda e: e.activation(out=nb[:], in_=xt[:], func=AF.Copy, scale=rs[:, 0:1]), r=[xt, rs], w=[nb])
                    transpose_to(P, nb, D, mT, t, (gpre, PRE["mem_kv_norm_g"]))
                sk = load_w(P, wb["w_mk"], 0, KC, 0, 512)
                for hh in range(4 if dstop > 0 else 0):
                    acc = pfr.next()
                    for kc in range(KC):
                        k.pe(lambda e, kc=kc: e.matmul(acc[:, 0:NMEM], lhsT=sk[:, kc, hh * 128:(hh + 1) * 128], rhs=mT[:, kc, 0:NMEM],
                                                       start=(kc == 0), stop=(kc == KC - 1)), r=[sk, mT], w=[acc])
                    k.act(lambda e: e.activation(out=KmT[:, hh, :], in_=acc[:, 0:NMEM], func=AF.Copy), r=[acc], w=[KmT])
                sv = load_w(P, wb["w_mv"], 0, KC, 0, 512)
                for t in range(NMEM // 128 if dstop > 0 else 0):
                    acc = pfr.next()
                    for kc in range(KC):
                        k.pe(lambda e, kc=kc: e.matmul(acc[:, :], lhsT=mT[:, kc, t * 128:(t + 1) * 128], rhs=sv[:, kc, :],
                                                       start=(kc == 0), stop=(kc == KC - 1)), r=[sv, mT], w=[acc])
                    k.act(lambda e: e.activation(out=Vm[:, t, :], in_=acc[:, :], func=AF.Copy), r=[acc], w=[Vm])
                om_r = [dt_("omla"), dt_("ogdn")]
                for blk in range(NB if dstop > 1 else 0):
                    if dbg == "D1" and blk == 1:
                        break
                    t0 = blk * TB
                    hrow = dt_("h1%d" % blk)
                    k.dma(nT[:], omixT_d[:, :, t0:t0 + TB], r=om_r, w=[nT])
                    for dg in range(4):
                        so = load_w(P, wb["w_out"], 0, KC, dg * 512, 512)
                        for t in range(TPB):
                            acc = pfr.next()
                            for kc in range(KC):
                                k.pe(lambda e, kc=kc: e.matmul(acc[:, :], lhsT=nT[:, kc, t * 128:(t + 1) * 128], rhs=so[:, kc, :],
                                                               start=(kc == 0), stop=(kc == KC - 1)), r=[so, nT], w=[acc])
                            k.act(lambda e: e.activation(out=ysb[t][:, dg * 512:(dg + 1) * 512], in_=acc[:, :], func=AF.Copy),
                                  r=[acc], w=[ysb[t]])

                    def mk_loader(gname, half):
                        def ld(t, junk):
                            xt = xr.next()
                            k.dma(xt[:], h1_d[t0 + t * 128:t0 + (t + 1) * 128, :], r=[hrow], w=[xt])
                            post_residual(P, ysb[t], gname, xt, xt, half, junk)
                            k.dma(h1_d[t0 + t * 128:t0 + (t + 1) * 128, :], xt[:], r=[xt], w=[hrow], own=xt, accum_w=True, q="pool")
                            return xt
                        return ld
                    norm_transpose_block(P, mk_loader("mix_post_g", False), "mem_pre_g", nT)
                    if dstop == 2:
                        break
                    sq_ = load_w(P, wb["w_mq"], 0, KC, 0, 512)
                    for hh in range(4):
                        acc = pfr.next()
                        for kc in range(KC):
                            k.pe(lambda e, kc=kc: e.matmul(acc[:, 0:TB], lhsT=sq_[:, kc, hh * 128:(hh + 1) * 128], rhs=nT[:, kc, :],
                                                           start=(kc == 0), stop=(kc == KC - 1)), r=[sq_, nT], w=[acc])
                        k.dve(lambda e: e.tensor_copy(out=qmT[:, hh, :], in_=acc[:, 0:TB]), r=[acc], w=[qmT])
                    for hh in range(4):
                        accO, accR = pf[0], pf[1]
                        for mt in range(2):
                            sp_ = pf[2 + mt]
                            k.pe(lambda e: e.matmul(sp_[:, 0:TB], lhsT=KmT[:, hh, mt * 128:(mt + 1) * 128], rhs=qmT[:, hh, :],
                                                    start=True, stop=True), r=[KmT, qmT], w=[sp_])
                            p_ = Pr.next()
                            k.act(lambda e: e.activation(out=p_[:], in_=sp_[:, 0:TB], func=AF.Exp, scale=mscale), r=[sp_], w=[p_])
                            k.pe(lambda e: e.matmul(accO[:, 0:TB], lhsT=Vm[:, mt, hh * 128:(hh + 1) * 128], rhs=p_[:],
                                                    start=(mt == 0), stop=(mt == 1)), r=[Vm, p_], w=[accO])
                            k.pe(lambda e: e.matmul(accR[:, 0:TB], lhsT=ONESB, rhs=p_[:], start=(mt == 0), stop=(mt == 1)),
                                 r=[cst_b, p_], w=[accR])
                        ri = rinv.next()
                        k.dve(lambda e: e.reciprocal(out=ri[:], in_=accR[:, 0:TB]), r=[accR], w=[ri])
                        k.dve(lambda e: e.tensor_tensor(out=omT[:, hh, :], in0=accO[:, 0:TB], in1=ri[:], op=ALU.mult),
                              r=[accO, ri], w=[omT])
                    pfr.i = 0
                    for dg in range(4):
                        so = load_w(P, wb["w_mo"], 0, 4, dg * 512, 512)
                        for t in range(TPB):
                            acc = pfr.next()
                            for hh in range(4):
                                k.pe(lambda e, hh=hh: e.matmul(acc[:, :], lhsT=omT[:, hh, t * 128:(t + 1) * 128], rhs=so[:, hh, :],
                                                               start=(hh == 0), stop=(hh == 3)), r=[so, omT], w=[acc])
                            k.act(lambda e: e.activation(out=ysb[t][:, dg * 512:(dg + 1) * 512], in_=acc[:, :], func=AF.Copy),
                                  r=[acc], w=[ysb[t]])
                    if dstop == 3:
                        break
                    norm_transpose_block(P, mk_loader("mem_post_g", False), "ffn2_pre_g", nT)
                    ffn(P, nT, wb["ffn2_w_gate"], wb["ffn2_w_up"], wb["ffn2_w_down"], ysb)
                    if dstop == 4:
                        break
                    for t in range(TPB):
                        xt = xr.next()
                        k.dma(xt[:], h1_d[t0 + t * 128:t0 + (t + 1) * 128, :], r=[hrow], w=[xt])
                        jk = P["nbf"].next()
                        post_residual(P, ysb[t], "ffn2_post_g", xt, xt, True, jk)
                        rs = rstd_of(P, (xt, xt[:, :]), D, jk)
                        gB = load_gain(P, "final_norm_g", 1)
                        k.dve(lambda e: e.scalar_tensor_tensor(out=xt[:], in0=xt[:], scalar=rs[:, 0:1], in1=gB[:],
                                                               op0=ALU.mult, op1=ALU.mult), r=[xt, rs, gB], w=[xt])
                        k.dma(y_d[sq, t0 + t * 128:t0 + (t + 1) * 128, :], xt[:], r=[xt], w=[dt_("y")], own=xt, accum_w=True, q="pool")
            k.barrier()
        k.finish()
        print("BUILD stats: nins=%d nwait=%d nsems=%d" % (k.nins, k.nwait, len(k.sems)), flush=True)
    return nc


def host_layouts(inp, S):
    f = np.float32
    out = {}
    for n in ["ffn1_w_gate", "ffn1_w_up", "ffn1_w_down", "ffn2_w_gate", "ffn2_w_up", "ffn2_w_down",
              "w_in", "w_uq", "w_ukv", "w_out", "w_mq", "w_mk", "w_mv", "w_mo"]:
        out[n] = np.ascontiguousarray(np.asarray(inp[n], f)[0])
    for n in ["ffn1_pre_g", "ffn1_post_g", "mix_pre_g", "mix_post_g", "mem_pre_g", "mem_kv_norm_g",
              "mem_post_g", "ffn2_pre_g", "ffn2_post_g", "final_norm_g"]:
        out[n + "_bc"] = np.ascontiguousarray(np.broadcast_to(np.asarray(inp[n], f)[0][None, :], (128, D)))
    pre = ["ffn1_pre_g", "mix_pre_g", "mem_pre_g", "mem_kv_norm_g", "ffn2_pre_g"]
    out["gpre"] = np.ascontiguousarray(
        np.stack([np.asarray(inp[n], f)[0].reshape(KC, 128).T for n in pre], axis=1))
    qg = np.asarray(inp["mla_q_norm_g"], f)[0].reshape(4, 128).T
    kg = np.asarray(inp["mla_kv_norm_g"], f)[0].reshape(2, 128).T
    out["qkg"] = np.ascontiguousarray(np.concatenate([qg, kg], axis=1))
    i = np.arange(128)
    p, fr = i[:, None], i[None, :]
    out["consts"] = np.ascontiguousarray(np.stack(
        [np.eye(128), np.ones((128, 128)), fr <= p, fr < p, fr >= p, fr > p], axis=1).astype(f))
    NT = S // 128
    inv_freq = (10000.0 ** (-np.arange(0, ROPE, 2, dtype=f) / f(ROPE))).astype(f)
    ang = (np.arange(S, dtype=f)[:, None] * inv_freq[None, :]).astype(f)
    cos = np.cos(ang).astype(f).reshape(NT, 128, 32).transpose(1, 0, 2)
    sin = np.sin(ang).astype(f).reshape(NT, 128, 32).transpose(1, 0, 2)
    out["rope"] = np.ascontiguousarray(np.stack([cos, sin], axis=1))
    al = np.asarray(inp["gdn_a_log"], f)[0].reshape(16)
    dtb = np.asarray(inp["gdn_dt_bias"], f)[0].reshape(16)
    out["gdnp"] = np.ascontiguousarray(np.broadcast_to(np.stack([al, dtb], 0)[None], (128, 2, 16)))
    out["gon"] = np.ascontiguousarray(np.broadcast_to(np.asarray(inp["gdn_out_norm_g"], f)[0][None, :], (128, 128)))
    cw = np.asarray(inp["gdn_conv_w"], f)[0]
    out["cw"] = np.ascontiguousarray(cw.reshape(5, 24, 128).transpose(2, 1, 0))
    return out


S_FULL = 4096
_NC_CACHE = {}


def kernel(**inputs):
    S = S_FULL
    NSEQ = 2
    if "nc" not in _NC_CACHE:
        _NC_CACHE["nc"] = build(S, NSEQ)
    nc = _NC_CACHE["nc"]
    hl = host_layouts(inputs, S)
    xp = np.asarray(inputs["x_prompt"], np.float32)
    xs = np.asarray(inputs["x_sample"], np.float32)
    mp = np.asarray(inputs["mem_prompt"], np.float32)
    ms = np.asarray(inputs["mem_sample"], np.float32)
    in_maps = []
    for c in range(8):
        m = dict(hl)
        m["x"] = np.ascontiguousarray(np.stack([xp[c], xs[c % 2]], axis=0))
        m["mem"] = np.ascontiguousarray(np.stack([mp[c], ms[c % 2]], axis=0))
        in_maps.append(m)
    res = run_bass_kernel_spmd(nc, in_maps, core_ids=list(range(8)))
    yp = np.stack([np.asarray(res.results[c]["y"])[0] for c in range(8)], axis=0).astype(np.float32)
    ys = np.stack([np.asarray(res.results[c]["y"])[1] for c in range(2)], axis=0).astype(np.float32)
    return (yp, ys)
```
